# Optimizing a Trainium2 kernel written in Bass

```python
import jax, jax.numpy as jnp
from jax import lax
import numpy as np

D_MODEL = 4096
BATCH = 1
SEQ = 8192
DEPTH = 1
DEC_BATCH = 128
DEC_SEQ = 4
PAST_LEN = 8192
PAGE_SIZE = 128

N_META = 16
ATTN_WIDTH = D_MODEL // 2
CONV_WIDTH = D_MODEL - ATTN_WIDTH
HEAD_DIM = 64
N_HEADS = ATTN_WIDTH // HEAD_DIM
N_KV_HEADS = N_HEADS // 8
GROUP = N_HEADS // N_KV_HEADS
ROPE_DIM = HEAD_DIM // 4
ROPE_THETA = 500000.0
WINDOW = 128
BLOCK = 128
CONV_K = 3
D_FF = ((8 * D_MODEL // 3 + 255) // 256) * 256
EPS = 1e-5
Q_W = N_HEADS * HEAD_DIM
KV_W = N_KV_HEADS * HEAD_DIM
IN_W = Q_W + 2 * KV_W + 3 * CONV_WIDTH
NEG = -1e30

kernel_name = "hybrid_swa_sink_shortconv_convffn_step"


def rmsnorm(x, g):
    xf = x.astype(jnp.float32)
    y = xf * lax.rsqrt(jnp.mean(xf * xf, axis=-1, keepdims=True) + EPS) * g.astype(jnp.float32)
    return y.astype(x.dtype)


def rope(x, pos):
    half = ROPE_DIM // 2
    inv = ROPE_THETA ** (-jnp.arange(half, dtype=jnp.float32) * 2.0 / ROPE_DIM)
    ang = pos.astype(jnp.float32)[:, None] * inv
    cos = jnp.cos(ang)[:, None, :]
    sin = jnp.sin(ang)[:, None, :]
    x1 = x[..., :half].astype(jnp.float32)
    x2 = x[..., half:ROPE_DIM].astype(jnp.float32)
    r1 = (x1 * cos - x2 * sin).astype(x.dtype)
    r2 = (x2 * cos + x1 * sin).astype(x.dtype)
    return jnp.concatenate([r1, r2, x[..., ROPE_DIM:]], axis=-1)


def sink_attend(q, k, v, mask, sinks):
    s = jnp.einsum('...qkgd,...skd->...kgqs', q.astype(jnp.float32), k.astype(jnp.float32)) * (HEAD_DIM ** -0.5)
    s = jnp.where(mask, s, NEG)
    sink = sinks.astype(jnp.float32).reshape(N_KV_HEADS, GROUP)[:, :, None, None]
    m = jnp.maximum(jnp.max(s, axis=-1, keepdims=True), sink)
    p = jnp.exp(s - m)
    p = p / (jnp.sum(p, axis=-1, keepdims=True) + jnp.exp(sink - m))
    o = jnp.einsum('...kgqs,...skd->...qkgd', p, v.astype(jnp.float32))
    return o.astype(q.dtype)


def attn_prompt(q, k, v, sinks):
    B, L = q.shape[0], q.shape[1]
    pad = (-N_META) % BLOCK
    Lp = L + pad
    nb = Lp // BLOCK
    qb = jnp.pad(q, ((0, 0), (pad, 0), (0, 0), (0, 0))).reshape(B, nb, BLOCK, N_KV_HEADS, GROUP, HEAD_DIM)
    kc = jnp.pad(k, ((0, 0), (pad, 0), (0, 0), (0, 0))).reshape(B, nb, BLOCK, N_KV_HEADS, HEAD_DIM)
    vc = jnp.pad(v, ((0, 0), (pad, 0), (0, 0), (0, 0))).reshape(B, nb, BLOCK, N_KV_HEADS, HEAD_DIM)
    bpad = ((0, 0), (1, 0), (0, 0), (0, 0), (0, 0))
    kb = jnp.concatenate([jnp.pad(kc, bpad)[:, :-1], kc], axis=2)
    vb = jnp.concatenate([jnp.pad(vc, bpad)[:, :-1], vc], axis=2)
    qpos = (jnp.arange(Lp) - pad).reshape(nb, BLOCK)
    kpos = jnp.concatenate([qpos - BLOCK, qpos], axis=1)
    qq = qpos[:, :, None]
    kk = kpos[:, None, :]
    mask = (kk >= 0) & (kk <= qq) & (kk > qq - WINDOW)
    o = sink_attend(qb, kb, vb, mask[None, :, None, None], sinks)
    o = o.reshape(B, Lp, Q_W)[:, pad:]
    return o, k[:, -WINDOW:], v[:, -WINDOW:]


def attn_sample(q, k, v, sinks, cache_k, cache_v):
    DB, T = q.shape[0], q.shape[1]
    W = cache_k.shape[1]
    kk_all = jnp.concatenate([cache_k.astype(k.dtype), k], axis=1)
    vv_all = jnp.concatenate([cache_v.astype(v.dtype), v], axis=1)
    kpos = jnp.arange(W + T) + (PAST_LEN - W)
    qpos = jnp.arange(T) + PAST_LEN
    qq = qpos[:, None]
    kk = kpos[None, :]
    mask = (kk <= qq) & (kk > qq - WINDOW)
    o = sink_attend(q.reshape(DB, T, N_KV_HEADS, GROUP, HEAD_DIM), kk_all, vv_all, mask, sinks)
    return o.reshape(DB, T, Q_W), kk_all[:, -W:], vv_all[:, -W:]


def causal_dwconv(u, w, prev):
    T = u.shape[1]
    up = jnp.concatenate([prev.astype(u.dtype), u], axis=1)
    y = up[:, 0:T] * w[0]
    for i in range(1, CONV_K):
        y = y + up[:, i:i + T] * w[i]
    return y, up[:, -(CONV_K - 1):]


def layer(h, pos, attn_fn, conv_prev, ffn_prev, g_mix, w_in, sinks, conv_w, g_attn_out,
          g_conv_out, w_out, g_ffn, w_gate_up, ffn_conv_w, w_down):
    B, T = h.shape[0], h.shape[1]
    a = rmsnorm(h, g_mix)
    proj = a @ w_in
    splits = [Q_W, Q_W + KV_W, Q_W + 2 * KV_W, Q_W + 2 * KV_W + CONV_WIDTH, Q_W + 2 * KV_W + 2 * CONV_WIDTH]
    q, k, v, cb, cc, cx = jnp.split(proj, splits, axis=-1)
    q = rope(q.reshape(B, T, N_HEADS, HEAD_DIM), pos)
    k = rope(k.reshape(B, T, N_KV_HEADS, HEAD_DIM), pos)
    v = v.reshape(B, T, N_KV_HEADS, HEAD_DIM)
    attn_o, new_k, new_v = attn_fn(q, k, v, sinks)
    conv_y, new_conv = causal_dwconv(cc * cx, conv_w, conv_prev)
    conv_o = cb * conv_y
    mixed = jnp.concatenate([rmsnorm(attn_o, g_attn_out), rmsnorm(conv_o, g_conv_out)], axis=-1) @ w_out
    h = h + mixed
    f = rmsnorm(h, g_ffn)
    gate, up = jnp.split(f @ w_gate_up, [D_FF], axis=-1)
    gate_c, new_ffn = causal_dwconv(gate, ffn_conv_w, ffn_prev)
    h = h + (jax.nn.silu(gate_c) * up) @ w_down
    return h, new_k, new_v, new_conv, new_ffn


def setup_inputs(seed: int = 0) -> dict:
    key = jax.random.key(seed)
    ks = jax.random.split(key, 24)
    f32 = jnp.float32
    nrm = lambda k, shape, s: jax.random.normal(k, shape, f32) * s
    gain = lambda k, shape: 1.0 + 0.02 * jax.random.normal(k, shape, f32)
    return {
        "x_prompt": nrm(ks[0], (BATCH, SEQ, D_MODEL), 1.0),
        "x_sample": nrm(ks[1], (DEC_BATCH, DEC_SEQ, D_MODEL), 1.0),
        "cache_k": nrm(ks[2], (DEPTH, DEC_BATCH, WINDOW, N_KV_HEADS, HEAD_DIM), 1.0),
        "cache_v": nrm(ks[3], (DEPTH, DEC_BATCH, WINDOW, N_KV_HEADS, HEAD_DIM), 1.0),
        "state_conv": nrm(ks[4], (DEPTH, DEC_BATCH, CONV_K - 1, CONV_WIDTH), 1.0),
        "state_ffn_conv": nrm(ks[5], (DEPTH, DEC_BATCH, CONV_K - 1, D_FF), 1.0),
        "meta_tokens": nrm(ks[6], (N_META, D_MODEL), 1.0),
        "g_mix": gain(ks[7], (DEPTH, D_MODEL)),
        "w_in": nrm(ks[8], (DEPTH, D_MODEL, IN_W), D_MODEL ** -0.5),
        "attn_sinks": nrm(ks[9], (DEPTH, N_HEADS), 0.5),
        "conv_w": nrm(ks[10], (DEPTH, CONV_K, CONV_WIDTH), CONV_K ** -0.5),
        "g_attn_out": gain(ks[11], (DEPTH, ATTN_WIDTH)),
        "g_conv_out": gain(ks[12], (DEPTH, CONV_WIDTH)),
        "w_out": nrm(ks[13], (DEPTH, D_MODEL, D_MODEL), D_MODEL ** -0.5),
        "g_ffn": gain(ks[14], (DEPTH, D_MODEL)),
        "w_gate_up": nrm(ks[15], (DEPTH, D_MODEL, 2 * D_FF), D_MODEL ** -0.5),
        "ffn_conv_w": nrm(ks[16], (DEPTH, CONV_K, D_FF), CONV_K ** -0.5),
        "w_down": nrm(ks[17], (DEPTH, D_FF, D_MODEL), D_FF ** -0.5),
        "g_final": gain(ks[18], (D_MODEL,)),
    }


def reference(x_prompt, x_sample, cache_k, cache_v, state_conv, state_ffn_conv, meta_tokens,
              g_mix, w_in, attn_sinks, conv_w, g_attn_out, g_conv_out, w_out, g_ffn,
              w_gate_up, ffn_conv_w, w_down, g_final):
    B = x_prompt.shape[0]
    DB, T = x_sample.shape[0], x_sample.shape[1]
    meta = jnp.broadcast_to(meta_tokens.astype(x_prompt.dtype)[None], (B, N_META, D_MODEL))
    hp = jnp.concatenate([meta, x_prompt], axis=1)
    hs = x_sample
    pos_p = jnp.arange(hp.shape[1])
    pos_s = jnp.arange(T) + PAST_LEN
    nk_p, nv_p, nc_p, nf_p = [], [], [], []
    nk_s, nv_s, nc_s, nf_s = [], [], [], []
    for l in range(DEPTH):
        w = (g_mix[l], w_in[l], attn_sinks[l], conv_w[l], g_attn_out[l], g_conv_out[l], w_out[l],
             g_ffn[l], w_gate_up[l], ffn_conv_w[l], w_down[l])
        zc = jnp.zeros((B, CONV_K - 1, CONV_WIDTH), hp.dtype)
        zf = jnp.zeros((B, CONV_K - 1, D_FF), hp.dtype)
        hp, k1, v1, c1, f1 = layer(hp, pos_p, attn_prompt, zc, zf, *w)
        ck, cv = cache_k[l], cache_v[l]
        samp_fn = lambda q, k, v, s, ck=ck, cv=cv: attn_sample(q, k, v, s, ck, cv)
        hs, k2, v2, c2, f2 = layer(hs, pos_s, samp_fn, state_conv[l], state_ffn_conv[l], *w)
        nk_p.append(k1); nv_p.append(v1); nc_p.append(c1); nf_p.append(f1)
        nk_s.append(k2); nv_s.append(v2); nc_s.append(c2); nf_s.append(f2)
    y_prompt = rmsnorm(hp[:, N_META:], g_final)
    y_sample = rmsnorm(hs, g_final)
    return (y_prompt, y_sample,
            jnp.stack(nk_p), jnp.stack(nv_p), jnp.stack(nc_p), jnp.stack(nf_p),
            jnp.stack(nk_s), jnp.stack(nv_s), jnp.stack(nc_s), jnp.stack(nf_s))
```

```python
import numpy as np
import concourse.bass as bass
import concourse.mybir as mybir
from concourse.bass_utils import run_bass_kernel_spmd

F32, BF16, U8 = mybir.dt.float32, mybir.dt.bfloat16, mybir.dt.uint8
AF = mybir.ActivationFunctionType
ALU = mybir.AluOpType

D = 4096
DFF = 11008
NFT = 86
TI = 800
M0, M1 = 252, 800
NM = M1 - M0
O0 = 256
NO = M1 - O0
TILES = [(0, 128), (128, 128), (256, 128), (384, 128), (512, 128), (640, 128), (768, 32)]
MP = [(252, 4), (256, 128), (384, 128), (512, 128), (640, 128), (768, 32)]
OP = [(256, 128), (384, 128), (512, 128), (640, 128), (768, 32)]
EPS = 1e-5
NPFM = 32 + 32 + 16 + 16 + 48 + 258 + 32

G0 = 0
S0 = 6912
ZX = 32768
ZC = 126464
ZR = 161792
ARENA = 210944


class Res:
    __slots__ = ("name", "space", "lo", "hi", "w", "r", "al")

    def __init__(self, name, space, lo, hi):
        self.name, self.space, self.lo, self.hi = name, space, lo, hi
        self.w = {}
        self.r = {}
        self.al = [self]


class Eng:
    def __init__(self, name, h, sem, key):
        self.name, self.h, self.sem, self.key = name, h, sem, key
        self.cnt = 0
        self.seen = {}


class DQ:
    def __init__(self, eng, sems):
        self.eng = eng
        self.sems = sems
        self.tot = [0] * len(sems)
        self.n = 0


class Builder:
    def __init__(self, debug=False):
        self.debug = debug
        nc = bass.Bass("TRN2", target_bir_lowering=False)
        self.nc = nc
        self.res = []
        self.semh = {}
        self.nkey = 0
        self.PE = self.mk_eng("pe", nc.tensor)
        self.ACT = self.mk_eng("act", nc.scalar)
        self.DVE = self.mk_eng("dve", nc.vector)
        self.POOL = self.mk_eng("pool", nc.gpsimd)
        self.SP = Eng("sp", nc.sync, None, -1)
        self.qsp = DQ(self.SP, [self.mk_sem("dsp%d" % i) for i in range(16)])
        self.qw = DQ(self.POOL, [self.mk_sem("dw%d" % i) for i in range(NSLOT)])
        self.arena = nc.alloc_sbuf_tensor("arena", [128, ARENA], U8)
        self.psum = nc.alloc_psum_tensor("psum", [128, 8, 512], F32)
        self.bank = [self.new_res("bank%d" % b, "psum", b * 2048, (b + 1) * 2048) for b in range(8)]
        self.dbg_outs = []

    def mk_sem(self, name):
        s = self.nc.alloc_semaphore(name)
        k = self.nkey
        self.nkey += 1
        self.semh[k] = s
        return (s, k)

    def mk_eng(self, name, h):
        s, k = self.mk_sem("s_" + name)
        return Eng(name, h, s, k)

    def new_res(self, name, space, lo, hi):
        r = Res(name, space, lo, hi)
        for o in self.res:
            if o.space == space and o.lo < hi and lo < o.hi:
                o.al.append(r)
                r.al.append(o)
        self.res.append(r)
        return r

    def sb(self, name, off, shape, dt, nres=0):
        n = 1
        for s in shape[1:]:
            n *= s
        esz = 4 if dt == F32 else 2
        nb = n * esz
        assert off % 4 == 0 and off + nb <= ARENA, (name, off, nb)
        v = self.arena[0:shape[0], off:off + nb].bitcast(dt)
        if len(shape) == 3:
            v = v.rearrange("p (a b) -> p a b", a=shape[1])
        elif len(shape) == 4:
            v = v.rearrange("p (a b c) -> p a b c", a=shape[1], b=shape[2])
        if nres:
            per = nb // nres
            rs = [self.new_res("%s%d" % (name, i), "sbuf", off + i * per, off + (i + 1) * per) for i in range(nres)]
            return v, rs
        return v, self.new_res(name, "sbuf", off, off + nb)

    def dres(self, name):
        return self.new_res(name, "d:" + name, 0, 1)

    def sync(self, eng, reads, writes):
        raw = {}
        oth = {}
        for r in reads:
            for a in r.al:
                for k, v in a.w.items():
                    if v > raw.get(k, 0):
                        raw[k] = v
        for w in writes:
            for a in w.al:
                for k, v in a.w.items():
                    if v > oth.get(k, 0):
                        oth[k] = v
                for k, v in a.r.items():
                    if v > oth.get(k, 0):
                        oth[k] = v
        need = {}
        for k, v in raw.items():
            if k == eng.key and eng.name == "pe":
                continue
            need[k] = v
        for k, v in oth.items():
            if k == eng.key:
                continue
            if v > need.get(k, 0):
                need[k] = v
        for k, v in need.items():
            if eng.seen.get(k, 0) < v:
                eng.h.wait_ge(self.semh[k], v)
                eng.seen[k] = v

    def mark(self, key, val, reads, writes):
        for r in reads:
            if r.r.get(key, 0) < val:
                r.r[key] = val
        for w in writes:
            w.w = {key: val}
            w.r = {}

    def done(self, eng, ins, reads, writes):
        ins.then_inc(eng.sem, 1)
        eng.cnt += 1
        self.mark(eng.key, eng.cnt, reads, writes)

    def dma(self, q, out, in_, reads, writes, maxd=16384):
        i = q.n % len(q.sems)
        q.n += 1
        sem, key = q.sems[i]
        eng = q.eng
        if q.tot[i] > 0 and eng.seen.get(key, 0) < q.tot[i]:
            eng.h.wait_ge(sem, q.tot[i])
            eng.seen[key] = q.tot[i]
        self.sync(eng, reads, writes)
        ins = eng.h.dma_start(out=out, in_=in_, max_dma_last_dim=maxd)
        ins.then_inc(sem, 16)
        q.tot[i] += 16
        self.mark(key, q.tot[i], reads, writes)

    def act(self, out, in_, func, reads, writes, scale=1.0, bias=None, accum=None):
        self.sync(self.ACT, reads, writes)
        kw = {}
        if bias is not None:
            kw["bias"] = bias
        if accum is not None:
            kw["accum_out"] = accum
        ins = self.nc.scalar.activation(out=out, in_=in_, func=func, scale=scale, **kw)
        self.done(self.ACT, ins, reads, writes)

    def tt(self, out, in0, in1, op, reads, writes, eng=None):
        eng = eng or self.DVE
        self.sync(eng, reads, writes)
        ins = eng.h.tensor_tensor(out=out, in0=in0, in1=in1, op=op)
        self.done(eng, ins, reads, writes)

    def ts(self, out, in0, s1, op0, reads, writes, s2=None, op1=None, eng=None):
        eng = eng or self.DVE
        self.sync(eng, reads, writes)
        if op1 is None:
            ins = eng.h.tensor_scalar(out=out, in0=in0, scalar1=s1, scalar2=None, op0=op0)
        else:
            ins = eng.h.tensor_scalar(out=out, in0=in0, scalar1=s1, scalar2=s2, op0=op0, op1=op1)
        self.done(eng, ins, reads, writes)

    def stt(self, out, in0, scalar, in1, op0, op1, reads, writes):
        self.sync(self.DVE, reads, writes)
        ins = self.nc.vector.scalar_tensor_tensor(out=out, in0=in0, scalar=scalar, in1=in1, op0=op0, op1=op1)
        self.done(self.DVE, ins, reads, writes)

    def recip(self, out, in_, reads, writes, fast=False):
        if fast:
            self.act(out, in_, AF.Ln, reads, writes)
            self.act(out, out, AF.Exp, writes, writes, scale=-1.0)
            return
        self.sync(self.DVE, reads, writes)
        ins = self.nc.vector.reciprocal(out=out, in_=in_)
        self.done(self.DVE, ins, reads, writes)

    def copy(self, eng, out, in_, reads, writes):
        self.sync(eng, reads, writes)
        ins = eng.h.tensor_copy(out=out, in_=in_)
        self.done(eng, ins, reads, writes)

    def memset(self, eng, ap, val, writes):
        self.sync(eng, [], writes)
        ins = eng.h.memset(ap, val)
        self.done(eng, ins, [], writes)

    def mms(self, ops, reads, writes):
        self.sync(self.PE, reads, writes)
        ins = None
        for (o, l, r, st, sp) in ops:
            ins = self.nc.tensor.matmul(o, l, r, start=st, stop=sp)
        self.done(self.PE, ins, reads, writes)

    def transposes(self, ops, reads, writes):
        self.sync(self.PE, reads, writes)
        ins = None
        for (o, i, ident) in ops:
            ins = self.nc.tensor.transpose(o, i, ident)
        self.done(self.PE, ins, reads, writes)

    def tap(self, name, view, res, shape):
        if not self.debug:
            return
        d = self.nc.dram_tensor("dbg_" + name, list(shape), view.dtype, kind="ExternalOutput").ap()
        r = self.dres("dbg_" + name)
        n = shape[1]
        for c0 in range(0, n, 4096):
            c1 = min(n, c0 + 4096)
            self.dma(self.qsp, d[:, c0:c1], view[:, c0:c1], [res] if not isinstance(res, list) else res, [])
        self.dbg_outs.append(("dbg_" + name, r))


NSLOT = 6


class WStream:
    def __init__(self, B):
        self.B = B
        self.slabs = []
        self.issued = 0
        self.taken = 0
        self.donec = 0
        self.slots = [B.sb("ring%d" % s, ZR + s * 8192, [128, 16, 256], BF16) for s in range(NSLOT)]
        self.wres = {}

    def add(self, W, wname, k0, nk, c0, nkt=32):
        nslab = -(-nkt // 16)
        row0 = ((c0 // 256) * nslab + k0 // 16) * 128
        src = W[row0:row0 + 128, 0:nk * 256].rearrange("p (k c) -> p k c", k=nk)
        v, r = self.slots[len(self.slabs) % NSLOT]
        if wname not in self.wres:
            self.wres[wname] = self.B.dres(wname)
        self.slabs.append((src, v[:, 0:nk, :], r, self.wres[wname]))

    def fill(self):
        while self.issued < len(self.slabs) and self.issued < self.donec + NSLOT:
            src, v, r, wr = self.slabs[self.issued]
            self.B.dma(self.B.qw, v, src, [wr], [r], maxd=1024)
            self.issued += 1

    def take(self):
        assert self.taken < self.issued, (self.taken, self.issued)
        src, v, r, wr = self.slabs[self.taken]
        self.taken += 1
        return v, r

    def done(self):
        self.donec += 1
        self.fill()


def build_program(npass=2, debug=False, stop_after=99):
    B = Builder(debug)
    nc = B.nc

    def din(name, shape):
        return nc.dram_tensor(name, list(shape), F32, kind="ExternalInput").ap(), B.dres(name)

    def dout(name, shape, kind="ExternalOutput"):
        return nc.dram_tensor(name, list(shape), F32, kind=kind).ap(), B.dres(name)

    xh, xh_r = din("xh", [npass * TI, D])
    ck, ck_r = din("ck", [npass * 8 * 128, 256])
    cv, cv_r = din("cv", [npass * 8 * 128, 256])
    sc, sc_r = din("sc", [npass * 16, 2048])
    sf, sf_r = din("sf", [npass * 16, DFF])
    w_in, _ = din("w_in", [34 * 2 * 128, 4096])
    w_out, _ = din("w_out", [16 * 2 * 128, 4096])
    w_gu, _ = din("w_gu", [86 * 2 * 128, 4096])
    w_dn, _ = din("w_dn", [16 * 6 * 128, 4096])
    pfm_d, pfm_r = din("pfm", [128, NPFM])
    sinks_d, sinks_r = din("sinks", [1, 32])
    c128_d, c128_r = din("c128", [128, 644])
    maskn_d, maskn_r = din("maskn", [32, 32])
    tfm_d, tfm_r = din("tfm", [npass * 128, 2 * NM])
    ttok_d, ttok_r = din("ttok", [npass * 128, 7 * 32])
    kval_d, kval_r = din("kval", [npass * 128, 2])

    y_p, y_p_r = dout("y_p", [npass * 512, D])
    y_s, y_s_r = dout("y_s", [npass * 32, D])
    nk_p, nk_p_r = dout("nk_p", [npass * 128, 256])
    nv_p, nv_p_r = dout("nv_p", [npass * 128, 256])
    nc_p, nc_p_r = dout("nc_p", [npass * 2, 2048])
    nf_p, nf_p_r = dout("nf_p", [npass * 2, DFF])
    nk_s, nk_s_r = dout("nk_s", [npass * 8 * 128, 256])
    nv_s, nv_s_r = dout("nv_s", [npass * 8 * 128, 256])
    nc_s, nc_s_r = dout("nc_s", [npass * 16, 2048])
    nf_s, nf_s_r = dout("nf_s", [npass * 16, DFF])
    hmid, _ = dout("hmid", [npass * 32 * 128, NM], kind="Internal")
    hout, _ = dout("hout", [npass * 32 * 128, NO], kind="Internal")
    hmid_rs = [[B.dres("hmid%d_%d" % (p, j)) for j in range(32)] for p in range(npass)]
    hout_rs = [[B.dres("hout%d_%d" % (p, j)) for j in range(32)] for p in range(npass)]
    out_res = [y_p_r, y_s_r, nk_p_r, nv_p_r, nc_p_r, nf_p_r, nk_s_r, nv_s_r, nc_s_r, nf_s_r]

    PS = B.psum
    bank = B.bank

    o = G0
    identf, identf_r = B.sb("identf", o, [128, 128], F32); o += 512
    onesf, onesf_r = B.sb("onesf", o, [128, 128], F32); o += 512
    identb, identb_r = B.sb("identb", o, [128, 128], BF16); o += 256
    onesb, onesb_r = B.sb("onesb", o, [128, 128], BF16); o += 256
    permb, permb_r = B.sb("permb", o, [128, 128], BF16); o += 256
    maskO, maskO_r = B.sb("maskO", o, [128, 128], BF16); o += 256
    maskP, maskP_r = B.sb("maskP", o, [128, 128], BF16); o += 256
    maskc, maskc_r = B.sb("maskc", o, [128, 4], BF16); o += 64
    masknb, masknb_r = B.sb("masknb", o, [128, 32], BF16); o += 64
    pfm, pfm_sr = B.sb("pfm", o, [128, NPFM], F32); o += 1792
    sinkexp, sinkexp_r = B.sb("sinkexp", o, [128, 32], F32); o += 128
    epsc, epsc_r = B.sb("epsc", o, [128, 1], F32); o += 64
    kvalid, kvalid_r = B.sb("kvalid", o, [128, 2], F32); o += 64
    ssv, ssv_r = B.sb("ssv", o, [128, 8], F32); o += 64
    rsv, rsv_r = B.sb("rsv", o, [128, 8], F32); o += 64
    rstd_f, rstd_f_r = B.sb("rstd_f", o, [128, NM], F32); o += 2240
    assert o <= S0, o
    gmix = pfm[:, 0:32]
    gffn = pfm[:, 32:64]
    gattn = pfm[:, 64:80]
    gconv = pfm[:, 80:96]
    convw = pfm[:, 96:144].rearrange("p (j i) -> p j i", i=3)
    fconvw = pfm[:, 144:402].rearrange("p (j i) -> p j i", i=3)
    gfin = pfm[:, 402:434]

    WS = WStream(B)

    def add_pair(W, wname, nkt, c0):
        k0 = 0
        while k0 < nkt:
            nk = min(16, nkt - k0)
            WS.add(W, wname, k0, nk, c0, nkt=nkt)
            k0 += nk

    for p in range(npass):
        add_pair(w_in, "w_in", 32, 2048)
        add_pair(w_in, "w_in", 32, 2304)
        for s in range(8):
            add_pair(w_in, "w_in", 32, 256 * s)
        for sg in range(8):
            add_pair(w_in, "w_in", 32, 2560 + 256 * sg)
            add_pair(w_in, "w_in", 32, 4608 + 256 * sg)
            add_pair(w_in, "w_in", 32, 6656 + 256 * sg)
        for s in range(16):
            add_pair(w_out, "w_out", 32, 256 * s)
        for s in range(43):
            add_pair(w_gu, "w_gu", 32, 256 * s)
            add_pair(w_gu, "w_gu", 32, DFF + 256 * s)
        for s in range(16):
            add_pair(w_dn, "w_dn", NFT, 256 * s)

    def pair_view(pi):
        return PS[:, 2 * pi:2 * pi + 2, :].rearrange("p a b -> p (a b)")

    def pair_res(pi):
        return [bank[2 * pi], bank[2 * pi + 1]]

    cst, cst_r = B.sb("cst", ZX, [128, 644], F32)
    B.dma(B.qsp, cst, c128_d, [c128_r], [cst_r])
    B.copy(B.DVE, identf, cst[:, 0:128], [cst_r], [identf_r])
    B.copy(B.DVE, identb, cst[:, 0:128], [cst_r], [identb_r])
    B.copy(B.DVE, maskO, cst[:, 128:256], [cst_r], [maskO_r])
    B.copy(B.DVE, maskP, cst[:, 256:384], [cst_r], [maskP_r])
    B.copy(B.DVE, permb, cst[:, 384:512], [cst_r], [permb_r])
    B.copy(B.DVE, onesb, cst[:, 512:640], [cst_r], [onesb_r])
    B.copy(B.DVE, onesf, cst[:, 512:640], [cst_r], [onesf_r])
    B.copy(B.DVE, maskc, cst[:, 640:644], [cst_r], [maskc_r])
    cst2, cst2_r = B.sb("cst2", ZX + 4096, [128, 32], F32)
    B.dma(B.qsp, cst2[0:32, :], maskn_d, [maskn_r], [cst2_r])
    B.copy(B.DVE, masknb[0:32, :], cst2[0:32, :], [cst2_r], [masknb_r])
    B.dma(B.qsp, pfm, pfm_d, [pfm_r], [pfm_sr])
    B.dma(B.qsp, cst2, sinks_d[0, :].partition_broadcast(128), [sinks_r], [cst2_r])
    B.act(sinkexp, cst2, AF.Exp, [cst2_r], [sinkexp_r])
    B.memset(B.DVE, epsc, EPS, [epsc_r])

    WS.fill()

    for p in range(npass):
        E0 = ZX + 51200 + 17664
        o = E0
        CT, CT_r = B.sb("CT", o, [128, NM], F32); o += 2240
        ST, ST_r = B.sb("ST", o, [128, NM], F32); o += 2240
        ttok, ttok_r2 = B.sb("ttok", o, [128, 7, 32], F32); o += 896
        acc, acc_r = B.sb("acc", o, [128, NM], F32); o += 2240
        rstd_t, rstd_t_r = B.sb("rstd_t", o, [128, NM], F32); o += 2240
        cstT, cstT_r = B.sb("cstT", o, [128, 16, 16], F32); o += 1024
        sqt, sqt_r = B.sb("sqt", o, [128, NM], F32); o += 2240
        scf, scf_r = B.sb("scf", o, [16, 2048], F32); o += 8192
        assert o <= ZC, o
        B.dma(B.qsp, CT, tfm_d[p * 128:(p + 1) * 128, 0:NM], [tfm_r], [CT_r])
        B.dma(B.qsp, ST, tfm_d[p * 128:(p + 1) * 128, NM:2 * NM], [tfm_r], [ST_r])
        B.dma(B.qsp, ttok.rearrange("p a b -> p (a b)"), ttok_d[p * 128:(p + 1) * 128, :], [ttok_r], [ttok_r2])
        B.dma(B.qsp, kvalid, kval_d[p * 128:(p + 1) * 128, :], [kval_r], [kvalid_r])

        if stop_after <= 0:
            break
        aT, aT_r = B.sb("aT", ZX, [128, 32, TI], BF16, nres=32)
        convT, convT_r = B.sb("convT", ZX + 51200, [128, 16, NM], BF16, nres=16)
        qT, qT_r = B.sb("qT", ZC, [128, 16, NM], BF16, nres=16)
        KT, KT_r = B.sb("KT", ZC + 17536, [128, 4, TI], BF16)
        Vd, Vd_r = B.sb("Vd", ZC + 17536 + 6400, [128, 7, 4, 128], BF16, nres=7)

        xin = [B.sb("xin%d" % i, ZC + i * 16384, [128, D], F32) for i in range(2)]
        xb = [B.sb("xb%d" % i, ZX + 51200 + i * 8192, [128, D], BF16) for i in range(2)]
        for i, (r0, nr) in enumerate(TILES):
            xi, xi_r = xin[i % 2]
            xbb, xbb_r = xb[i % 2]
            B.dma(B.qsp, xi[:nr, :], xh[p * TI + r0:p * TI + r0 + nr, :], [xh_r], [xi_r])
            import os
            cut = int(os.environ.get("K1CUT", "9"))
            if cut <= 0:
                continue
            B.act(xbb[:nr, :], xi[:nr, :], AF.Square, [xi_r], [xbb_r, ssv_r], accum=ssv[:nr, i:i + 1])
            if cut <= 1:
                continue
            B.act(rsv[:nr, i:i + 1], ssv[:nr, i:i + 1], AF.Ln, [ssv_r, epsc_r], [rsv_r], scale=1.0 / D, bias=epsc[:nr, :])
            B.act(rsv[:nr, i:i + 1], rsv[:nr, i:i + 1], AF.Exp, [rsv_r], [rsv_r], scale=-0.5)
            B.act(xbb[:nr, :], xi[:nr, :], AF.Copy, [xi_r, rsv_r], [xbb_r], scale=rsv[:nr, i:i + 1])
            if cut <= 2:
                continue
            for kg in range(4):
                b = (i * 4 + kg) % 6
                pt = PS[:, b, :].bitcast(BF16).rearrange("p (a c) -> p a c", a=8)
                ops = []
                for kk in range(8):
                    k = kg * 8 + kk
                    ops.append((pt[:, kk, 0:nr], xbb[:nr, k * 128:(k + 1) * 128], identb[:nr, :nr]))
                B.transposes(ops, [xbb_r, identb_r], [bank[b]])
                if cut <= 3:
                    continue
                B.tt(aT[:, kg * 8:(kg + 1) * 8, r0:r0 + nr], pt[:, :, 0:nr],
                     gmix[:, kg * 8:(kg + 1) * 8].unsqueeze(2).to_broadcast([128, 8, nr]), ALU.mult,
                     [bank[b], pfm_sr], aT_r[kg * 8:(kg + 1) * 8])
        if p == 0 and cut >= 9:
            B.tap("aT", aT.rearrange("p a b -> p (a b)"), aT_r, [128, 32 * TI])
        if stop_after <= 1:
            break

        o = S0
        Kf, Kf_r = B.sb("Kf", o, [128, 256], F32); o += 1024
        Vf, Vf_r = B.sb("Vf", o, [128, 256], F32); o += 1024
        Kd, Kd_r = B.sb("Kd", o, [128, 4, 128], BF16); o += 1024
        rA, rA_r = B.sb("rA", o, [128, 4, 16], F32); o += 256
        rB, rB_r = B.sb("rB", o, [128, 4, 16], F32); o += 256
        S_after_kv = o
        wKs = [WS.take(), WS.take()]
        wVs = [WS.take(), WS.take()]
        wK_rs = [r for (_, r) in wKs]
        wV_rs = [r for (_, r) in wVs]
        for i, (r0, nr) in enumerate(TILES):
            pk = PS[:, 6, 0:256]
            pv = PS[:, 7, 0:256]
            ops = [(pk[:nr, :], aT[:, k, r0:r0 + nr], wKs[k // 16][0][:, k % 16, :], k == 0, k == 31) for k in range(32)]
            B.mms(ops, aT_r + wK_rs, [bank[6]])
            ops = [(pv[:nr, :], aT[:, k, r0:r0 + nr], wVs[k // 16][0][:, k % 16, :], k == 0, k == 31) for k in range(32)]
            B.mms(ops, aT_r + wV_rs, [bank[7]])
            pv3 = pv[:nr, :].rearrange("p (g d) -> p g d", g=4)
            B.act(Vd[:nr, i, :, 0:64], pv3, AF.Copy, [bank[7]], [Vd_r[i]])
            B.act(Vd[:nr, i, :, 64:128], pv3, AF.Copy, [bank[7]], [Vd_r[i]])
            if i >= 5:
                B.act(Vf[:nr, :], pv[:nr, :], AF.Copy, [bank[7]], [Vf_r])
            B.act(Kf[:nr, :], pk[:nr, :], AF.Copy, [bank[6]], [Kf_r])
            K3 = Kf[:nr, :].rearrange("p (g d) -> p g d", g=4)
            cc_b = ttok[:nr, i, 0:16].unsqueeze(1).to_broadcast([nr, 4, 16])
            ss_b = ttok[:nr, i, 16:32].unsqueeze(1).to_broadcast([nr, 4, 16])
            B.tt(rA[:nr], K3[:, :, 0:16], cc_b, ALU.mult, [Kf_r, ttok_r2], [rA_r])
            B.tt(rB[:nr, :, 0:8], K3[:, :, 8:16], ss_b[:, :, 0:8], ALU.mult, [Kf_r, ttok_r2], [rB_r])
            B.tt(rB[:nr, :, 8:16], K3[:, :, 0:8], ss_b[:, :, 8:16], ALU.mult, [Kf_r, ttok_r2], [rB_r])
            B.tt(K3[:, :, 0:16], rA[:nr], rB[:nr], ALU.add, [rA_r, rB_r], [Kf_r])
            B.copy(B.DVE, Kd[:nr, :, 0:64], K3, [Kf_r], [Kd_r])
            B.copy(B.DVE, Kd[:nr, :, 64:128], K3, [Kf_r], [Kd_r])
            pT = PS[:, 6, :].bitcast(BF16)[:, 0:512].rearrange("p (g c) -> p g c", g=4)
            ops = [(pT[:, g, 0:nr], Kd[:nr, g, :], identb[:nr, :nr]) for g in range(4)]
            B.transposes(ops, [Kd_r, identb_r], [bank[6]])
            B.act(KT[:, :, r0:r0 + nr], pT[:, :, 0:nr], AF.Copy, [bank[6]], [KT_r])
            if i == 5:
                B.dma(B.qsp, nk_p[p * 128:(p + 1) * 128, :], Kf, [Kf_r], [])
                B.dma(B.qsp, nv_p[p * 128:(p + 1) * 128, :], Vf, [Vf_r], [])
            if i == 6:
                nk3 = nk_s.rearrange("(s w) c -> s w c", w=128)
                nv3 = nv_s.rearrange("(s w) c -> s w c", w=128)
                ck3 = ck.rearrange("(s w) c -> s w c", w=128)
                cv3 = cv.rearrange("(s w) c -> s w c", w=128)
                B.dma(B.qsp, nk3[p * 8:(p + 1) * 8, 124:128, :], Kf[0:32, :], [Kf_r], [])
                B.dma(B.qsp, nv3[p * 8:(p + 1) * 8, 124:128, :], Vf[0:32, :], [Vf_r], [])
                B.dma(B.qsp, nk3[p * 8:(p + 1) * 8, 0:124, :], ck3[p * 8:(p + 1) * 8, 4:128, :], [ck_r], [])
                B.dma(B.qsp, nv3[p * 8:(p + 1) * 8, 0:124, :], cv3[p * 8:(p + 1) * 8, 4:128, :], [cv_r], [])
        for _ in range(4):
            WS.done()
        if p == 0:
            B.tap("KT", KT.rearrange("p a b -> p (a b)"), KT_r, [128, 4 * TI])
            B.tap("Vd", Vd.rearrange("p a b c -> p (a b c)"), Vd_r, [128, 7 * 4 * 128])

        o = S_after_kv
        qraw = [B.sb("qraw%d" % i, o + i * 1152, [128, NM], BF16) for i in range(2)]; o += 2304
        m1 = [B.sb("m1_%d" % i, o + i * 2240, [128, NM], F32) for i in range(2)]; o += 4480
        m2 = [B.sb("m2_%d" % i, o + i * 2240, [128, NM], F32) for i in range(2)]; o += 4480
        S_after_q = o

        def pair_mm(nkt, rhs_of_k, rhs_res, c0=M0):
            pis = [next_pair(), next_pair()]
            k0 = 0
            while k0 < nkt:
                nk = min(16, nkt - k0)
                w, w_r = WS.take()
                for jj in range(2):
                    pv_ = pair_view(pis[jj])
                    ops = []
                    for kk in range(nk):
                        k = k0 + kk
                        ops.append((pv_[:, c0:512], w[:, kk, jj * 128:(jj + 1) * 128], rhs_of_k(k, c0, 512), k == 0, k == nkt - 1))
                    for kk in range(nk):
                        k = k0 + kk
                        ops.append((pv_[:, 512:800], w[:, kk, jj * 128:(jj + 1) * 128], rhs_of_k(k, 512, 800), k == 0, k == nkt - 1))
                    B.mms(ops, rhs_res[k0:k0 + nk] + [w_r], pair_res(pis[jj]))
                WS.done()
                k0 += nk
            return pis

        rot = [0]

        def next_pair():
            pi = rot[0] % 3
            rot[0] += 1
            return pi

        for s in range(8):
            pis = pair_mm(32, lambda k, a, b: aT[:, k, a:b], aT_r)
            for jj in range(2):
                j = 2 * s + jj
                pi = pis[jj]
                pm = pair_view(pi)[:, M0:M1]
                qr, qr_r = qraw[j % 2]
                B.act(qr, pm, AF.Copy, pair_res(pi), [qr_r])
                p3 = pair_view(3)
                B.mms([(p3[:, 252:512], permb, qr[:, 0:260], True, True), (p3[:, 512:800], permb, qr[:, 260:NM], True, True)],
                      [permb_r, qr_r], pair_res(3))
                a1, a1_r = m1[j % 2]
                a2, a2_r = m2[j % 2]
                B.tt(a1, p3[:, M0:M1], ST, ALU.mult, pair_res(3) + [ST_r], [a1_r])
                B.tt(a2, qr, CT, ALU.mult, [qr_r, CT_r], [a2_r])
                B.tt(qT[:, j, :], a1, a2, ALU.add, [a1_r, a2_r], [qT_r[j]])
        if p == 0:
            B.tap("qT", qT.rearrange("p a b -> p (a b)"), qT_r, [128, 16 * NM])
        if stop_after <= 2:
            break

        B.dma(B.qsp, scf, sc[p * 16:(p + 1) * 16, :], [sc_r], [scf_r])
        for jg in range(4):
            pt = PS[:, 6, 0:64].rearrange("p (a c) -> p a c", a=4)
            ops = [(pt[:, jj, :], scf[0:16, (jg * 4 + jj) * 128:(jg * 4 + jj + 1) * 128], identf[0:16, 0:16]) for jj in range(4)]
            B.transposes(ops, [scf_r, identf_r], [bank[6]])
            B.act(cstT[:, jg * 4:(jg + 1) * 4, :], pt, AF.Copy, [bank[6]], [cstT_r])
        o = S_after_kv
        cbS = [B.sb("cbS%d" % i, o + i * 2240, [128, NM], F32) for i in range(2)]; o += 4480
        ccS = [B.sb("ccS%d" % i, o + i * 2240, [128, NM], F32) for i in range(2)]; o += 4480
        ub = [B.sb("ub%d" % i, o + i * 2240, [128, NM], F32) for i in range(2)]; o += 4480
        yb, yb_r = B.sb("yb", o, [128, NM], F32); o += 2240
        ob, ob_r = B.sb("ob", o, [128, NM], F32); o += 2240
        ext, ext_r = B.sb("ext", o, [128, 8, 6], F32); o += 192
        ysb, ysb_r = B.sb("ysb", o, [128, 8, 4], F32); o += 128
        stg, stg_r = B.sb("stg", o, [128, 2, 18], F32); o += 192
        stO = [B.sb("stO%d" % i, o + i * 1024, [18, 256], F32) for i in range(2)]; o += 2048
        assert o <= ZX, o
        B.memset(B.DVE, ob[:, 0:2], 0.0, [ob_r])
        B.memset(B.DVE, acc, 0.0, [acc_r])
        def flush_stg(sg_):
            pt = PS[:, 7, 0:256].rearrange("p (a c) -> p a c", a=2)
            B.transposes([(pt[0:18, jj, :], stg[:, jj, :], identf) for jj in range(2)], [stg_r, identf_r], [bank[7]])
            so, so_r = stO[sg_ % 2]
            B.act(so.rearrange("p (a c) -> p a c", a=2), pt[0:18], AF.Copy, [bank[7]], [so_r])
            B.dma(B.qsp, nc_p[p * 2:(p + 1) * 2, sg_ * 256:(sg_ + 1) * 256], so[0:2, :], [so_r], [])
            B.dma(B.qsp, nc_s[p * 16:(p + 1) * 16, sg_ * 256:(sg_ + 1) * 256], so[2:18, :], [so_r], [])

        for sg in range(8):
            pis = pair_mm(32, lambda k, a, b: aT[:, k, a:b], aT_r)
            if sg > 0:
                flush_stg(sg - 1)
            for jj in range(2):
                B.act(cbS[jj][0], pair_view(pis[jj])[:, M0:M1], AF.Copy, pair_res(pis[jj]), [cbS[jj][1]])
            pis = pair_mm(32, lambda k, a, b: aT[:, k, a:b], aT_r)
            for jj in range(2):
                B.act(ccS[jj][0], pair_view(pis[jj])[:, M0:M1], AF.Copy, pair_res(pis[jj]), [ccS[jj][1]])
            pis = pair_mm(32, lambda k, a, b: aT[:, k, a:b], aT_r)
            for jj in range(2):
                j = 2 * sg + jj
                pi = pis[jj]
                pm = pair_view(pi)[:, M0:M1]
                u, u_r = ub[jj]
                B.tt(u, pm, ccS[jj][0], ALU.mult, pair_res(pi) + [ccS[jj][1]], [u_r])
                B.act(yb[:, 2:516], u[:, 0:514], AF.Copy, [u_r, pfm_sr], [yb_r], scale=convw[:, j, 0:1])
                B.stt(yb[:, 2:516], u[:, 1:515], convw[:, j, 1:2], yb[:, 2:516], ALU.mult, ALU.add, [u_r, pfm_sr, yb_r], [yb_r])
                B.stt(yb[:, 2:516], u[:, 2:516], convw[:, j, 2:3], yb[:, 2:516], ALU.mult, ALU.add, [u_r, pfm_sr, yb_r], [yb_r])
                B.copy(B.DVE, ext[:, :, 0:2], cstT[:, j, :].rearrange("p (s r) -> p s r", r=2), [cstT_r], [ext_r])
                B.copy(B.DVE, ext[:, :, 2:6], u[:, 516:548].rearrange("p (s t) -> p s t", t=4), [u_r], [ext_r])
                B.ts(ysb, ext[:, :, 0:4], convw[:, j, 0:1], ALU.mult, [ext_r, pfm_sr], [ysb_r])
                B.stt(ysb, ext[:, :, 1:5], convw[:, j, 1:2], ysb, ALU.mult, ALU.add, [ext_r, pfm_sr, ysb_r], [ysb_r])
                B.stt(ysb, ext[:, :, 2:6], convw[:, j, 2:3], ysb, ALU.mult, ALU.add, [ext_r, pfm_sr, ysb_r], [ysb_r])
                B.copy(B.DVE, yb[:, 516:548].rearrange("p (s t) -> p s t", t=4), ysb, [ysb_r], [yb_r])
                B.tt(ob[:, 2:NM], yb[:, 2:NM], cbS[jj][0][:, 2:NM], ALU.mult, [yb_r, cbS[jj][1]], [ob_r])
                B.act(convT[:, j, :], ob, AF.Copy, [ob_r], [convT_r[j]])
                B.act(sqt, ob, AF.Square, [ob_r], [sqt_r])
                B.tt(acc, acc, sqt, ALU.add, [acc_r, sqt_r], [acc_r])
                B.copy(B.DVE, stg[:, jj, 0:2], u[:, 514:516], [u_r], [stg_r])
                B.copy(B.DVE, stg[:, jj, 2:18].rearrange("p (s r) -> p s r", r=2), ext[:, :, 4:6], [ext_r], [stg_r])
        flush_stg(7)

        def finish_norm(dst, dst_r, nfeat):
            p3 = pair_view(3)
            B.mms([(p3[:, 252:512], onesf, acc[:, 0:260], True, True), (p3[:, 512:800], onesf, acc[:, 260:NM], True, True)],
                  [onesf_r, acc_r], pair_res(3))
            B.act(dst, p3[:, M0:M1], AF.Ln, pair_res(3) + [epsc_r], [dst_r], scale=1.0 / nfeat, bias=epsc)
            B.act(dst, dst, AF.Exp, [dst_r], [dst_r], scale=-0.5)

        finish_norm(rstd_t, rstd_t_r, 2048)
        for j in range(16):
            B.stt(convT[:, j, :], convT[:, j, :], gconv[:, j:j + 1], rstd_t, ALU.mult, ALU.mult, [convT_r[j], pfm_sr, rstd_t_r], [convT_r[j]])
        if p == 0:
            B.tap("convT", convT.rearrange("p a b -> p (a b)"), convT_r, [128, 16 * NM])
        if stop_after <= 3:
            break

        AOT, AOT_r = B.sb("AOT", ZX, [128, 16, NM], BF16, nres=16)
        o = ZX + 17536
        cKT, cKT_r = B.sb("cKT", o, [128, 8, 4, 128], BF16); o += 8192
        cVd, cVd_r = B.sb("cVd", o, [128, 8, 4, 128], BF16); o += 8192
        Pp = [B.sb("Pp%d" % i, o + i * 1024, [128, 4, 128], BF16) for i in range(2)]; o += 2048
        Po = [B.sb("Po%d" % i, o + i * 1024, [128, 4, 128], BF16) for i in range(2)]; o += 2048
        dsm = [B.sb("dsm%d" % i, o + i * 2048, [128, 512], F32) for i in range(2)]; o += 4096
        ckf = [B.sb("ckf%d" % i, o + i * 1024, [128, 256], F32) for i in range(2)]; o += 2048
        cvf = [B.sb("cvf%d" % i, o + i * 1024, [128, 256], F32) for i in range(2)]; o += 2048
        cKd = [B.sb("cKd%d" % i, o + i * 1024, [128, 4, 128], BF16) for i in range(2)]; o += 2048
        Pc, Pc_r = B.sb("Pc", o, [128, 256], BF16); o += 512
        Pn, Pn_r = B.sb("Pn", o, [128, 256], BF16); o += 512
        assert o <= ZX + 51200

        stT, stT_r = B.sb("stT", S0 + 4480, [128, NFT, 16], F32)
        sff = [B.sb("sff%d" % i, S0 + 4480 + 5504 + i * 4096, [16, 1024], F32) for i in range(3)]
        for step in range(13):
            if step < 11:
                bt = step
                nsub = 8 if bt < 10 else 6
                sfb, sfb_r = sff[bt % 3]
                B.dma(B.qsp, sfb[:, 0:nsub * 128], sf[p * 16:(p + 1) * 16, bt * 1024:bt * 1024 + nsub * 128], [sf_r], [sfb_r])
            if step >= 2:
                bt = step - 2
                nsub = 8 if bt < 10 else 6
                sfb, sfb_r = sff[bt % 3]
                pt = PS[:, bt % 2, 0:128].rearrange("p (a c) -> p a c", a=8)
                B.transposes([(pt[:, jj, :], sfb[0:16, jj * 128:(jj + 1) * 128], identf[0:16, 0:16]) for jj in range(nsub)],
                             [sfb_r, identf_r], [bank[bt % 2]])
                B.act(stT[:, 8 * bt:8 * bt + nsub, :], pt[:, 0:nsub, :], AF.Copy, [bank[bt % 2]], [stT_r])

        QB = [(252, 4, 0, 1, 124)] + [(256 + 128 * i, 128, 1 + i, 2 + i, 0) for i in range(4)]
        import os
        c3 = int(os.environ.get("A3CUT", "99"))
        units = [(qc0, n, tp, to, moff, g, hg) for (qc0, n, tp, to, moff) in QB for g in range(4) for hg in range(2)]

        def stage_a(ui):
            qc0, n, tp, to, moff, g, hg = units[ui]
            u2 = ui % 2
            bsS = u2 * 2
            SX = PS[:, bsS + 0, :].rearrange("p (t a q) -> p t a q", t=2, a=2)
            SY = PS[:, bsS + 1, :].rearrange("p (t a q) -> p t a q", t=2, a=2)
            jt0 = 4 * g + 2 * hg
            for hf, SB_, bk in ((0, SX, bsS + 0), (1, SY, bsS + 1)):
                ops = []
                for a_ in range(2):
                    jt = jt0 + a_
                    rhs = qT[64 * hf:64 * hf + 64, jt, qc0 - M0:qc0 - M0 + n]
                    ops.append((SB_[:, 0, a_, 0:n], KT[64 * hf:64 * hf + 64, g, tp * 128:(tp + 1) * 128], rhs, True, True))
                    ops.append((SB_[:, 1, a_, 0:n], KT[64 * hf:64 * hf + 64, g, to * 128:(to + 1) * 128], rhs, True, True))
                B.mms(ops, [KT_r, qT_r[jt0], qT_r[jt0 + 1]], [bank[bk]])
            pp, pp_r = Pp[u2]
            po, po_r = Po[u2]
            pp4 = pp.rearrange("p (a b) q -> p a b q", b=2)
            po4 = po.rearrange("p (a b) q -> p a b q", b=2)
            for hf, SB_, bk in ((0, SX, bsS + 0), (1, SY, bsS + 1)):
                B.act(pp4[:, :, hf, 0:n], SB_[:, 0, :, 0:n], AF.Exp, [bank[bk]], [pp_r], scale=0.125)
                B.act(po4[:, :, hf, 0:n], SB_[:, 1, :, 0:n], AF.Exp, [bank[bk]], [po_r], scale=0.125)
            mP = maskP[:, moff:moff + n].unsqueeze(1).to_broadcast([128, 4, n])
            mO = maskO[:, moff:moff + n].unsqueeze(1).to_broadcast([128, 4, n])
            if tp <= 1:
                B.stt(pp[:, :, 0:n], pp[:, :, 0:n], kvalid[:, tp:tp + 1], mP, ALU.mult, ALU.mult, [pp_r, kvalid_r, maskP_r], [pp_r])
            else:
                B.tt(pp[:, :, 0:n], pp[:, :, 0:n], mP, ALU.mult, [pp_r, maskP_r], [pp_r])
            if to <= 1:
                B.stt(po[:, :, 0:n], po[:, :, 0:n], kvalid[:, to:to + 1], mO, ALU.mult, ALU.mult, [po_r, kvalid_r, maskO_r], [po_r])
            else:
                B.tt(po[:, :, 0:n], po[:, :, 0:n], mO, ALU.mult, [po_r, maskO_r], [po_r])

        def stage_b(ui):
            qc0, n, tp, to, moff, g, hg = units[ui]
            u2 = ui % 2
            bsO = 4 + u2 * 2
            Ob = PS[:, bsO + 0, :].rearrange("p (e q) -> p e q", e=4)
            Db = PS[:, bsO + 1, :].rearrange("p (e q) -> p e q", e=4)
            jt0 = 4 * g + 2 * hg
            pp, pp_r = Pp[u2]
            po, po_r = Po[u2]
            B.mms([(Ob[:, :, 0:n], Vd[:, tp, g, :], pp[:, :, 0:n], True, False),
                   (Ob[:, :, 0:n], Vd[:, to, g, :], po[:, :, 0:n], False, True)],
                  [Vd_r[tp], Vd_r[to], pp_r, po_r], [bank[bsO + 0]])
            B.mms([(Db[:, :, 0:n], onesb, pp[:, :, 0:n], True, False),
                   (Db[:, :, 0:n], onesb, po[:, :, 0:n], False, True)],
                  [onesb_r, pp_r, po_r], [bank[bsO + 1]])
            ds, ds_r = dsm[u2]
            ds3 = ds.rearrange("p (e q) -> p e q", e=4)
            h0 = 8 * g + 4 * hg
            B.tt(ds3[:, :, 0:n], Db[:, :, 0:n], sinkexp[:, h0:h0 + 4].unsqueeze(2).to_broadcast([128, 4, n]), ALU.add,
                 [bank[bsO + 1], sinkexp_r], [ds_r])
            B.recip(ds3[:, :, 0:n], ds3[:, :, 0:n], [ds_r], [ds_r], fast=True)
            for hf in range(2):
                O4 = Ob.rearrange("p (a b) q -> p a b q", b=2)[64 * hf:64 * hf + 64, :, hf, 0:n]
                R4 = ds3.rearrange("p (a b) q -> p a b q", b=2)[64 * hf:64 * hf + 64, :, hf, 0:n]
                B.tt(AOT[64 * hf:64 * hf + 64, jt0:jt0 + 2, qc0 - M0:qc0 - M0 + n], O4, R4, ALU.mult,
                     [bank[bsO + 0], ds_r], [AOT_r[jt0], AOT_r[jt0 + 1]])

        for ui in range(len(units)):
            stage_a(ui)
            if ui >= 1:
                stage_b(ui - 1)
        stage_b(len(units) - 1)

        for s in range(8 if c3 > 4 else 0):
            kf, kf_r = ckf[s % 2]
            vf, vf_r = cvf[s % 2]
            kd, kd_r = cKd[s % 2]
            row0 = (p * 8 + s) * 128
            B.dma(B.qsp, kf, ck[row0:row0 + 128, :], [ck_r], [kf_r])
            B.dma(B.qsp, vf, cv[row0:row0 + 128, :], [cv_r], [vf_r])
            kf3 = kf.rearrange("p (g d) -> p g d", g=4)
            vf3 = vf.rearrange("p (g d) -> p g d", g=4)
            B.copy(B.DVE, kd[:, :, 0:64], kf3, [kf_r], [kd_r])
            B.copy(B.DVE, kd[:, :, 64:128], kf3, [kf_r], [kd_r])
            B.act(cVd[:, s, :, 0:64], vf3, AF.Copy, [vf_r], [cVd_r])
            B.act(cVd[:, s, :, 64:128], vf3, AF.Copy, [vf_r], [cVd_r])
            b = s % 2
            pT = PS[:, b, :].bitcast(BF16)[:, 0:512].rearrange("p (g c) -> p g c", g=4)
            B.transposes([(pT[:, g, :], kd[:, g, :], identb) for g in range(4)], [kd_r, identb_r], [bank[b]])
            B.act(cKT[:, s, :, :], pT, AF.Copy, [bank[b]], [cKT_r])
        SC0 = 516
        for g in range(4 if c3 > 5 else 0):
            bs = (g % 2) * 4
            ScX = PS[:, 0, 0:128]
            ScY = PS[:, 1, 0:128]
            SnX = PS[:, 2, 0:128]
            SnY = PS[:, 3, 0:128]
            Ob = PS[:, 4, 0:256]
            Db = PS[:, 5, 0:256]
            bs = 2
            for hf, Sc_, Sn_, bc, bn in ((0, ScX, SnX, 0, 2), (1, ScY, SnY, 1, 3)):
                ops = []
                for s in range(8):
                    for a in range(4):
                        jt = 4 * g + a
                        ops.append((Sc_[:, a * 32 + s * 4:a * 32 + s * 4 + 4], cKT[64 * hf:64 * hf + 64, s, g, :],
                                    qT[64 * hf:64 * hf + 64, jt, SC0 + 4 * s:SC0 + 4 * s + 4], True, True))
                B.mms(ops, [cKT_r] + qT_r[4 * g:4 * g + 4], [bank[bc]])
                ops = []
                for a in range(4):
                    jt = 4 * g + a
                    ops.append((Sn_[0:32, a * 32:(a + 1) * 32], KT[64 * hf:64 * hf + 64, g, 768:800],
                                qT[64 * hf:64 * hf + 64, jt, SC0:SC0 + 32], True, True))
                B.mms(ops, [KT_r] + qT_r[4 * g:4 * g + 4], [bank[bn]])
            if c3 <= 6:
                continue
            Pc5 = Pc.rearrange("p (a b c) -> p a b c", a=4, b=2)
            Pn5 = Pn.rearrange("p (a b c) -> p a b c", a=4, b=2)
            for hf, Sc_, Sn_, bc, bn in ((0, ScX, SnX, 0, 2), (1, ScY, SnY, 1, 3)):
                B.act(Pc5[:, :, hf, :], Sc_.rearrange("p (a c) -> p a c", a=4), AF.Exp, [bank[bc]], [Pc_r], scale=0.125)
                B.act(Pn5[0:32, :, hf, :], Sn_[0:32, :].rearrange("p (a c) -> p a c", a=4), AF.Exp, [bank[bn]], [Pn_r], scale=0.125)
            Pc3 = Pc.rearrange("p (a t) -> p a t", t=4)
            B.tt(Pc3, Pc3, maskc.unsqueeze(1).to_broadcast([128, 64, 4]), ALU.mult, [Pc_r, maskc_r], [Pc_r])
            Pn3 = Pn[0:32, :].rearrange("p (e c) -> p e c", e=8)
            B.tt(Pn3, Pn3, masknb[0:32, :].unsqueeze(1).to_broadcast([32, 8, 32]), ALU.mult, [Pn_r, masknb_r], [Pn_r])
            if c3 <= 7:
                continue
            Pc4 = Pc.rearrange("p (e s t) -> p e s t", e=8, s=8)
            Ob4 = Ob.rearrange("p (e s t) -> p e s t", e=8, s=8)
            ops = [(Ob, Vd[0:32, 6, g, :], Pn[0:32, :], True, False)]
            for s in range(8):
                ops.append((Ob4[:, :, s, :], cVd[:, s, g, :], Pc4[:, :, s, :], False, s == 7))
            B.mms(ops, [Vd_r[6], cVd_r, Pn_r, Pc_r], [bank[4]])
            B.mms([(Db, onesb[0:32, :], Pn[0:32, :], True, False), (Db, onesb, Pc, False, True)],
                  [onesb_r, Pn_r, Pc_r], [bank[5]])
            ds, ds_r = dsm[g % 2]
            ds3 = ds[:, 0:256].rearrange("p (e c) -> p e c", e=8)
            Db3 = Db.rearrange("p (e c) -> p e c", e=8)
            Ob3 = Ob.rearrange("p (e c) -> p e c", e=8)
            B.tt(ds3, Db3, sinkexp[:, 8 * g:8 * g + 8].unsqueeze(2).to_broadcast([128, 8, 32]), ALU.add, [bank[5], sinkexp_r], [ds_r])
            B.recip(ds3, ds3, [ds_r], [ds_r], fast=True)
            for hf in range(2):
                O4 = Ob3.rearrange("p (a b) c -> p a b c", b=2)[64 * hf:64 * hf + 64, :, hf, :]
                R4 = ds3.rearrange("p (a b) c -> p a b c", b=2)[64 * hf:64 * hf + 64, :, hf, :]
                B.tt(AOT[64 * hf:64 * hf + 64, 4 * g:4 * g + 4, SC0:SC0 + 32], O4, R4, ALU.mult,
                     [bank[4], ds_r], AOT_r[4 * g:4 * g + 4])

        B.memset(B.DVE, acc, 0.0, [acc_r])
        for j in range(16):
            B.act(sqt, AOT[:, j, :], AF.Square, [AOT_r[j]], [sqt_r])
            B.tt(acc, acc, sqt, ALU.add, [acc_r, sqt_r], [acc_r])
        finish_norm(rstd_t, rstd_t_r, 2048)
        for j in range(16):
            B.stt(AOT[:, j, :], AOT[:, j, :], gattn[:, j:j + 1], rstd_t, ALU.mult, ALU.mult, [AOT_r[j], pfm_sr, rstd_t_r], [AOT_r[j]])
        if p == 0:
            B.tap("AOT", AOT.rearrange("p a b -> p (a b)"), AOT_r, [128, 16 * NM])
        if stop_after <= 4:
            break

        fT, fT_r = B.sb("fT", ZC, [128, 32, NM], BF16, nres=32)
        xres = [B.sb("xres%d" % i, ZX + 17536 + i * 12288, [128, 6, 512], F32) for i in range(2)]
        o = S0
        hm = [B.sb("hm%d" % i, o + i * 2240, [128, NM], F32) for i in range(2)]; o += 4480
        B.memset(B.DVE, acc, 0.0, [acc_r])

        def load_xres(gi):
            xr, xr_r = xres[gi % 2]
            for pc, (r0, nr) in enumerate(MP):
                B.dma(B.qsp, xr[:nr, pc, :], xh[p * TI + r0:p * TI + r0 + nr, gi * 512:(gi + 1) * 512], [xh_r], [xr_r])

        load_xres(0)

        def rhs4(k, a, b_):
            return (AOT[:, k, a - M0:b_ - M0] if k < 16 else convT[:, k - 16, a - M0:b_ - M0])

        for s in range(16):
            if s % 2 == 0 and s // 2 + 1 < 8:
                load_xres(s // 2 + 1)
            xr, xr_r = xres[(s // 2) % 2]
            pis = pair_mm(32, rhs4, AOT_r + convT_r)
            for jj in range(2):
                j = 2 * s + jj
                pi = pis[jj]
                pm = pair_view(pi)[:, M0:M1]
                xc0 = (s % 2) * 256 + jj * 128
                p3 = pair_view(3)
                B.transposes([(p3[:, r0:r0 + nr], xr[:nr, pc, xc0:xc0 + 128], identf[:nr, :nr]) for pc, (r0, nr) in enumerate(MP)],
                             [xr_r, identf_r], pair_res(3))
                h, h_r = hm[j % 2]
                B.act(h, pm, AF.Copy, pair_res(pi), [h_r])
                B.tt(h, h, p3[:, M0:M1], ALU.add, [h_r] + pair_res(3), [h_r])
                B.dma(B.qsp, hmid[(p * 32 + j) * 128:(p * 32 + j + 1) * 128, :], h, [h_r], [hmid_rs[p][j]])
                B.act(sqt, h, AF.Square, [h_r], [sqt_r])
                B.tt(acc, acc, sqt, ALU.add, [acc_r, sqt_r], [acc_r])
                B.ts(fT[:, j, :], h, gffn[:, j:j + 1], ALU.mult, [h_r, pfm_sr], [fT_r[j]])
        finish_norm(rstd_f, rstd_f_r, D)
        if p == 0:
            B.tap("fT", fT.rearrange("p a b -> p (a b)"), fT_r, [128, 32 * NM])
            B.tap("rstd_f", rstd_f, rstd_f_r, [128, NM])
        if stop_after <= 5:
            break

        actT, actT_r = B.sb("actT", ZX, [128, NFT, NO], BF16, nres=NFT)
        o = S0 + 4480 + 5504
        gs = [B.sb("gs%d" % i, S0 + i * 2240, [128, NM], F32) for i in range(2)]
        yg, yg_r = B.sb("yg", o, [128, NM], F32); o += 2240
        sg_ = [B.sb("sg%d" % i, o + i * 2240, [128, NM], F32) for i in range(4)]; o += 8960
        extf, extf_r = B.sb("extf", o, [128, 8, 6], F32); o += 192
        ysf, ysf_r = B.sb("ysf", o, [128, 8, 4], F32); o += 128
        stgf, stgf_r = B.sb("stgf", o, [128, 2, 18], F32); o += 192
        stOf = [B.sb("stOf%d" % i, o + i * 1024, [18, 256], F32) for i in range(2)]; o += 2048
        assert o <= ZX, o
        def rhs5(k, a, b_):
            return fT[:, k, a - M0:b_ - M0]

        for s in range(43):
            pis = pair_mm(32, rhs5, fT_r)
            for jj in range(2):
                j = 2 * s + jj
                pi = pis[jj]
                pm = pair_view(pi)[:, M0:M1]
                g_, g_r = gs[jj]
                B.tt(g_, pm, rstd_f, ALU.mult, pair_res(pi) + [rstd_f_r], [g_r])
                B.act(yg[:, 2:516], g_[:, 0:514], AF.Copy, [g_r, pfm_sr], [yg_r], scale=fconvw[:, j, 0:1])
                B.stt(yg[:, 2:516], g_[:, 1:515], fconvw[:, j, 1:2], yg[:, 2:516], ALU.mult, ALU.add, [g_r, pfm_sr, yg_r], [yg_r])
                B.stt(yg[:, 2:516], g_[:, 2:516], fconvw[:, j, 2:3], yg[:, 2:516], ALU.mult, ALU.add, [g_r, pfm_sr, yg_r], [yg_r])
                B.copy(B.DVE, extf[:, :, 0:2], stT[:, j, :].rearrange("p (s r) -> p s r", r=2), [stT_r], [extf_r])
                B.copy(B.DVE, extf[:, :, 2:6], g_[:, 516:548].rearrange("p (s t) -> p s t", t=4), [g_r], [extf_r])
                B.ts(ysf, extf[:, :, 0:4], fconvw[:, j, 0:1], ALU.mult, [extf_r, pfm_sr], [ysf_r])
                B.stt(ysf, extf[:, :, 1:5], fconvw[:, j, 1:2], ysf, ALU.mult, ALU.add, [extf_r, pfm_sr, ysf_r], [ysf_r])
                B.stt(ysf, extf[:, :, 2:6], fconvw[:, j, 2:3], ysf, ALU.mult, ALU.add, [extf_r, pfm_sr, ysf_r], [ysf_r])
                B.copy(B.DVE, yg[:, 516:548].rearrange("p (s t) -> p s t", t=4), ysf, [ysf_r], [yg_r])
                sgb, sgb_r = sg_[(s % 2) * 2 + jj]
                B.act(sgb[:, 4:NM], yg[:, 4:NM], AF.Silu, [yg_r], [sgb_r])
                B.tt(sgb[:, 4:NM], sgb[:, 4:NM], rstd_f[:, 4:NM], ALU.mult, [sgb_r, rstd_f_r], [sgb_r])
                B.copy(B.DVE, stgf[:, jj, 0:2], g_[:, 514:516], [g_r], [stgf_r])
                B.copy(B.DVE, stgf[:, jj, 2:18].rearrange("p (s r) -> p s r", r=2), extf[:, :, 4:6], [extf_r], [stgf_r])
            pis = pair_mm(32, rhs5, fT_r)
            pt = PS[:, 6 + s % 2, 0:256].rearrange("p (a c) -> p a c", a=2)
            B.transposes([(pt[0:18, jj, :], stgf[:, jj, :], identf) for jj in range(2)], [stgf_r, identf_r], [bank[6 + s % 2]])
            so, so_r = stOf[s % 2]
            B.act(so.rearrange("p (a c) -> p a c", a=2), pt[0:18], AF.Copy, [bank[6 + s % 2]], [so_r])
            B.dma(B.qsp, nf_p[p * 2:(p + 1) * 2, s * 256:(s + 1) * 256], so[0:2, :], [so_r], [])
            B.dma(B.qsp, nf_s[p * 16:(p + 1) * 16, s * 256:(s + 1) * 256], so[2:18, :], [so_r], [])
            for jj in range(2):
                j = 2 * s + jj
                pi = pis[jj]
                pm = pair_view(pi)[:, M0:M1]
                sgb, sgb_r = sg_[(s % 2) * 2 + jj]
                B.tt(actT[:, j, :], pm[:, 4:NM], sgb[:, 4:NM], ALU.mult, pair_res(pi) + [sgb_r], [actT_r[j]])
        if p == 0:
            B.tap("actT", actT[:, 0:4, :].rearrange("p a b -> p (a b)"), actT_r[0:4], [128, 4 * NO])
        if stop_after <= 6:
            break

        o = S0
        hmb = [B.sb("hmb%d" % i, o + i * 2240, [128, NM], F32) for i in range(4)]; o += 8960
        hob = [B.sb("hob%d" % i, o + i * 2240, [128, NO], F32) for i in range(2)]; o += 4480
        sq6, sq6_r = B.sb("sq6", o, [128, NO], F32); o += 2240
        acc6, acc6_r = B.sb("acc6", o, [128, NM], F32); o += 2240
        rstd_y, rstd_y_r = B.sb("rstd_y", o, [128, NM], F32); o += 2240
        B.memset(B.DVE, acc6, 0.0, [acc6_r])
        def rhs6(k, a, b_):
            return actT[:, k, a - O0:b_ - O0]

        for sp in range(16):
            for jj in range(2):
                j = 2 * sp + jj
                hb, hb_r = hmb[(sp % 2) * 2 + jj]
                B.dma(B.qsp, hb, hmid[(p * 32 + j) * 128:(p * 32 + j + 1) * 128, :], [hmid_rs[p][j]], [hb_r])
            pis = pair_mm(NFT, rhs6, actT_r, c0=O0)
            for jj in range(2):
                j = 2 * sp + jj
                pv_ = pair_view(pis[jj])
                hb, hb_r = hmb[(sp % 2) * 2 + jj]
                ho, ho_r = hob[jj]
                B.tt(ho, pv_[:, O0:M1], hb[:, 4:NM], ALU.add, pair_res(pis[jj]) + [hb_r], [ho_r])
                B.dma(B.qsp, hout[(p * 32 + j) * 128:(p * 32 + j + 1) * 128, :], ho, [ho_r], [hout_rs[p][j]])
                B.act(sq6, ho, AF.Square, [ho_r], [sq6_r])
                B.tt(acc6[:, 4:NM], acc6[:, 4:NM], sq6, ALU.add, [acc6_r, sq6_r], [acc6_r])
        p3 = pair_view(3)
        B.mms([(p3[:, 252:512], onesf, acc6[:, 0:260], True, True), (p3[:, 512:800], onesf, acc6[:, 260:NM], True, True)],
              [onesf_r, acc6_r], pair_res(3))
        B.act(rstd_y, p3[:, M0:M1], AF.Ln, pair_res(3) + [epsc_r], [rstd_y_r], scale=1.0 / D, bias=epsc)
        B.act(rstd_y, rstd_y, AF.Exp, [rstd_y_r], [rstd_y_r], scale=-0.5)
        if stop_after <= 7:
            break

        hin = [B.sb("hin%d" % i, ZX + i * 2240, [128, NO], F32) for i in range(3)]
        yT = [B.sb("yT%d" % i, ZX + 6720 + i * 2240, [128, NO], F32) for i in range(2)]
        ytok = [B.sb("ytok%d" % i, ZX + 11264 + i * 10240, [128, 5, 512], F32) for i in range(2)]
        def load_hin(jx):
            B.dma(B.qsp, hin[jx % 3][0], hout[(p * 32 + jx) * 128:(p * 32 + jx + 1) * 128, :], [hout_rs[p][jx]], [hin[jx % 3][1]])

        load_hin(0)
        load_hin(1)
        for j in range(32):
            if j + 2 < 32:
                load_hin(j + 2)
            hi, hi_r = hin[j % 3]
            yt, yt_r = yT[j % 2]
            B.stt(yt, hi, gfin[:, j:j + 1], rstd_y[:, 4:NM], ALU.mult, ALU.mult, [hi_r, pfm_sr, rstd_y_r], [yt_r])
            b0 = (j % 2) * 2
            pa = PS[:, b0, :].rearrange("p (a c) -> p a c", a=4)
            pb = PS[:, b0 + 1, 0:128]
            ops = [(pa[:, i, :], yt[:, i * 128:(i + 1) * 128], identf) for i in range(4)]
            ops.append((pb[0:32, :], yt[:, 512:544], identf))
            B.transposes(ops, [yt_r, identf_r], [bank[b0], bank[b0 + 1]])
            yk, yk_r = ytok[(j // 4) % 2]
            jc = (j % 4) * 128
            B.act(yk[:, 0:4, jc:jc + 128], pa, AF.Copy, [bank[b0]], [yk_r])
            B.copy(B.DVE, yk[0:32, 4, jc:jc + 128], pb[0:32, :], [bank[b0 + 1]], [yk_r])
            if j % 4 == 3:
                c0 = (j // 4) * 512
                for i in range(4):
                    B.dma(B.qsp, y_p[p * 512 + i * 128:p * 512 + (i + 1) * 128, c0:c0 + 512], yk[:, i, :], [yk_r], [])
                B.dma(B.qsp, y_s[p * 32:(p + 1) * 32, c0:c0 + 512], yk[0:32, 4, :], [yk_r], [])

    fin = B.ACT
    for q in (B.qsp, B.qw):
        for i, (sem, key) in enumerate(q.sems):
            if q.tot[i] > 0 and fin.seen.get(key, 0) < q.tot[i]:
                fin.h.wait_ge(sem, q.tot[i])
                fin.seen[key] = q.tot[i]
    return B


_CACHE = {}


def _rope_tables(pos):
    half = 8
    inv = (500000.0 ** (-np.arange(half, dtype=np.float32) * 2.0 / 16)).astype(np.float32)
    ang = pos.astype(np.float32)[:, None] * inv[None, :]
    return np.cos(ang).astype(np.float32), np.sin(ang).astype(np.float32)


def _const_tables(v):
    pos = np.zeros(TI, np.int64)
    pos[:768] = 512 * v - 240 + np.arange(768)
    for s in range(8):
        for t in range(4):
            pos[768 + 4 * s + t] = 8192 + t
    posc = np.maximum(pos, 0)
    cos, sin = _rope_tables(posc)
    CT = np.ones((128, NM), np.float32)
    ST = np.zeros((128, NM), np.float32)
    for hb in (0, 64):
        CT[hb:hb + 8, :] = cos[M0:M1].T
        CT[hb + 8:hb + 16, :] = cos[M0:M1].T
        ST[hb:hb + 8, :] = -sin[M0:M1].T
        ST[hb + 8:hb + 16, :] = sin[M0:M1].T
    tfm = np.concatenate([CT, ST], axis=1)
    ttok = np.zeros((128, 7, 32), np.float32)
    for i, (r0, nr) in enumerate(TILES):
        ttok[:nr, i, 0:8] = cos[r0:r0 + nr]
        ttok[:nr, i, 8:16] = cos[r0:r0 + nr]
        ttok[:nr, i, 16:24] = -sin[r0:r0 + nr]
        ttok[:nr, i, 24:32] = sin[r0:r0 + nr]
    kval = np.ones((128, 2), np.float32)
    if v == 0:
        kval[:, 0] = 0.0
        kval[:112, 1] = 0.0
    return tfm, ttok.reshape(128, 224), kval


def _c128():
    c = np.zeros((128, 644), np.float32)
    i = np.arange(128)
    c[:, 0:128] = np.eye(128, dtype=np.float32)
    c[:, 128:256] = (i[:, None] <= i[None, :])
    c[:, 256:384] = (i[:, None] > i[None, :])
    perm = np.zeros((128, 128), np.float32)
    for hb in (0, 64):
        for d in range(8):
            perm[hb + d + 8, hb + d] = 1.0
            perm[hb + d, hb + d + 8] = 1.0
    c[:, 384:512] = perm
    c[:, 512:640] = 1.0
    c[:, 640:644] = (i[:, None] > np.arange(4)[None, :])
    maskn = np.zeros((32, 32), np.float32)
    for s2 in range(8):
        for t2 in range(4):
            for t in range(4):
                if t2 <= t:
                    maskn[4 * s2 + t2, 4 * s2 + t] = 1.0
    return c, maskn


def _tile_w(W):
    K, N = W.shape
    nkc, npair = K // 128, N // 256
    nslab = -(-nkc // 16)
    out = np.zeros((npair * nslab * 128, 16 * 256), np.float32)
    Wr = W.reshape(nkc, 128, npair, 256)
    for pr in range(npair):
        for ks in range(nslab):
            k0 = ks * 16
            nk = min(16, nkc - k0)
            r = (pr * nslab + ks) * 128
            out[r:r + 128, :nk * 256] = Wr[k0:k0 + nk, :, pr, :].transpose(1, 0, 2).reshape(128, nk * 256)
    return out


def kernel(x_prompt, x_sample, cache_k, cache_v, state_conv, state_ffn_conv, meta_tokens,
           g_mix, w_in, attn_sinks, conv_w, g_attn_out, g_conv_out, w_out, g_ffn,
           w_gate_up, ffn_conv_w, w_down, g_final, _debug=False, _ncores=8, _stop_after=99):
    f32 = np.float32
    x_prompt = np.asarray(x_prompt, f32); x_sample = np.asarray(x_sample, f32)
    ncores = _ncores
    npass = 2
    key = (npass, _debug, _stop_after)
    if key not in _CACHE:
        _CACHE[key] = build_program(npass=npass, debug=_debug, stop_after=_stop_after)
    B = _CACHE[key]

    xp = x_prompt[0]
    xs = x_sample
    fm = lambda a, n: np.ascontiguousarray(np.asarray(a, f32).reshape(n, 128).T)
    pfm = np.zeros((128, NPFM), f32)
    pfm[:, 0:32] = fm(g_mix[0], 32)
    pfm[:, 32:64] = fm(g_ffn[0], 32)
    pfm[:, 64:80] = fm(g_attn_out[0], 16)
    pfm[:, 80:96] = fm(g_conv_out[0], 16)
    cw = np.asarray(conv_w[0], f32)
    pfm[:, 96:144] = np.transpose(cw.reshape(3, 16, 128), (2, 1, 0)).reshape(128, 48)
    fw = np.asarray(ffn_conv_w[0], f32)
    pfm[:, 144:402] = np.transpose(fw.reshape(3, NFT, 128), (2, 1, 0)).reshape(128, 258)
    pfm[:, 402:434] = fm(g_final, 32)
    c128, maskn = _c128()
    w_in2 = _tile_w(np.asarray(w_in[0], f32)); w_out2 = _tile_w(np.asarray(w_out[0], f32))
    w_gu2 = _tile_w(np.asarray(w_gate_up[0], f32)); w_dn2 = _tile_w(np.asarray(w_down[0], f32))
    ck_all = np.asarray(cache_k[0], f32).reshape(128, 128, 256)
    cv_all = np.asarray(cache_v[0], f32).reshape(128, 128, 256)
    sc_all = np.asarray(state_conv[0], f32)
    sf_all = np.asarray(state_ffn_conv[0], f32)
    meta = np.asarray(meta_tokens, f32)

    in_maps = []
    for c in range(ncores):
        xh = np.zeros((npass, TI, D), f32)
        tfm_l, ttok_l, kval_l = [], [], []
        for p in range(npass):
            v = 2 * c + p
            if v == 0:
                xh[p, 240:256] = meta
            else:
                xh[p, 0:256] = xp[512 * v - 256:512 * v]
            xh[p, 256:768] = xp[512 * v:512 * v + 512]
            xh[p, 768:800] = xs[8 * v:8 * v + 8].reshape(32, D)
            a, b, k = _const_tables(v)
            tfm_l.append(a); ttok_l.append(b); kval_l.append(k)
        m = {
            "xh": xh.reshape(npass * TI, D),
            "ck": ck_all[16 * c:16 * c + 16].reshape(16 * 128, 256),
            "cv": cv_all[16 * c:16 * c + 16].reshape(16 * 128, 256),
            "sc": sc_all[16 * c:16 * c + 16].reshape(32, 2048),
            "sf": sf_all[16 * c:16 * c + 16].reshape(32, DFF),
            "w_in": w_in2, "w_out": w_out2, "w_gu": w_gu2, "w_dn": w_dn2,
            "pfm": pfm, "sinks": np.asarray(attn_sinks, f32).reshape(1, 32),
            "c128": c128, "maskn": maskn,
            "tfm": np.concatenate(tfm_l, 0), "ttok": np.concatenate(ttok_l, 0), "kval": np.concatenate(kval_l, 0),
        }
        in_maps.append(m)
    res = run_bass_kernel_spmd(B.nc, in_maps, core_ids=list(range(ncores)))
    R = res.results
    if _debug:
        return R
    y_prompt = np.concatenate([R[c]["y_p"] for c in range(ncores)], 0).reshape(1, ncores * 1024, D)
    y_sample = np.concatenate([R[c]["y_s"] for c in range(ncores)], 0).reshape(ncores * 16, 4, D)
    last = R[ncores - 1]
    nk_p = last["nk_p"][128:256].reshape(1, 1, 128, 4, 64)
    nv_p = last["nv_p"][128:256].reshape(1, 1, 128, 4, 64)
    nc_p = last["nc_p"][2:4].reshape(1, 1, 2, 2048)
    nf_p = last["nf_p"][2:4].reshape(1, 1, 2, DFF)
    nk_s = np.concatenate([R[c]["nk_s"] for c in range(ncores)], 0).reshape(1, ncores * 16, 128, 4, 64)
    nv_s = np.concatenate([R[c]["nv_s"] for c in range(ncores)], 0).reshape(1, ncores * 16, 128, 4, 64)
    nc_s = np.concatenate([R[c]["nc_s"] for c in range(ncores)], 0).reshape(1, ncores * 16, 2, 2048)
    nf_s = np.concatenate([R[c]["nf_s"] for c in range(ncores)], 0).reshape(1, ncores * 16, 2, DFF)
    return (y_prompt.astype(f32), y_sample.astype(f32), nk_p.astype(f32), nv_p.astype(f32), nc_p.astype(f32),
            nf_p.astype(f32), nk_s.astype(f32), nv_s.astype(f32), nc_s.astype(f32), nf_s.astype(f32))
```

```python
import numpy as np
import concourse.bass as bass
import concourse.mybir as mybir
from concourse.bass_utils import run_bass_kernel_spmd

F32, BF16, U8 = mybir.dt.float32, mybir.dt.bfloat16, mybir.dt.uint8
AF = mybir.ActivationFunctionType
ALU = mybir.AluOpType

D = 4096
DFF = 11008
NFT = 86
TI = 800
M0, M1 = 252, 800
NM = M1 - M0
O0 = 256
NO = M1 - O0
TILES = [(0, 128), (128, 128), (256, 128), (384, 128), (512, 128), (640, 128), (768, 32)]
MP = [(252, 4), (256, 128), (384, 128), (512, 128), (640, 128), (768, 32)]
OP = [(256, 128), (384, 128), (512, 128), (640, 128), (768, 32)]
EPS = 1e-5
NPFM = 32 + 32 + 16 + 16 + 48 + 258 + 32

G0 = 0
S0 = 6912
ZX = 32768
ZC = 126464
ZR = 161792
ARENA = 210944


class Res:
    __slots__ = ("name", "space", "lo", "hi", "w", "r", "al")

    def __init__(self, name, space, lo, hi):
        self.name, self.space, self.lo, self.hi = name, space, lo, hi
        self.w = {}
        self.r = {}
        self.al = [self]


class Eng:
    def __init__(self, name, h, sem, key):
        self.name, self.h, self.sem, self.key = name, h, sem, key
        self.cnt = 0
        self.seen = {}


class DQ:
    def __init__(self, eng, sems):
        self.eng = eng
        self.sems = sems
        self.tot = [0] * len(sems)
        self.n = 0


class Builder:
    def __init__(self, debug=False):
        self.debug = debug
        nc = bass.Bass("TRN2", target_bir_lowering=False)
        self.nc = nc
        self.res = []
        self.semh = {}
        self.nkey = 0
        self.PE = self.mk_eng("pe", nc.tensor)
        self.ACT = self.mk_eng("act", nc.scalar)
        self.DVE = self.mk_eng("dve", nc.vector)
        self.POOL = self.mk_eng("pool", nc.gpsimd)
        self.SP = Eng("sp", nc.sync, None, -1)
        self.qsp = DQ(self.SP, [self.mk_sem("dsp%d" % i) for i in range(16)])
        self.qw = DQ(self.POOL, [self.mk_sem("dw%d" % i) for i in range(NSLOT)])
        self.arena = nc.alloc_sbuf_tensor("arena", [128, ARENA], U8)
        self.psum = nc.alloc_psum_tensor("psum", [128, 8, 512], F32)
        self.bank = [self.new_res("bank%d" % b, "psum", b * 2048, (b + 1) * 2048) for b in range(8)]
        self.dbg_outs = []

    def mk_sem(self, name):
        s = self.nc.alloc_semaphore(name)
        k = self.nkey
        self.nkey += 1
        self.semh[k] = s
        return (s, k)

    def mk_eng(self, name, h):
        s, k = self.mk_sem("s_" + name)
        return Eng(name, h, s, k)

    def new_res(self, name, space, lo, hi):
        r = Res(name, space, lo, hi)
        for o in self.res:
            if o.space == space and o.lo < hi and lo < o.hi:
                o.al.append(r)
                r.al.append(o)
        self.res.append(r)
        return r

    def sb(self, name, off, shape, dt, nres=0):
        n = 1
        for s in shape[1:]:
            n *= s
        esz = 4 if dt == F32 else 2
        nb = n * esz
        assert off % 4 == 0 and off + nb <= ARENA, (name, off, nb)
        v = self.arena[0:shape[0], off:off + nb].bitcast(dt)
        if len(shape) == 3:
            v = v.rearrange("p (a b) -> p a b", a=shape[1])
        elif len(shape) == 4:
            v = v.rearrange("p (a b c) -> p a b c", a=shape[1], b=shape[2])
        if nres:
            per = nb // nres
            rs = [self.new_res("%s%d" % (name, i), "sbuf", off + i * per, off + (i + 1) * per) for i in range(nres)]
            return v, rs
        return v, self.new_res(name, "sbuf", off, off + nb)

    def dres(self, name):
        return self.new_res(name, "d:" + name, 0, 1)

    def sync(self, eng, reads, writes):
        raw = {}
        oth = {}
        for r in reads:
            for a in r.al:
                for k, v in a.w.items():
                    if v > raw.get(k, 0):
                        raw[k] = v
        for w in writes:
            for a in w.al:
                for k, v in a.w.items():
                    if v > oth.get(k, 0):
                        oth[k] = v
                for k, v in a.r.items():
                    if v > oth.get(k, 0):
                        oth[k] = v
        need = {}
        for k, v in raw.items():
            if k == eng.key and eng.name == "pe":
                continue
            need[k] = v
        for k, v in oth.items():
            if k == eng.key:
                continue
            if v > need.get(k, 0):
                need[k] = v
        for k, v in need.items():
            if eng.seen.get(k, 0) < v:
                eng.h.wait_ge(self.semh[k], v)
                eng.seen[k] = v

    def mark(self, key, val, reads, writes):
        for r in reads:
            if r.r.get(key, 0) < val:
                r.r[key] = val
        for w in writes:
            w.w = {key: val}
            w.r = {}

    def done(self, eng, ins, reads, writes):
        ins.then_inc(eng.sem, 1)
        eng.cnt += 1
        self.mark(eng.key, eng.cnt, reads, writes)

    def dma(self, q, out, in_, reads, writes):
        i = q.n % len(q.sems)
        q.n += 1
        sem, key = q.sems[i]
        eng = q.eng
        if q.tot[i] > 0 and eng.seen.get(key, 0) < q.tot[i]:
            eng.h.wait_ge(sem, q.tot[i])
            eng.seen[key] = q.tot[i]
        self.sync(eng, reads, writes)
        ins = eng.h.dma_start(out=out, in_=in_, max_dma_last_dim=16384)
        ins.then_inc(sem, 16)
        q.tot[i] += 16
        self.mark(key, q.tot[i], reads, writes)

    def act(self, out, in_, func, reads, writes, scale=1.0, bias=None, accum=None):
        self.sync(self.ACT, reads, writes)
        kw = {}
        if bias is not None:
            kw["bias"] = bias
        if accum is not None:
            kw["accum_out"] = accum
        ins = self.nc.scalar.activation(out=out, in_=in_, func=func, scale=scale, **kw)
        self.done(self.ACT, ins, reads, writes)

    def tt(self, out, in0, in1, op, reads, writes, eng=None):
        eng = eng or self.DVE
        self.sync(eng, reads, writes)
        ins = eng.h.tensor_tensor(out=out, in0=in0, in1=in1, op=op)
        self.done(eng, ins, reads, writes)

    def ts(self, out, in0, s1, op0, reads, writes, s2=None, op1=None, eng=None):
        eng = eng or self.DVE
        self.sync(eng, reads, writes)
        if op1 is None:
            ins = eng.h.tensor_scalar(out=out, in0=in0, scalar1=s1, scalar2=None, op0=op0)
        else:
            ins = eng.h.tensor_scalar(out=out, in0=in0, scalar1=s1, scalar2=s2, op0=op0, op1=op1)
        self.done(eng, ins, reads, writes)

    def stt(self, out, in0, scalar, in1, op0, op1, reads, writes):
        self.sync(self.DVE, reads, writes)
        ins = self.nc.vector.scalar_tensor_tensor(out=out, in0=in0, scalar=scalar, in1=in1, op0=op0, op1=op1)
        self.done(self.DVE, ins, reads, writes)

    def recip(self, out, in_, reads, writes, fast=False):
        if fast:
            self.act(out, in_, AF.Ln, reads, writes)
            self.act(out, out, AF.Exp, writes, writes, scale=-1.0)
            return
        self.sync(self.DVE, reads, writes)
        ins = self.nc.vector.reciprocal(out=out, in_=in_)
        self.done(self.DVE, ins, reads, writes)

    def copy(self, eng, out, in_, reads, writes):
        self.sync(eng, reads, writes)
        ins = eng.h.tensor_copy(out=out, in_=in_)
        self.done(eng, ins, reads, writes)

    def memset(self, eng, ap, val, writes):
        self.sync(eng, [], writes)
        ins = eng.h.memset(ap, val)
        self.done(eng, ins, [], writes)

    def mms(self, ops, reads, writes):
        self.sync(self.PE, reads, writes)
        ins = None
        for (o, l, r, st, sp) in ops:
            ins = self.nc.tensor.matmul(o, l, r, start=st, stop=sp)
        self.done(self.PE, ins, reads, writes)

    def transposes(self, ops, reads, writes):
        self.sync(self.PE, reads, writes)
        ins = None
        for (o, i, ident) in ops:
            ins = self.nc.tensor.transpose(o, i, ident)
        self.done(self.PE, ins, reads, writes)

    def tap(self, name, view, res, shape):
        if not self.debug:
            return
        d = self.nc.dram_tensor("dbg_" + name, list(shape), view.dtype, kind="ExternalOutput").ap()
        r = self.dres("dbg_" + name)
        n = shape[1]
        for c0 in range(0, n, 4096):
            c1 = min(n, c0 + 4096)
            self.dma(self.qsp, d[:, c0:c1], view[:, c0:c1], [res] if not isinstance(res, list) else res, [])
        self.dbg_outs.append(("dbg_" + name, r))


NSLOT = 6


class WStream:
    def __init__(self, B):
        self.B = B
        self.slabs = []
        self.issued = 0
        self.taken = 0
        self.donec = 0
        self.slots = [B.sb("ring%d" % s, ZR + s * 8192, [128, 16, 256], BF16) for s in range(NSLOT)]
        self.wres = {}

    def add(self, W, wname, k0, nk, c0):
        src = W[k0 * 128:(k0 + nk) * 128, c0:c0 + 256].rearrange("(k p) c -> p k c", p=128)
        v, r = self.slots[len(self.slabs) % NSLOT]
        if wname not in self.wres:
            self.wres[wname] = self.B.dres(wname)
        self.slabs.append((src, v[:, 0:nk, :], r, self.wres[wname]))

    def fill(self):
        while self.issued < len(self.slabs) and self.issued < self.donec + NSLOT:
            src, v, r, wr = self.slabs[self.issued]
            self.B.dma(self.B.qw, v, src, [wr], [r])
            self.issued += 1

    def take(self):
        assert self.taken < self.issued, (self.taken, self.issued)
        src, v, r, wr = self.slabs[self.taken]
        self.taken += 1
        return v, r

    def done(self):
        self.donec += 1
        self.fill()


def build_program(npass=2, debug=False, stop_after=99):
    B = Builder(debug)
    nc = B.nc

    def din(name, shape):
        return nc.dram_tensor(name, list(shape), F32, kind="ExternalInput").ap(), B.dres(name)

    def dout(name, shape, kind="ExternalOutput"):
        return nc.dram_tensor(name, list(shape), F32, kind=kind).ap(), B.dres(name)

    xh, xh_r = din("xh", [npass * TI, D])
    ck, ck_r = din("ck", [npass * 8 * 128, 256])
    cv, cv_r = din("cv", [npass * 8 * 128, 256])
    sc, sc_r = din("sc", [npass * 16, 2048])
    sf, sf_r = din("sf", [npass * 16, DFF])
    w_in, _ = din("w_in", [D, 8704])
    w_out, _ = din("w_out", [D, D])
    w_gu, _ = din("w_gu", [D, 2 * DFF])
    w_dn, _ = din("w_dn", [DFF, D])
    pfm_d, pfm_r = din("pfm", [128, NPFM])
    sinks_d, sinks_r = din("sinks", [1, 32])
    c128_d, c128_r = din("c128", [128, 644])
    maskn_d, maskn_r = din("maskn", [32, 32])
    tfm_d, tfm_r = din("tfm", [npass * 128, 2 * NM])
    ttok_d, ttok_r = din("ttok", [npass * 128, 7 * 32])
    kval_d, kval_r = din("kval", [npass * 128, 2])

    y_p, y_p_r = dout("y_p", [npass * 512, D])
    y_s, y_s_r = dout("y_s", [npass * 32, D])
    nk_p, nk_p_r = dout("nk_p", [npass * 128, 256])
    nv_p, nv_p_r = dout("nv_p", [npass * 128, 256])
    nc_p, nc_p_r = dout("nc_p", [npass * 2, 2048])
    nf_p, nf_p_r = dout("nf_p", [npass * 2, DFF])
    nk_s, nk_s_r = dout("nk_s", [npass * 8 * 128, 256])
    nv_s, nv_s_r = dout("nv_s", [npass * 8 * 128, 256])
    nc_s, nc_s_r = dout("nc_s", [npass * 16, 2048])
    nf_s, nf_s_r = dout("nf_s", [npass * 16, DFF])
    hmid, _ = dout("hmid", [npass * 32 * 128, NM], kind="Internal")
    hout, _ = dout("hout", [npass * 32 * 128, NO], kind="Internal")
    hmid_rs = [[B.dres("hmid%d_%d" % (p, j)) for j in range(32)] for p in range(npass)]
    hout_rs = [[B.dres("hout%d_%d" % (p, j)) for j in range(32)] for p in range(npass)]
    out_res = [y_p_r, y_s_r, nk_p_r, nv_p_r, nc_p_r, nf_p_r, nk_s_r, nv_s_r, nc_s_r, nf_s_r]

    PS = B.psum
    bank = B.bank

    o = G0
    identf, identf_r = B.sb("identf", o, [128, 128], F32); o += 512
    onesf, onesf_r = B.sb("onesf", o, [128, 128], F32); o += 512
    identb, identb_r = B.sb("identb", o, [128, 128], BF16); o += 256
    onesb, onesb_r = B.sb("onesb", o, [128, 128], BF16); o += 256
    permb, permb_r = B.sb("permb", o, [128, 128], BF16); o += 256
    maskO, maskO_r = B.sb("maskO", o, [128, 128], BF16); o += 256
    maskP, maskP_r = B.sb("maskP", o, [128, 128], BF16); o += 256
    maskc, maskc_r = B.sb("maskc", o, [128, 4], BF16); o += 64
    masknb, masknb_r = B.sb("masknb", o, [128, 32], BF16); o += 64
    pfm, pfm_sr = B.sb("pfm", o, [128, NPFM], F32); o += 1792
    sinkexp, sinkexp_r = B.sb("sinkexp", o, [128, 32], F32); o += 128
    epsc, epsc_r = B.sb("epsc", o, [128, 1], F32); o += 64
    kvalid, kvalid_r = B.sb("kvalid", o, [128, 2], F32); o += 64
    ssv, ssv_r = B.sb("ssv", o, [128, 8], F32); o += 64
    rsv, rsv_r = B.sb("rsv", o, [128, 8], F32); o += 64
    rstd_f, rstd_f_r = B.sb("rstd_f", o, [128, NM], F32); o += 2240
    assert o <= S0, o
    gmix = pfm[:, 0:32]
    gffn = pfm[:, 32:64]
    gattn = pfm[:, 64:80]
    gconv = pfm[:, 80:96]
    convw = pfm[:, 96:144].rearrange("p (j i) -> p j i", i=3)
    fconvw = pfm[:, 144:402].rearrange("p (j i) -> p j i", i=3)
    gfin = pfm[:, 402:434]

    WS = WStream(B)

    def add_pair(W, wname, nkt, c0):
        k0 = 0
        while k0 < nkt:
            nk = min(16, nkt - k0)
            WS.add(W, wname, k0, nk, c0)
            k0 += nk

    for p in range(npass):
        add_pair(w_in, "w_in", 32, 2048)
        add_pair(w_in, "w_in", 32, 2304)
        for s in range(8):
            add_pair(w_in, "w_in", 32, 256 * s)
        for sg in range(8):
            add_pair(w_in, "w_in", 32, 2560 + 256 * sg)
            add_pair(w_in, "w_in", 32, 4608 + 256 * sg)
            add_pair(w_in, "w_in", 32, 6656 + 256 * sg)
        for s in range(16):
            add_pair(w_out, "w_out", 32, 256 * s)
        for s in range(43):
            add_pair(w_gu, "w_gu", 32, 256 * s)
            add_pair(w_gu, "w_gu", 32, DFF + 256 * s)
        for s in range(16):
            add_pair(w_dn, "w_dn", NFT, 256 * s)

    def pair_view(pi):
        return PS[:, 2 * pi:2 * pi + 2, :].rearrange("p a b -> p (a b)")

    def pair_res(pi):
        return [bank[2 * pi], bank[2 * pi + 1]]

    cst, cst_r = B.sb("cst", ZX, [128, 644], F32)
    B.dma(B.qsp, cst, c128_d, [c128_r], [cst_r])
    B.copy(B.DVE, identf, cst[:, 0:128], [cst_r], [identf_r])
    B.copy(B.DVE, identb, cst[:, 0:128], [cst_r], [identb_r])
    B.copy(B.DVE, maskO, cst[:, 128:256], [cst_r], [maskO_r])
    B.copy(B.DVE, maskP, cst[:, 256:384], [cst_r], [maskP_r])
    B.copy(B.DVE, permb, cst[:, 384:512], [cst_r], [permb_r])
    B.copy(B.DVE, onesb, cst[:, 512:640], [cst_r], [onesb_r])
    B.copy(B.DVE, onesf, cst[:, 512:640], [cst_r], [onesf_r])
    B.copy(B.DVE, maskc, cst[:, 640:644], [cst_r], [maskc_r])
    cst2, cst2_r = B.sb("cst2", ZX + 4096, [128, 32], F32)
    B.dma(B.qsp, cst2[0:32, :], maskn_d, [maskn_r], [cst2_r])
    B.copy(B.DVE, masknb[0:32, :], cst2[0:32, :], [cst2_r], [masknb_r])
    B.dma(B.qsp, pfm, pfm_d, [pfm_r], [pfm_sr])
    B.dma(B.qsp, cst2, sinks_d[0, :].partition_broadcast(128), [sinks_r], [cst2_r])
    B.act(sinkexp, cst2, AF.Exp, [cst2_r], [sinkexp_r])
    B.memset(B.DVE, epsc, EPS, [epsc_r])

    WS.fill()

    for p in range(npass):
        E0 = ZX + 51200 + 17664
        o = E0
        CT, CT_r = B.sb("CT", o, [128, NM], F32); o += 2240
        ST, ST_r = B.sb("ST", o, [128, NM], F32); o += 2240
        ttok, ttok_r2 = B.sb("ttok", o, [128, 7, 32], F32); o += 896
        acc, acc_r = B.sb("acc", o, [128, NM], F32); o += 2240
        rstd_t, rstd_t_r = B.sb("rstd_t", o, [128, NM], F32); o += 2240
        cstT, cstT_r = B.sb("cstT", o, [128, 16, 16], F32); o += 1024
        sqt, sqt_r = B.sb("sqt", o, [128, NM], F32); o += 2240
        scf, scf_r = B.sb("scf", o, [16, 2048], F32); o += 8192
        assert o <= ZC, o
        B.dma(B.qsp, CT, tfm_d[p * 128:(p + 1) * 128, 0:NM], [tfm_r], [CT_r])
        B.dma(B.qsp, ST, tfm_d[p * 128:(p + 1) * 128, NM:2 * NM], [tfm_r], [ST_r])
        B.dma(B.qsp, ttok.rearrange("p a b -> p (a b)"), ttok_d[p * 128:(p + 1) * 128, :], [ttok_r], [ttok_r2])
        B.dma(B.qsp, kvalid, kval_d[p * 128:(p + 1) * 128, :], [kval_r], [kvalid_r])

        if stop_after <= 0:
            break
        aT, aT_r = B.sb("aT", ZX, [128, 32, TI], BF16, nres=32)
        convT, convT_r = B.sb("convT", ZX + 51200, [128, 16, NM], BF16, nres=16)
        qT, qT_r = B.sb("qT", ZC, [128, 16, NM], BF16, nres=16)
        KT, KT_r = B.sb("KT", ZC + 17536, [128, 4, TI], BF16)
        Vd, Vd_r = B.sb("Vd", ZC + 17536 + 6400, [128, 7, 4, 128], BF16, nres=7)

        xin = [B.sb("xin%d" % i, ZC + i * 16384, [128, D], F32) for i in range(2)]
        xb = [B.sb("xb%d" % i, ZX + 51200 + i * 8192, [128, D], BF16) for i in range(2)]
        for i, (r0, nr) in enumerate(TILES):
            xi, xi_r = xin[i % 2]
            xbb, xbb_r = xb[i % 2]
            B.dma(B.qsp, xi[:nr, :], xh[p * TI + r0:p * TI + r0 + nr, :], [xh_r], [xi_r])
            import os
            cut = int(os.environ.get("K1CUT", "9"))
            if cut <= 0:
                continue
            B.act(xbb[:nr, :], xi[:nr, :], AF.Square, [xi_r], [xbb_r, ssv_r], accum=ssv[:nr, i:i + 1])
            if cut <= 1:
                continue
            B.act(rsv[:nr, i:i + 1], ssv[:nr, i:i + 1], AF.Ln, [ssv_r, epsc_r], [rsv_r], scale=1.0 / D, bias=epsc[:nr, :])
            B.act(rsv[:nr, i:i + 1], rsv[:nr, i:i + 1], AF.Exp, [rsv_r], [rsv_r], scale=-0.5)
            B.act(xbb[:nr, :], xi[:nr, :], AF.Copy, [xi_r, rsv_r], [xbb_r], scale=rsv[:nr, i:i + 1])
            if cut <= 2:
                continue
            for kg in range(4):
                b = (i * 4 + kg) % 6
                pt = PS[:, b, :].bitcast(BF16).rearrange("p (a c) -> p a c", a=8)
                ops = []
                for kk in range(8):
                    k = kg * 8 + kk
                    ops.append((pt[:, kk, 0:nr], xbb[:nr, k * 128:(k + 1) * 128], identb[:nr, :nr]))
                B.transposes(ops, [xbb_r, identb_r], [bank[b]])
                if cut <= 3:
                    continue
                B.tt(aT[:, kg * 8:(kg + 1) * 8, r0:r0 + nr], pt[:, :, 0:nr],
                     gmix[:, kg * 8:(kg + 1) * 8].unsqueeze(2).to_broadcast([128, 8, nr]), ALU.mult,
                     [bank[b], pfm_sr], aT_r[kg * 8:(kg + 1) * 8])
        if p == 0 and cut >= 9:
            B.tap("aT", aT.rearrange("p a b -> p (a b)"), aT_r, [128, 32 * TI])
        if stop_after <= 1:
            break

        o = S0
        Kf, Kf_r = B.sb("Kf", o, [128, 256], F32); o += 1024
        Vf, Vf_r = B.sb("Vf", o, [128, 256], F32); o += 1024
        Kd, Kd_r = B.sb("Kd", o, [128, 4, 128], BF16); o += 1024
        rA, rA_r = B.sb("rA", o, [128, 4, 16], F32); o += 256
        rB, rB_r = B.sb("rB", o, [128, 4, 16], F32); o += 256
        S_after_kv = o
        wKs = [WS.take(), WS.take()]
        wVs = [WS.take(), WS.take()]
        wK_rs = [r for (_, r) in wKs]
        wV_rs = [r for (_, r) in wVs]
        for i, (r0, nr) in enumerate(TILES):
            pk = PS[:, 6, 0:256]
            pv = PS[:, 7, 0:256]
            ops = [(pk[:nr, :], aT[:, k, r0:r0 + nr], wKs[k // 16][0][:, k % 16, :], k == 0, k == 31) for k in range(32)]
            B.mms(ops, aT_r + wK_rs, [bank[6]])
            ops = [(pv[:nr, :], aT[:, k, r0:r0 + nr], wVs[k // 16][0][:, k % 16, :], k == 0, k == 31) for k in range(32)]
            B.mms(ops, aT_r + wV_rs, [bank[7]])
            pv3 = pv[:nr, :].rearrange("p (g d) -> p g d", g=4)
            B.act(Vd[:nr, i, :, 0:64], pv3, AF.Copy, [bank[7]], [Vd_r[i]])
            B.act(Vd[:nr, i, :, 64:128], pv3, AF.Copy, [bank[7]], [Vd_r[i]])
            if i >= 5:
                B.act(Vf[:nr, :], pv[:nr, :], AF.Copy, [bank[7]], [Vf_r])
            B.act(Kf[:nr, :], pk[:nr, :], AF.Copy, [bank[6]], [Kf_r])
            K3 = Kf[:nr, :].rearrange("p (g d) -> p g d", g=4)
            cc_b = ttok[:nr, i, 0:16].unsqueeze(1).to_broadcast([nr, 4, 16])
            ss_b = ttok[:nr, i, 16:32].unsqueeze(1).to_broadcast([nr, 4, 16])
            B.tt(rA[:nr], K3[:, :, 0:16], cc_b, ALU.mult, [Kf_r, ttok_r2], [rA_r])
            B.tt(rB[:nr, :, 0:8], K3[:, :, 8:16], ss_b[:, :, 0:8], ALU.mult, [Kf_r, ttok_r2], [rB_r])
            B.tt(rB[:nr, :, 8:16], K3[:, :, 0:8], ss_b[:, :, 8:16], ALU.mult, [Kf_r, ttok_r2], [rB_r])
            B.tt(K3[:, :, 0:16], rA[:nr], rB[:nr], ALU.add, [rA_r, rB_r], [Kf_r])
            B.copy(B.DVE, Kd[:nr, :, 0:64], K3, [Kf_r], [Kd_r])
            B.copy(B.DVE, Kd[:nr, :, 64:128], K3, [Kf_r], [Kd_r])
            pT = PS[:, 6, :].bitcast(BF16)[:, 0:512].rearrange("p (g c) -> p g c", g=4)
            ops = [(pT[:, g, 0:nr], Kd[:nr, g, :], identb[:nr, :nr]) for g in range(4)]
            B.transposes(ops, [Kd_r, identb_r], [bank[6]])
            B.act(KT[:, :, r0:r0 + nr], pT[:, :, 0:nr], AF.Copy, [bank[6]], [KT_r])
            if i == 5:
                B.dma(B.qsp, nk_p[p * 128:(p + 1) * 128, :], Kf, [Kf_r], [])
                B.dma(B.qsp, nv_p[p * 128:(p + 1) * 128, :], Vf, [Vf_r], [])
            if i == 6:
                nk3 = nk_s.rearrange("(s w) c -> s w c", w=128)
                nv3 = nv_s.rearrange("(s w) c -> s w c", w=128)
                ck3 = ck.rearrange("(s w) c -> s w c", w=128)
                cv3 = cv.rearrange("(s w) c -> s w c", w=128)
                B.dma(B.qsp, nk3[p * 8:(p + 1) * 8, 124:128, :], Kf[0:32, :], [Kf_r], [])
                B.dma(B.qsp, nv3[p * 8:(p + 1) * 8, 124:128, :], Vf[0:32, :], [Vf_r], [])
                B.dma(B.qsp, nk3[p * 8:(p + 1) * 8, 0:124, :], ck3[p * 8:(p + 1) * 8, 4:128, :], [ck_r], [])
                B.dma(B.qsp, nv3[p * 8:(p + 1) * 8, 0:124, :], cv3[p * 8:(p + 1) * 8, 4:128, :], [cv_r], [])
        for _ in range(4):
            WS.done()
        if p == 0:
            B.tap("KT", KT.rearrange("p a b -> p (a b)"), KT_r, [128, 4 * TI])
            B.tap("Vd", Vd.rearrange("p a b c -> p (a b c)"), Vd_r, [128, 7 * 4 * 128])

        o = S_after_kv
        qraw = [B.sb("qraw%d" % i, o + i * 1152, [128, NM], BF16) for i in range(2)]; o += 2304
        m1 = [B.sb("m1_%d" % i, o + i * 2240, [128, NM], F32) for i in range(2)]; o += 4480
        m2 = [B.sb("m2_%d" % i, o + i * 2240, [128, NM], F32) for i in range(2)]; o += 4480
        S_after_q = o

        def pair_mm(nkt, rhs_of_k, rhs_res, c0=M0):
            pis = [next_pair(), next_pair()]
            k0 = 0
            while k0 < nkt:
                nk = min(16, nkt - k0)
                w, w_r = WS.take()
                for jj in range(2):
                    pv_ = pair_view(pis[jj])
                    ops = []
                    for kk in range(nk):
                        k = k0 + kk
                        ops.append((pv_[:, c0:512], w[:, kk, jj * 128:(jj + 1) * 128], rhs_of_k(k, c0, 512), k == 0, k == nkt - 1))
                    for kk in range(nk):
                        k = k0 + kk
                        ops.append((pv_[:, 512:800], w[:, kk, jj * 128:(jj + 1) * 128], rhs_of_k(k, 512, 800), k == 0, k == nkt - 1))
                    B.mms(ops, rhs_res[k0:k0 + nk] + [w_r], pair_res(pis[jj]))
                WS.done()
                k0 += nk
            return pis

        rot = [0]

        def next_pair():
            pi = rot[0] % 3
            rot[0] += 1
            return pi

        for s in range(8):
            pis = pair_mm(32, lambda k, a, b: aT[:, k, a:b], aT_r)
            for jj in range(2):
                j = 2 * s + jj
                pi = pis[jj]
                pm = pair_view(pi)[:, M0:M1]
                qr, qr_r = qraw[j % 2]
                B.act(qr, pm, AF.Copy, pair_res(pi), [qr_r])
                p3 = pair_view(3)
                B.mms([(p3[:, 252:512], permb, qr[:, 0:260], True, True), (p3[:, 512:800], permb, qr[:, 260:NM], True, True)],
                      [permb_r, qr_r], pair_res(3))
                a1, a1_r = m1[j % 2]
                a2, a2_r = m2[j % 2]
                B.tt(a1, p3[:, M0:M1], ST, ALU.mult, pair_res(3) + [ST_r], [a1_r])
                B.tt(a2, qr, CT, ALU.mult, [qr_r, CT_r], [a2_r])
                B.tt(qT[:, j, :], a1, a2, ALU.add, [a1_r, a2_r], [qT_r[j]])
        if p == 0:
            B.tap("qT", qT.rearrange("p a b -> p (a b)"), qT_r, [128, 16 * NM])
        if stop_after <= 2:
            break

        B.dma(B.qsp, scf, sc[p * 16:(p + 1) * 16, :], [sc_r], [scf_r])
        for jg in range(4):
            pt = PS[:, 6, 0:64].rearrange("p (a c) -> p a c", a=4)
            ops = [(pt[:, jj, :], scf[0:16, (jg * 4 + jj) * 128:(jg * 4 + jj + 1) * 128], identf[0:16, 0:16]) for jj in range(4)]
            B.transposes(ops, [scf_r, identf_r], [bank[6]])
            B.act(cstT[:, jg * 4:(jg + 1) * 4, :], pt, AF.Copy, [bank[6]], [cstT_r])
        o = S_after_kv
        cbS = [B.sb("cbS%d" % i, o + i * 2240, [128, NM], F32) for i in range(2)]; o += 4480
        ccS = [B.sb("ccS%d" % i, o + i * 2240, [128, NM], F32) for i in range(2)]; o += 4480
        ub = [B.sb("ub%d" % i, o + i * 2240, [128, NM], F32) for i in range(2)]; o += 4480
        yb, yb_r = B.sb("yb", o, [128, NM], F32); o += 2240
        ob, ob_r = B.sb("ob", o, [128, NM], F32); o += 2240
        ext, ext_r = B.sb("ext", o, [128, 8, 6], F32); o += 192
        ysb, ysb_r = B.sb("ysb", o, [128, 8, 4], F32); o += 128
        stg, stg_r = B.sb("stg", o, [128, 2, 18], F32); o += 192
        stO = [B.sb("stO%d" % i, o + i * 1024, [18, 256], F32) for i in range(2)]; o += 2048
        assert o <= ZX, o
        B.memset(B.DVE, ob[:, 0:2], 0.0, [ob_r])
        B.memset(B.DVE, acc, 0.0, [acc_r])
        def flush_stg(sg_):
            pt = PS[:, 7, 0:256].rearrange("p (a c) -> p a c", a=2)
            B.transposes([(pt[0:18, jj, :], stg[:, jj, :], identf) for jj in range(2)], [stg_r, identf_r], [bank[7]])
            so, so_r = stO[sg_ % 2]
            B.act(so.rearrange("p (a c) -> p a c", a=2), pt[0:18], AF.Copy, [bank[7]], [so_r])
            B.dma(B.qsp, nc_p[p * 2:(p + 1) * 2, sg_ * 256:(sg_ + 1) * 256], so[0:2, :], [so_r], [])
            B.dma(B.qsp, nc_s[p * 16:(p + 1) * 16, sg_ * 256:(sg_ + 1) * 256], so[2:18, :], [so_r], [])

        for sg in range(8):
            pis = pair_mm(32, lambda k, a, b: aT[:, k, a:b], aT_r)
            if sg > 0:
                flush_stg(sg - 1)
            for jj in range(2):
                B.act(cbS[jj][0], pair_view(pis[jj])[:, M0:M1], AF.Copy, pair_res(pis[jj]), [cbS[jj][1]])
            pis = pair_mm(32, lambda k, a, b: aT[:, k, a:b], aT_r)
            for jj in range(2):
                B.act(ccS[jj][0], pair_view(pis[jj])[:, M0:M1], AF.Copy, pair_res(pis[jj]), [ccS[jj][1]])
            pis = pair_mm(32, lambda k, a, b: aT[:, k, a:b], aT_r)
            for jj in range(2):
                j = 2 * sg + jj
                pi = pis[jj]
                pm = pair_view(pi)[:, M0:M1]
                u, u_r = ub[jj]
                B.tt(u, pm, ccS[jj][0], ALU.mult, pair_res(pi) + [ccS[jj][1]], [u_r])
                B.act(yb[:, 2:516], u[:, 0:514], AF.Copy, [u_r, pfm_sr], [yb_r], scale=convw[:, j, 0:1])
                B.stt(yb[:, 2:516], u[:, 1:515], convw[:, j, 1:2], yb[:, 2:516], ALU.mult, ALU.add, [u_r, pfm_sr, yb_r], [yb_r])
                B.stt(yb[:, 2:516], u[:, 2:516], convw[:, j, 2:3], yb[:, 2:516], ALU.mult, ALU.add, [u_r, pfm_sr, yb_r], [yb_r])
                B.copy(B.DVE, ext[:, :, 0:2], cstT[:, j, :].rearrange("p (s r) -> p s r", r=2), [cstT_r], [ext_r])
                B.copy(B.DVE, ext[:, :, 2:6], u[:, 516:548].rearrange("p (s t) -> p s t", t=4), [u_r], [ext_r])
                B.ts(ysb, ext[:, :, 0:4], convw[:, j, 0:1], ALU.mult, [ext_r, pfm_sr], [ysb_r])
                B.stt(ysb, ext[:, :, 1:5], convw[:, j, 1:2], ysb, ALU.mult, ALU.add, [ext_r, pfm_sr, ysb_r], [ysb_r])
                B.stt(ysb, ext[:, :, 2:6], convw[:, j, 2:3], ysb, ALU.mult, ALU.add, [ext_r, pfm_sr, ysb_r], [ysb_r])
                B.copy(B.DVE, yb[:, 516:548].rearrange("p (s t) -> p s t", t=4), ysb, [ysb_r], [yb_r])
                B.tt(ob[:, 2:NM], yb[:, 2:NM], cbS[jj][0][:, 2:NM], ALU.mult, [yb_r, cbS[jj][1]], [ob_r])
                B.act(convT[:, j, :], ob, AF.Copy, [ob_r], [convT_r[j]])
                B.act(sqt, ob, AF.Square, [ob_r], [sqt_r])
                B.tt(acc, acc, sqt, ALU.add, [acc_r, sqt_r], [acc_r])
                B.copy(B.DVE, stg[:, jj, 0:2], u[:, 514:516], [u_r], [stg_r])
                B.copy(B.DVE, stg[:, jj, 2:18].rearrange("p (s r) -> p s r", r=2), ext[:, :, 4:6], [ext_r], [stg_r])
        flush_stg(7)

        def finish_norm(dst, dst_r, nfeat):
            p3 = pair_view(3)
            B.mms([(p3[:, 252:512], onesf, acc[:, 0:260], True, True), (p3[:, 512:800], onesf, acc[:, 260:NM], True, True)],
                  [onesf_r, acc_r], pair_res(3))
            B.act(dst, p3[:, M0:M1], AF.Ln, pair_res(3) + [epsc_r], [dst_r], scale=1.0 / nfeat, bias=epsc)
            B.act(dst, dst, AF.Exp, [dst_r], [dst_r], scale=-0.5)

        finish_norm(rstd_t, rstd_t_r, 2048)
        for j in range(16):
            B.stt(convT[:, j, :], convT[:, j, :], gconv[:, j:j + 1], rstd_t, ALU.mult, ALU.mult, [convT_r[j], pfm_sr, rstd_t_r], [convT_r[j]])
        if p == 0:
            B.tap("convT", convT.rearrange("p a b -> p (a b)"), convT_r, [128, 16 * NM])
        if stop_after <= 3:
            break

        AOT, AOT_r = B.sb("AOT", ZX, [128, 16, NM], BF16, nres=16)
        o = ZX + 17536
        cKT, cKT_r = B.sb("cKT", o, [128, 8, 4, 128], BF16); o += 8192
        cVd, cVd_r = B.sb("cVd", o, [128, 8, 4, 128], BF16); o += 8192
        Pp = [B.sb("Pp%d" % i, o + i * 1024, [128, 4, 128], BF16) for i in range(2)]; o += 2048
        Po = [B.sb("Po%d" % i, o + i * 1024, [128, 4, 128], BF16) for i in range(2)]; o += 2048
        dsm = [B.sb("dsm%d" % i, o + i * 2048, [128, 512], F32) for i in range(2)]; o += 4096
        ckf = [B.sb("ckf%d" % i, o + i * 1024, [128, 256], F32) for i in range(2)]; o += 2048
        cvf = [B.sb("cvf%d" % i, o + i * 1024, [128, 256], F32) for i in range(2)]; o += 2048
        cKd = [B.sb("cKd%d" % i, o + i * 1024, [128, 4, 128], BF16) for i in range(2)]; o += 2048
        Pc, Pc_r = B.sb("Pc", o, [128, 256], BF16); o += 512
        Pn, Pn_r = B.sb("Pn", o, [128, 256], BF16); o += 512
        assert o <= ZX + 51200

        stT, stT_r = B.sb("stT", S0 + 4480, [128, NFT, 16], F32)
        sff = [B.sb("sff%d" % i, S0 + 4480 + 5504 + i * 4096, [16, 1024], F32) for i in range(3)]
        for step in range(13):
            if step < 11:
                bt = step
                nsub = 8 if bt < 10 else 6
                sfb, sfb_r = sff[bt % 3]
                B.dma(B.qsp, sfb[:, 0:nsub * 128], sf[p * 16:(p + 1) * 16, bt * 1024:bt * 1024 + nsub * 128], [sf_r], [sfb_r])
            if step >= 2:
                bt = step - 2
                nsub = 8 if bt < 10 else 6
                sfb, sfb_r = sff[bt % 3]
                pt = PS[:, bt % 2, 0:128].rearrange("p (a c) -> p a c", a=8)
                B.transposes([(pt[:, jj, :], sfb[0:16, jj * 128:(jj + 1) * 128], identf[0:16, 0:16]) for jj in range(nsub)],
                             [sfb_r, identf_r], [bank[bt % 2]])
                B.act(stT[:, 8 * bt:8 * bt + nsub, :], pt[:, 0:nsub, :], AF.Copy, [bank[bt % 2]], [stT_r])

        QB = [(252, 4, 0, 1, 124)] + [(256 + 128 * i, 128, 1 + i, 2 + i, 0) for i in range(4)]
        import os
        c3 = int(os.environ.get("A3CUT", "99"))
        units = [(qc0, n, tp, to, moff, g, hg) for (qc0, n, tp, to, moff) in QB for g in range(4) for hg in range(2)]

        def stage_a(ui, part):
            qc0, n, tp, to, moff, g, hg = units[ui]
            u2 = ui % 2
            bsS = u2 * 2
            SX = PS[:, bsS + 0, :].rearrange("p (t a q) -> p t a q", t=2, a=2)
            SY = PS[:, bsS + 1, :].rearrange("p (t a q) -> p t a q", t=2, a=2)
            jt0 = 4 * g + 2 * hg
            pp, pp_r = Pp[u2]
            po, po_r = Po[u2]
            if part == 1:
                for hf, SB_, bk in ((0, SX, bsS + 0), (1, SY, bsS + 1)):
                    ops = []
                    for a_ in range(2):
                        jt = jt0 + a_
                        rhs = qT[64 * hf:64 * hf + 64, jt, qc0 - M0:qc0 - M0 + n]
                        ops.append((SB_[:, 0, a_, 0:n], KT[64 * hf:64 * hf + 64, g, tp * 128:(tp + 1) * 128], rhs, True, True))
                        ops.append((SB_[:, 1, a_, 0:n], KT[64 * hf:64 * hf + 64, g, to * 128:(to + 1) * 128], rhs, True, True))
                    B.mms(ops, [KT_r, qT_r[jt0], qT_r[jt0 + 1]], [bank[bk]])
                pp4 = pp.rearrange("p (a b) q -> p a b q", b=2)
                po4 = po.rearrange("p (a b) q -> p a b q", b=2)
                for hf, SB_, bk in ((0, SX, bsS + 0), (1, SY, bsS + 1)):
                    B.act(pp4[:, :, hf, 0:n], SB_[:, 0, :, 0:n], AF.Exp, [bank[bk]], [pp_r], scale=0.125)
                    B.act(po4[:, :, hf, 0:n], SB_[:, 1, :, 0:n], AF.Exp, [bank[bk]], [po_r], scale=0.125)
                return
            mP = maskP[:, moff:moff + n].unsqueeze(1).to_broadcast([128, 4, n])
            mO = maskO[:, moff:moff + n].unsqueeze(1).to_broadcast([128, 4, n])
            if tp <= 1:
                B.stt(pp[:, :, 0:n], pp[:, :, 0:n], kvalid[:, tp:tp + 1], mP, ALU.mult, ALU.mult, [pp_r, kvalid_r, maskP_r], [pp_r])
            else:
                B.tt(pp[:, :, 0:n], pp[:, :, 0:n], mP, ALU.mult, [pp_r, maskP_r], [pp_r])
            if to <= 1:
                B.stt(po[:, :, 0:n], po[:, :, 0:n], kvalid[:, to:to + 1], mO, ALU.mult, ALU.mult, [po_r, kvalid_r, maskO_r], [po_r])
            else:
                B.tt(po[:, :, 0:n], po[:, :, 0:n], mO, ALU.mult, [po_r, maskO_r], [po_r])

        def stage_b(ui, part):
            qc0, n, tp, to, moff, g, hg = units[ui]
            u2 = ui % 2
            bsO = 4 + u2 * 2
            Ob = PS[:, bsO + 0, :].rearrange("p (e q) -> p e q", e=4)
            Db = PS[:, bsO + 1, :].rearrange("p (e q) -> p e q", e=4)
            jt0 = 4 * g + 2 * hg
            pp, pp_r = Pp[u2]
            po, po_r = Po[u2]
            ds, ds_r = dsm[u2]
            ds3 = ds.rearrange("p (e q) -> p e q", e=4)
            if part == 1:
                B.mms([(Ob[:, :, 0:n], Vd[:, tp, g, :], pp[:, :, 0:n], True, False),
                       (Ob[:, :, 0:n], Vd[:, to, g, :], po[:, :, 0:n], False, True)],
                      [Vd_r[tp], Vd_r[to], pp_r, po_r], [bank[bsO + 0]])
                B.mms([(Db[:, :, 0:n], onesb, pp[:, :, 0:n], True, False),
                       (Db[:, :, 0:n], onesb, po[:, :, 0:n], False, True)],
                      [onesb_r, pp_r, po_r], [bank[bsO + 1]])
                h0 = 8 * g + 4 * hg
                B.tt(ds3[:, :, 0:n], Db[:, :, 0:n], sinkexp[:, h0:h0 + 4].unsqueeze(2).to_broadcast([128, 4, n]), ALU.add,
                     [bank[bsO + 1], sinkexp_r], [ds_r])
                return
            B.recip(ds3[:, :, 0:n], ds3[:, :, 0:n], [ds_r], [ds_r], fast=True)
            for hf in range(2):
                O4 = Ob.rearrange("p (a b) q -> p a b q", b=2)[64 * hf:64 * hf + 64, :, hf, 0:n]
                R4 = ds3.rearrange("p (a b) q -> p a b q", b=2)[64 * hf:64 * hf + 64, :, hf, 0:n]
                B.tt(AOT[64 * hf:64 * hf + 64, jt0:jt0 + 2, qc0 - M0:qc0 - M0 + n], O4, R4, ALU.mult,
                     [bank[bsO + 0], ds_r], [AOT_r[jt0], AOT_r[jt0 + 1]])

        nun = len(units)
        stage_a(0, 1)
        stage_a(0, 2)
        for ui in range(1, nun):
            stage_a(ui, 1)
            stage_b(ui - 1, 1)
            stage_a(ui, 2)
            stage_b(ui - 1, 2)
        stage_b(nun - 1, 1)
        stage_b(nun - 1, 2)

        for s in range(8 if c3 > 4 else 0):
            kf, kf_r = ckf[s % 2]
            vf, vf_r = cvf[s % 2]
            kd, kd_r = cKd[s % 2]
            row0 = (p * 8 + s) * 128
            B.dma(B.qsp, kf, ck[row0:row0 + 128, :], [ck_r], [kf_r])
            B.dma(B.qsp, vf, cv[row0:row0 + 128, :], [cv_r], [vf_r])
            kf3 = kf.rearrange("p (g d) -> p g d", g=4)
            vf3 = vf.rearrange("p (g d) -> p g d", g=4)
            B.copy(B.DVE, kd[:, :, 0:64], kf3, [kf_r], [kd_r])
            B.copy(B.DVE, kd[:, :, 64:128], kf3, [kf_r], [kd_r])
            B.act(cVd[:, s, :, 0:64], vf3, AF.Copy, [vf_r], [cVd_r])
            B.act(cVd[:, s, :, 64:128], vf3, AF.Copy, [vf_r], [cVd_r])
            b = s % 2
            pT = PS[:, b, :].bitcast(BF16)[:, 0:512].rearrange("p (g c) -> p g c", g=4)
            B.transposes([(pT[:, g, :], kd[:, g, :], identb) for g in range(4)], [kd_r, identb_r], [bank[b]])
            B.act(cKT[:, s, :, :], pT, AF.Copy, [bank[b]], [cKT_r])
        SC0 = 516
        for g in range(4 if c3 > 5 else 0):
            bs = (g % 2) * 4
            ScX = PS[:, 0, 0:128]
            ScY = PS[:, 1, 0:128]
            SnX = PS[:, 2, 0:128]
            SnY = PS[:, 3, 0:128]
            Ob = PS[:, 4, 0:256]
            Db = PS[:, 5, 0:256]
            bs = 2
            for hf, Sc_, Sn_, bc, bn in ((0, ScX, SnX, 0, 2), (1, ScY, SnY, 1, 3)):
                ops = []
                for s in range(8):
                    for a in range(4):
                        jt = 4 * g + a
                        ops.append((Sc_[:, a * 32 + s * 4:a * 32 + s * 4 + 4], cKT[64 * hf:64 * hf + 64, s, g, :],
                                    qT[64 * hf:64 * hf + 64, jt, SC0 + 4 * s:SC0 + 4 * s + 4], True, True))
                B.mms(ops, [cKT_r] + qT_r[4 * g:4 * g + 4], [bank[bc]])
                ops = []
                for a in range(4):
                    jt = 4 * g + a
                    ops.append((Sn_[0:32, a * 32:(a + 1) * 32], KT[64 * hf:64 * hf + 64, g, 768:800],
                                qT[64 * hf:64 * hf + 64, jt, SC0:SC0 + 32], True, True))
                B.mms(ops, [KT_r] + qT_r[4 * g:4 * g + 4], [bank[bn]])
            if c3 <= 6:
                continue
            Pc5 = Pc.rearrange("p (a b c) -> p a b c", a=4, b=2)
            Pn5 = Pn.rearrange("p (a b c) -> p a b c", a=4, b=2)
            for hf, Sc_, Sn_, bc, bn in ((0, ScX, SnX, 0, 2), (1, ScY, SnY, 1, 3)):
                B.act(Pc5[:, :, hf, :], Sc_.rearrange("p (a c) -> p a c", a=4), AF.Exp, [bank[bc]], [Pc_r], scale=0.125)
                B.act(Pn5[0:32, :, hf, :], Sn_[0:32, :].rearrange("p (a c) -> p a c", a=4), AF.Exp, [bank[bn]], [Pn_r], scale=0.125)
            Pc3 = Pc.rearrange("p (a t) -> p a t", t=4)
            B.tt(Pc3, Pc3, maskc.unsqueeze(1).to_broadcast([128, 64, 4]), ALU.mult, [Pc_r, maskc_r], [Pc_r])
            Pn3 = Pn[0:32, :].rearrange("p (e c) -> p e c", e=8)
            B.tt(Pn3, Pn3, masknb[0:32, :].unsqueeze(1).to_broadcast([32, 8, 32]), ALU.mult, [Pn_r, masknb_r], [Pn_r])
            if c3 <= 7:
                continue
            Pc4 = Pc.rearrange("p (e s t) -> p e s t", e=8, s=8)
            Ob4 = Ob.rearrange("p (e s t) -> p e s t", e=8, s=8)
            ops = [(Ob, Vd[0:32, 6, g, :], Pn[0:32, :], True, False)]
            for s in range(8):
                ops.append((Ob4[:, :, s, :], cVd[:, s, g, :], Pc4[:, :, s, :], False, s == 7))
            B.mms(ops, [Vd_r[6], cVd_r, Pn_r, Pc_r], [bank[4]])
            B.mms([(Db, onesb[0:32, :], Pn[0:32, :], True, False), (Db, onesb, Pc, False, True)],
                  [onesb_r, Pn_r, Pc_r], [bank[5]])
            ds, ds_r = dsm[g % 2]
            ds3 = ds[:, 0:256].rearrange("p (e c) -> p e c", e=8)
            Db3 = Db.rearrange("p (e c) -> p e c", e=8)
            Ob3 = Ob.rearrange("p (e c) -> p e c", e=8)
            B.tt(ds3, Db3, sinkexp[:, 8 * g:8 * g + 8].unsqueeze(2).to_broadcast([128, 8, 32]), ALU.add, [bank[5], sinkexp_r], [ds_r])
            B.recip(ds3, ds3, [ds_r], [ds_r], fast=True)
            for hf in range(2):
                O4 = Ob3.rearrange("p (a b) c -> p a b c", b=2)[64 * hf:64 * hf + 64, :, hf, :]
                R4 = ds3.rearrange("p (a b) c -> p a b c", b=2)[64 * hf:64 * hf + 64, :, hf, :]
                B.tt(AOT[64 * hf:64 * hf + 64, 4 * g:4 * g + 4, SC0:SC0 + 32], O4, R4, ALU.mult,
                     [bank[4], ds_r], AOT_r[4 * g:4 * g + 4])

        B.memset(B.DVE, acc, 0.0, [acc_r])
        for j in range(16):
            B.act(sqt, AOT[:, j, :], AF.Square, [AOT_r[j]], [sqt_r])
            B.tt(acc, acc, sqt, ALU.add, [acc_r, sqt_r], [acc_r])
        finish_norm(rstd_t, rstd_t_r, 2048)
        for j in range(16):
            B.stt(AOT[:, j, :], AOT[:, j, :], gattn[:, j:j + 1], rstd_t, ALU.mult, ALU.mult, [AOT_r[j], pfm_sr, rstd_t_r], [AOT_r[j]])
        if p == 0:
            B.tap("AOT", AOT.rearrange("p a b -> p (a b)"), AOT_r, [128, 16 * NM])
        if stop_after <= 4:
            break

        fT, fT_r = B.sb("fT", ZC, [128, 32, NM], BF16, nres=32)
        xres = [B.sb("xres%d" % i, ZX + 17536 + i * 12288, [128, 6, 512], F32) for i in range(2)]
        o = S0
        hm = [B.sb("hm%d" % i, o + i * 2240, [128, NM], F32) for i in range(2)]; o += 4480
        B.memset(B.DVE, acc, 0.0, [acc_r])

        def load_xres(gi):
            xr, xr_r = xres[gi % 2]
            for pc, (r0, nr) in enumerate(MP):
                B.dma(B.qsp, xr[:nr, pc, :], xh[p * TI + r0:p * TI + r0 + nr, gi * 512:(gi + 1) * 512], [xh_r], [xr_r])

        load_xres(0)

        def rhs4(k, a, b_):
            return (AOT[:, k, a - M0:b_ - M0] if k < 16 else convT[:, k - 16, a - M0:b_ - M0])

        for s in range(16):
            if s % 2 == 0 and s // 2 + 1 < 8:
                load_xres(s // 2 + 1)
            xr, xr_r = xres[(s // 2) % 2]
            pis = pair_mm(32, rhs4, AOT_r + convT_r)
            for jj in range(2):
                j = 2 * s + jj
                pi = pis[jj]
                pm = pair_view(pi)[:, M0:M1]
                xc0 = (s % 2) * 256 + jj * 128
                p3 = pair_view(3)
                B.transposes([(p3[:, r0:r0 + nr], xr[:nr, pc, xc0:xc0 + 128], identf[:nr, :nr]) for pc, (r0, nr) in enumerate(MP)],
                             [xr_r, identf_r], pair_res(3))
                h, h_r = hm[j % 2]
                B.act(h, pm, AF.Copy, pair_res(pi), [h_r])
                B.tt(h, h, p3[:, M0:M1], ALU.add, [h_r] + pair_res(3), [h_r])
                B.dma(B.qsp, hmid[(p * 32 + j) * 128:(p * 32 + j + 1) * 128, :], h, [h_r], [hmid_rs[p][j]])
                B.act(sqt, h, AF.Square, [h_r], [sqt_r])
                B.tt(acc, acc, sqt, ALU.add, [acc_r, sqt_r], [acc_r])
                B.ts(fT[:, j, :], h, gffn[:, j:j + 1], ALU.mult, [h_r, pfm_sr], [fT_r[j]])
        finish_norm(rstd_f, rstd_f_r, D)
        if p == 0:
            B.tap("fT", fT.rearrange("p a b -> p (a b)"), fT_r, [128, 32 * NM])
            B.tap("rstd_f", rstd_f, rstd_f_r, [128, NM])
        if stop_after <= 5:
            break

        actT, actT_r = B.sb("actT", ZX, [128, NFT, NO], BF16, nres=NFT)
        o = S0 + 4480 + 5504
        gs = [B.sb("gs%d" % i, S0 + i * 2240, [128, NM], F32) for i in range(2)]
        yg, yg_r = B.sb("yg", o, [128, NM], F32); o += 2240
        sg_ = [B.sb("sg%d" % i, o + i * 2240, [128, NM], F32) for i in range(4)]; o += 8960
        extf, extf_r = B.sb("extf", o, [128, 8, 6], F32); o += 192
        ysf, ysf_r = B.sb("ysf", o, [128, 8, 4], F32); o += 128
        stgf, stgf_r = B.sb("stgf", o, [128, 2, 18], F32); o += 192
        stOf = [B.sb("stOf%d" % i, o + i * 1024, [18, 256], F32) for i in range(2)]; o += 2048
        assert o <= ZX, o
        def rhs5(k, a, b_):
            return fT[:, k, a - M0:b_ - M0]

        for s in range(43):
            pis = pair_mm(32, rhs5, fT_r)
            for jj in range(2):
                j = 2 * s + jj
                pi = pis[jj]
                pm = pair_view(pi)[:, M0:M1]
                g_, g_r = gs[jj]
                B.tt(g_, pm, rstd_f, ALU.mult, pair_res(pi) + [rstd_f_r], [g_r])
                B.act(yg[:, 2:516], g_[:, 0:514], AF.Copy, [g_r, pfm_sr], [yg_r], scale=fconvw[:, j, 0:1])
                B.stt(yg[:, 2:516], g_[:, 1:515], fconvw[:, j, 1:2], yg[:, 2:516], ALU.mult, ALU.add, [g_r, pfm_sr, yg_r], [yg_r])
                B.stt(yg[:, 2:516], g_[:, 2:516], fconvw[:, j, 2:3], yg[:, 2:516], ALU.mult, ALU.add, [g_r, pfm_sr, yg_r], [yg_r])
                B.copy(B.DVE, extf[:, :, 0:2], stT[:, j, :].rearrange("p (s r) -> p s r", r=2), [stT_r], [extf_r])
                B.copy(B.DVE, extf[:, :, 2:6], g_[:, 516:548].rearrange("p (s t) -> p s t", t=4), [g_r], [extf_r])
                B.ts(ysf, extf[:, :, 0:4], fconvw[:, j, 0:1], ALU.mult, [extf_r, pfm_sr], [ysf_r])
                B.stt(ysf, extf[:, :, 1:5], fconvw[:, j, 1:2], ysf, ALU.mult, ALU.add, [extf_r, pfm_sr, ysf_r], [ysf_r])
                B.stt(ysf, extf[:, :, 2:6], fconvw[:, j, 2:3], ysf, ALU.mult, ALU.add, [extf_r, pfm_sr, ysf_r], [ysf_r])
                B.copy(B.DVE, yg[:, 516:548].rearrange("p (s t) -> p s t", t=4), ysf, [ysf_r], [yg_r])
                sgb, sgb_r = sg_[(s % 2) * 2 + jj]
                B.act(sgb[:, 4:NM], yg[:, 4:NM], AF.Silu, [yg_r], [sgb_r])
                B.tt(sgb[:, 4:NM], sgb[:, 4:NM], rstd_f[:, 4:NM], ALU.mult, [sgb_r, rstd_f_r], [sgb_r])
                B.copy(B.DVE, stgf[:, jj, 0:2], g_[:, 514:516], [g_r], [stgf_r])
                B.copy(B.DVE, stgf[:, jj, 2:18].rearrange("p (s r) -> p s r", r=2), extf[:, :, 4:6], [extf_r], [stgf_r])
            pis = pair_mm(32, rhs5, fT_r)
            pt = PS[:, 6 + s % 2, 0:256].rearrange("p (a c) -> p a c", a=2)
            B.transposes([(pt[0:18, jj, :], stgf[:, jj, :], identf) for jj in range(2)], [stgf_r, identf_r], [bank[6 + s % 2]])
            so, so_r = stOf[s % 2]
            B.act(so.rearrange("p (a c) -> p a c", a=2), pt[0:18], AF.Copy, [bank[6 + s % 2]], [so_r])
            B.dma(B.qsp, nf_p[p * 2:(p + 1) * 2, s * 256:(s + 1) * 256], so[0:2, :], [so_r], [])
            B.dma(B.qsp, nf_s[p * 16:(p + 1) * 16, s * 256:(s + 1) * 256], so[2:18, :], [so_r], [])
            for jj in range(2):
                j = 2 * s + jj
                pi = pis[jj]
                pm = pair_view(pi)[:, M0:M1]
                sgb, sgb_r = sg_[(s % 2) * 2 + jj]
                B.tt(actT[:, j, :], pm[:, 4:NM], sgb[:, 4:NM], ALU.mult, pair_res(pi) + [sgb_r], [actT_r[j]])
        if p == 0:
            B.tap("actT", actT[:, 0:4, :].rearrange("p a b -> p (a b)"), actT_r[0:4], [128, 4 * NO])
        if stop_after <= 6:
            break

        o = S0
        hmb = [B.sb("hmb%d" % i, o + i * 2240, [128, NM], F32) for i in range(4)]; o += 8960
        hob = [B.sb("hob%d" % i, o + i * 2240, [128, NO], F32) for i in range(2)]; o += 4480
        sq6, sq6_r = B.sb("sq6", o, [128, NO], F32); o += 2240
        acc6, acc6_r = B.sb("acc6", o, [128, NM], F32); o += 2240
        rstd_y, rstd_y_r = B.sb("rstd_y", o, [128, NM], F32); o += 2240
        B.memset(B.DVE, acc6, 0.0, [acc6_r])
        def rhs6(k, a, b_):
            return actT[:, k, a - O0:b_ - O0]

        for sp in range(16):
            for jj in range(2):
                j = 2 * sp + jj
                hb, hb_r = hmb[(sp % 2) * 2 + jj]
                B.dma(B.qsp, hb, hmid[(p * 32 + j) * 128:(p * 32 + j + 1) * 128, :], [hmid_rs[p][j]], [hb_r])
            pis = pair_mm(NFT, rhs6, actT_r, c0=O0)
            for jj in range(2):
                j = 2 * sp + jj
                pv_ = pair_view(pis[jj])
                hb, hb_r = hmb[(sp % 2) * 2 + jj]
                ho, ho_r = hob[jj]
                B.tt(ho, pv_[:, O0:M1], hb[:, 4:NM], ALU.add, pair_res(pis[jj]) + [hb_r], [ho_r])
                B.dma(B.qsp, hout[(p * 32 + j) * 128:(p * 32 + j + 1) * 128, :], ho, [ho_r], [hout_rs[p][j]])
                B.act(sq6, ho, AF.Square, [ho_r], [sq6_r])
                B.tt(acc6[:, 4:NM], acc6[:, 4:NM], sq6, ALU.add, [acc6_r, sq6_r], [acc6_r])
        p3 = pair_view(3)
        B.mms([(p3[:, 252:512], onesf, acc6[:, 0:260], True, True), (p3[:, 512:800], onesf, acc6[:, 260:NM], True, True)],
              [onesf_r, acc6_r], pair_res(3))
        B.act(rstd_y, p3[:, M0:M1], AF.Ln, pair_res(3) + [epsc_r], [rstd_y_r], scale=1.0 / D, bias=epsc)
        B.act(rstd_y, rstd_y, AF.Exp, [rstd_y_r], [rstd_y_r], scale=-0.5)
        if stop_after <= 7:
            break

        hin = [B.sb("hin%d" % i, ZX + i * 2240, [128, NO], F32) for i in range(3)]
        yT = [B.sb("yT%d" % i, ZX + 6720 + i * 2240, [128, NO], F32) for i in range(2)]
        ytok = [B.sb("ytok%d" % i, ZX + 11264 + i * 10240, [128, 5, 512], F32) for i in range(2)]
        def load_hin(jx):
            B.dma(B.qsp, hin[jx % 3][0], hout[(p * 32 + jx) * 128:(p * 32 + jx + 1) * 128, :], [hout_rs[p][jx]], [hin[jx % 3][1]])

        load_hin(0)
        load_hin(1)
        for j in range(32):
            if j + 2 < 32:
                load_hin(j + 2)
            hi, hi_r = hin[j % 3]
            yt, yt_r = yT[j % 2]
            B.stt(yt, hi, gfin[:, j:j + 1], rstd_y[:, 4:NM], ALU.mult, ALU.mult, [hi_r, pfm_sr, rstd_y_r], [yt_r])
            b0 = (j % 2) * 2
            pa = PS[:, b0, :].rearrange("p (a c) -> p a c", a=4)
            pb = PS[:, b0 + 1, 0:128]
            ops = [(pa[:, i, :], yt[:, i * 128:(i + 1) * 128], identf) for i in range(4)]
            ops.append((pb[0:32, :], yt[:, 512:544], identf))
            B.transposes(ops, [yt_r, identf_r], [bank[b0], bank[b0 + 1]])
            yk, yk_r = ytok[(j // 4) % 2]
            jc = (j % 4) * 128
            B.act(yk[:, 0:4, jc:jc + 128], pa, AF.Copy, [bank[b0]], [yk_r])
            B.copy(B.DVE, yk[0:32, 4, jc:jc + 128], pb[0:32, :], [bank[b0 + 1]], [yk_r])
            if j % 4 == 3:
                c0 = (j // 4) * 512
                for i in range(4):
                    B.dma(B.qsp, y_p[p * 512 + i * 128:p * 512 + (i + 1) * 128, c0:c0 + 512], yk[:, i, :], [yk_r], [])
                B.dma(B.qsp, y_s[p * 32:(p + 1) * 32, c0:c0 + 512], yk[0:32, 4, :], [yk_r], [])

    fin = B.ACT
    for q in (B.qsp, B.qw):
        for i, (sem, key) in enumerate(q.sems):
            if q.tot[i] > 0 and fin.seen.get(key, 0) < q.tot[i]:
                fin.h.wait_ge(sem, q.tot[i])
                fin.seen[key] = q.tot[i]
    return B


_CACHE = {}


def _rope_tables(pos):
    half = 8
    inv = (500000.0 ** (-np.arange(half, dtype=np.float32) * 2.0 / 16)).astype(np.float32)
    ang = pos.astype(np.float32)[:, None] * inv[None, :]
    return np.cos(ang).astype(np.float32), np.sin(ang).astype(np.float32)


def _const_tables(v):
    pos = np.zeros(TI, np.int64)
    pos[:768] = 512 * v - 240 + np.arange(768)
    for s in range(8):
        for t in range(4):
            pos[768 + 4 * s + t] = 8192 + t
    posc = np.maximum(pos, 0)
    cos, sin = _rope_tables(posc)
    CT = np.ones((128, NM), np.float32)
    ST = np.zeros((128, NM), np.float32)
    for hb in (0, 64):
        CT[hb:hb + 8, :] = cos[M0:M1].T
        CT[hb + 8:hb + 16, :] = cos[M0:M1].T
        ST[hb:hb + 8, :] = -sin[M0:M1].T
        ST[hb + 8:hb + 16, :] = sin[M0:M1].T
    tfm = np.concatenate([CT, ST], axis=1)
    ttok = np.zeros((128, 7, 32), np.float32)
    for i, (r0, nr) in enumerate(TILES):
        ttok[:nr, i, 0:8] = cos[r0:r0 + nr]
        ttok[:nr, i, 8:16] = cos[r0:r0 + nr]
        ttok[:nr, i, 16:24] = -sin[r0:r0 + nr]
        ttok[:nr, i, 24:32] = sin[r0:r0 + nr]
    kval = np.ones((128, 2), np.float32)
    if v == 0:
        kval[:, 0] = 0.0
        kval[:112, 1] = 0.0
    return tfm, ttok.reshape(128, 224), kval


def _c128():
    c = np.zeros((128, 644), np.float32)
    i = np.arange(128)
    c[:, 0:128] = np.eye(128, dtype=np.float32)
    c[:, 128:256] = (i[:, None] <= i[None, :])
    c[:, 256:384] = (i[:, None] > i[None, :])
    perm = np.zeros((128, 128), np.float32)
    for hb in (0, 64):
        for d in range(8):
            perm[hb + d + 8, hb + d] = 1.0
            perm[hb + d, hb + d + 8] = 1.0
    c[:, 384:512] = perm
    c[:, 512:640] = 1.0
    c[:, 640:644] = (i[:, None] > np.arange(4)[None, :])
    maskn = np.zeros((32, 32), np.float32)
    for s2 in range(8):
        for t2 in range(4):
            for t in range(4):
                if t2 <= t:
                    maskn[4 * s2 + t2, 4 * s2 + t] = 1.0
    return c, maskn


def kernel(x_prompt, x_sample, cache_k, cache_v, state_conv, state_ffn_conv, meta_tokens,
           g_mix, w_in, attn_sinks, conv_w, g_attn_out, g_conv_out, w_out, g_ffn,
           w_gate_up, ffn_conv_w, w_down, g_final, _debug=False, _ncores=8, _stop_after=99):
    f32 = np.float32
    x_prompt = np.asarray(x_prompt, f32); x_sample = np.asarray(x_sample, f32)
    ncores = _ncores
    npass = 2
    key = (npass, _debug, _stop_after)
    if key not in _CACHE:
        _CACHE[key] = build_program(npass=npass, debug=_debug, stop_after=_stop_after)
    B = _CACHE[key]

    xp = x_prompt[0]
    xs = x_sample
    fm = lambda a, n: np.ascontiguousarray(np.asarray(a, f32).reshape(n, 128).T)
    pfm = np.zeros((128, NPFM), f32)
    pfm[:, 0:32] = fm(g_mix[0], 32)
    pfm[:, 32:64] = fm(g_ffn[0], 32)
    pfm[:, 64:80] = fm(g_attn_out[0], 16)
    pfm[:, 80:96] = fm(g_conv_out[0], 16)
    cw = np.asarray(conv_w[0], f32)
    pfm[:, 96:144] = np.transpose(cw.reshape(3, 16, 128), (2, 1, 0)).reshape(128, 48)
    fw = np.asarray(ffn_conv_w[0], f32)
    pfm[:, 144:402] = np.transpose(fw.reshape(3, NFT, 128), (2, 1, 0)).reshape(128, 258)
    pfm[:, 402:434] = fm(g_final, 32)
    c128, maskn = _c128()
    w_in2 = np.asarray(w_in[0], f32); w_out2 = np.asarray(w_out[0], f32)
    w_gu2 = np.asarray(w_gate_up[0], f32); w_dn2 = np.asarray(w_down[0], f32)
    ck_all = np.asarray(cache_k[0], f32).reshape(128, 128, 256)
    cv_all = np.asarray(cache_v[0], f32).reshape(128, 128, 256)
    sc_all = np.asarray(state_conv[0], f32)
    sf_all = np.asarray(state_ffn_conv[0], f32)
    meta = np.asarray(meta_tokens, f32)

    in_maps = []
    for c in range(ncores):
        xh = np.zeros((npass, TI, D), f32)
        tfm_l, ttok_l, kval_l = [], [], []
        for p in range(npass):
            v = 2 * c + p
            if v == 0:
                xh[p, 240:256] = meta
            else:
                xh[p, 0:256] = xp[512 * v - 256:512 * v]
            xh[p, 256:768] = xp[512 * v:512 * v + 512]
            xh[p, 768:800] = xs[8 * v:8 * v + 8].reshape(32, D)
            a, b, k = _const_tables(v)
            tfm_l.append(a); ttok_l.append(b); kval_l.append(k)
        m = {
            "xh": xh.reshape(npass * TI, D),
            "ck": ck_all[16 * c:16 * c + 16].reshape(16 * 128, 256),
            "cv": cv_all[16 * c:16 * c + 16].reshape(16 * 128, 256),
            "sc": sc_all[16 * c:16 * c + 16].reshape(32, 2048),
            "sf": sf_all[16 * c:16 * c + 16].reshape(32, DFF),
            "w_in": w_in2, "w_out": w_out2, "w_gu": w_gu2, "w_dn": w_dn2,
            "pfm": pfm, "sinks": np.asarray(attn_sinks, f32).reshape(1, 32),
            "c128": c128, "maskn": maskn,
            "tfm": np.concatenate(tfm_l, 0), "ttok": np.concatenate(ttok_l, 0), "kval": np.concatenate(kval_l, 0),
        }
        in_maps.append(m)
    res = run_bass_kernel_spmd(B.nc, in_maps, core_ids=list(range(ncores)))
    R = res.results
    if _debug:
        return R
    y_prompt = np.concatenate([R[c]["y_p"] for c in range(ncores)], 0).reshape(1, ncores * 1024, D)
    y_sample = np.concatenate([R[c]["y_s"] for c in range(ncores)], 0).reshape(ncores * 16, 4, D)
    last = R[ncores - 1]
    nk_p = last["nk_p"][128:256].reshape(1, 1, 128, 4, 64)
    nv_p = last["nv_p"][128:256].reshape(1, 1, 128, 4, 64)
    nc_p = last["nc_p"][2:4].reshape(1, 1, 2, 2048)
    nf_p = last["nf_p"][2:4].reshape(1, 1, 2, DFF)
    nk_s = np.concatenate([R[c]["nk_s"] for c in range(ncores)], 0).reshape(1, ncores * 16, 128, 4, 64)
    nv_s = np.concatenate([R[c]["nv_s"] for c in range(ncores)], 0).reshape(1, ncores * 16, 128, 4, 64)
    nc_s = np.concatenate([R[c]["nc_s"] for c in range(ncores)], 0).reshape(1, ncores * 16, 2, 2048)
    nf_s = np.concatenate([R[c]["nf_s"] for c in range(ncores)], 0).reshape(1, ncores * 16, 2, DFF)
    return (y_prompt.astype(f32), y_sample.astype(f32), nk_p.astype(f32), nv_p.astype(f32), nc_p.astype(f32),
            nf_p.astype(f32), nk_s.astype(f32), nv_s.astype(f32), nc_s.astype(f32), nf_s.astype(f32))
```

```python
import numpy as np
import concourse.bass as bass
import concourse.mybir as mybir
from concourse.bass_utils import run_bass_kernel_spmd

F32, BF16, U8 = mybir.dt.float32, mybir.dt.bfloat16, mybir.dt.uint8
AF = mybir.ActivationFunctionType
ALU = mybir.AluOpType

D = 4096
DFF = 11008
NFT = 86
TI = 800
M0, M1 = 252, 800
NM = M1 - M0
O0 = 256
NO = M1 - O0
TILES = [(0, 128), (128, 128), (256, 128), (384, 128), (512, 128), (640, 128), (768, 32)]
MP = [(252, 4), (256, 128), (384, 128), (512, 128), (640, 128), (768, 32)]
OP = [(256, 128), (384, 128), (512, 128), (640, 128), (768, 32)]
EPS = 1e-5
NPFM = 32 + 32 + 16 + 16 + 48 + 258 + 32

G0 = 0
S0 = 6912
ZX = 32768
ZC = 126464
ZR = 161792
ARENA = 210944


class Res:
    __slots__ = ("name", "space", "lo", "hi", "w", "r", "al")

    def __init__(self, name, space, lo, hi):
        self.name, self.space, self.lo, self.hi = name, space, lo, hi
        self.w = {}
        self.r = {}
        self.al = [self]


class Eng:
    def __init__(self, name, h, sem, key):
        self.name, self.h, self.sem, self.key = name, h, sem, key
        self.cnt = 0
        self.seen = {}


class DQ:
    def __init__(self, eng, sems):
        self.eng = eng
        self.sems = sems
        self.tot = [0] * len(sems)
        self.n = 0


class Builder:
    def __init__(self, debug=False):
        self.debug = debug
        nc = bass.Bass("TRN2", target_bir_lowering=False)
        self.nc = nc
        self.res = []
        self.semh = {}
        self.nkey = 0
        self.PE = self.mk_eng("pe", nc.tensor)
        self.ACT = self.mk_eng("act", nc.scalar)
        self.DVE = self.mk_eng("dve", nc.vector)
        self.POOL = self.mk_eng("pool", nc.gpsimd)
        self.SP = Eng("sp", nc.sync, None, -1)
        self.qsp = DQ(self.SP, [self.mk_sem("dsp%d" % i) for i in range(16)])
        self.qw = DQ(self.POOL, [self.mk_sem("dw%d" % i) for i in range(NSLOT)])
        self.arena = nc.alloc_sbuf_tensor("arena", [128, ARENA], U8)
        self.psum = nc.alloc_psum_tensor("psum", [128, 8, 512], F32)
        self.bank = [self.new_res("bank%d" % b, "psum", b * 2048, (b + 1) * 2048) for b in range(8)]
        self.dbg_outs = []

    def mk_sem(self, name):
        s = self.nc.alloc_semaphore(name)
        k = self.nkey
        self.nkey += 1
        self.semh[k] = s
        return (s, k)

    def mk_eng(self, name, h):
        s, k = self.mk_sem("s_" + name)
        return Eng(name, h, s, k)

    def new_res(self, name, space, lo, hi):
        r = Res(name, space, lo, hi)
        for o in self.res:
            if o.space == space and o.lo < hi and lo < o.hi:
                o.al.append(r)
                r.al.append(o)
        self.res.append(r)
        return r

    def sb(self, name, off, shape, dt, nres=0):
        n = 1
        for s in shape[1:]:
            n *= s
        esz = 4 if dt == F32 else 2
        nb = n * esz
        assert off % 4 == 0 and off + nb <= ARENA, (name, off, nb)
        v = self.arena[0:shape[0], off:off + nb].bitcast(dt)
        if len(shape) == 3:
            v = v.rearrange("p (a b) -> p a b", a=shape[1])
        elif len(shape) == 4:
            v = v.rearrange("p (a b c) -> p a b c", a=shape[1], b=shape[2])
        if nres:
            per = nb // nres
            rs = [self.new_res("%s%d" % (name, i), "sbuf", off + i * per, off + (i + 1) * per) for i in range(nres)]
            return v, rs
        return v, self.new_res(name, "sbuf", off, off + nb)

    def dres(self, name):
        return self.new_res(name, "d:" + name, 0, 1)

    def sync(self, eng, reads, writes):
        raw = {}
        oth = {}
        for r in reads:
            for a in r.al:
                for k, v in a.w.items():
                    if v > raw.get(k, 0):
                        raw[k] = v
        for w in writes:
            for a in w.al:
                for k, v in a.w.items():
                    if v > oth.get(k, 0):
                        oth[k] = v
                for k, v in a.r.items():
                    if v > oth.get(k, 0):
                        oth[k] = v
        need = {}
        for k, v in raw.items():
            if k == eng.key and eng.name == "pe":
                continue
            need[k] = v
        for k, v in oth.items():
            if k == eng.key:
                continue
            if v > need.get(k, 0):
                need[k] = v
        for k, v in need.items():
            if eng.seen.get(k, 0) < v:
                eng.h.wait_ge(self.semh[k], v)
                eng.seen[k] = v

    def mark(self, key, val, reads, writes):
        for r in reads:
            if r.r.get(key, 0) < val:
                r.r[key] = val
        for w in writes:
            w.w = {key: val}
            w.r = {}

    def done(self, eng, ins, reads, writes):
        ins.then_inc(eng.sem, 1)
        eng.cnt += 1
        self.mark(eng.key, eng.cnt, reads, writes)

    def dma(self, q, out, in_, reads, writes):
        i = q.n % len(q.sems)
        q.n += 1
        sem, key = q.sems[i]
        eng = q.eng
        if q.tot[i] > 0 and eng.seen.get(key, 0) < q.tot[i]:
            eng.h.wait_ge(sem, q.tot[i])
            eng.seen[key] = q.tot[i]
        self.sync(eng, reads, writes)
        ins = eng.h.dma_start(out=out, in_=in_, max_dma_last_dim=16384)
        ins.then_inc(sem, 16)
        q.tot[i] += 16
        self.mark(key, q.tot[i], reads, writes)

    def act(self, out, in_, func, reads, writes, scale=1.0, bias=None, accum=None):
        self.sync(self.ACT, reads, writes)
        kw = {}
        if bias is not None:
            kw["bias"] = bias
        if accum is not None:
            kw["accum_out"] = accum
        ins = self.nc.scalar.activation(out=out, in_=in_, func=func, scale=scale, **kw)
        self.done(self.ACT, ins, reads, writes)

    def tt(self, out, in0, in1, op, reads, writes, eng=None):
        eng = eng or self.DVE
        self.sync(eng, reads, writes)
        ins = eng.h.tensor_tensor(out=out, in0=in0, in1=in1, op=op)
        self.done(eng, ins, reads, writes)

    def ts(self, out, in0, s1, op0, reads, writes, s2=None, op1=None, eng=None):
        eng = eng or self.DVE
        self.sync(eng, reads, writes)
        if op1 is None:
            ins = eng.h.tensor_scalar(out=out, in0=in0, scalar1=s1, scalar2=None, op0=op0)
        else:
            ins = eng.h.tensor_scalar(out=out, in0=in0, scalar1=s1, scalar2=s2, op0=op0, op1=op1)
        self.done(eng, ins, reads, writes)

    def stt(self, out, in0, scalar, in1, op0, op1, reads, writes):
        self.sync(self.DVE, reads, writes)
        ins = self.nc.vector.scalar_tensor_tensor(out=out, in0=in0, scalar=scalar, in1=in1, op0=op0, op1=op1)
        self.done(self.DVE, ins, reads, writes)

    def recip(self, out, in_, reads, writes, fast=False):
        if fast:
            self.act(out, in_, AF.Ln, reads, writes)
            self.act(out, out, AF.Exp, writes, writes, scale=-1.0)
            return
        self.sync(self.DVE, reads, writes)
        ins = self.nc.vector.reciprocal(out=out, in_=in_)
        self.done(self.DVE, ins, reads, writes)

    def copy(self, eng, out, in_, reads, writes):
        self.sync(eng, reads, writes)
        ins = eng.h.tensor_copy(out=out, in_=in_)
        self.done(eng, ins, reads, writes)

    def memset(self, eng, ap, val, writes):
        self.sync(eng, [], writes)
        ins = eng.h.memset(ap, val)
        self.done(eng, ins, [], writes)

    def mms(self, ops, reads, writes):
        self.sync(self.PE, reads, writes)
        ins = None
        for (o, l, r, st, sp) in ops:
            ins = self.nc.tensor.matmul(o, l, r, start=st, stop=sp)
        self.done(self.PE, ins, reads, writes)

    def transposes(self, ops, reads, writes):
        self.sync(self.PE, reads, writes)
        ins = None
        for (o, i, ident) in ops:
            ins = self.nc.tensor.transpose(o, i, ident)
        self.done(self.PE, ins, reads, writes)

    def tap(self, name, view, res, shape):
        if not self.debug:
            return
        d = self.nc.dram_tensor("dbg_" + name, list(shape), view.dtype, kind="ExternalOutput").ap()
        r = self.dres("dbg_" + name)
        n = shape[1]
        for c0 in range(0, n, 4096):
            c1 = min(n, c0 + 4096)
            self.dma(self.qsp, d[:, c0:c1], view[:, c0:c1], [res] if not isinstance(res, list) else res, [])
        self.dbg_outs.append(("dbg_" + name, r))


NSLOT = 6


class WStream:
    def __init__(self, B):
        self.B = B
        self.slabs = []
        self.issued = 0
        self.taken = 0
        self.donec = 0
        self.slots = [B.sb("ring%d" % s, ZR + s * 8192, [128, 16, 256], BF16) for s in range(NSLOT)]
        self.wres = {}

    def add(self, W, wname, k0, nk, c0):
        src = W[k0 * 128:(k0 + nk) * 128, c0:c0 + 256].rearrange("(k p) c -> p k c", p=128)
        v, r = self.slots[len(self.slabs) % NSLOT]
        if wname not in self.wres:
            self.wres[wname] = self.B.dres(wname)
        self.slabs.append((src, v[:, 0:nk, :], r, self.wres[wname]))

    def fill(self):
        while self.issued < len(self.slabs) and self.issued < self.donec + NSLOT:
            src, v, r, wr = self.slabs[self.issued]
            self.B.dma(self.B.qw, v, src, [wr], [r])
            self.issued += 1

    def take(self):
        assert self.taken < self.issued, (self.taken, self.issued)
        src, v, r, wr = self.slabs[self.taken]
        self.taken += 1
        return v, r

    def done(self):
        self.donec += 1
        self.fill()


def build_program(npass=2, debug=False, stop_after=99):
    B = Builder(debug)
    nc = B.nc

    def din(name, shape):
        return nc.dram_tensor(name, list(shape), F32, kind="ExternalInput").ap(), B.dres(name)

    def dout(name, shape, kind="ExternalOutput"):
        return nc.dram_tensor(name, list(shape), F32, kind=kind).ap(), B.dres(name)

    xh, xh_r = din("xh", [npass * TI, D])
    ck, ck_r = din("ck", [npass * 8 * 128, 256])
    cv, cv_r = din("cv", [npass * 8 * 128, 256])
    sc, sc_r = din("sc", [npass * 16, 2048])
    sf, sf_r = din("sf", [npass * 16, DFF])
    w_in, _ = din("w_in", [D, 8704])
    w_out, _ = din("w_out", [D, D])
    w_gu, _ = din("w_gu", [D, 2 * DFF])
    w_dn, _ = din("w_dn", [DFF, D])
    pfm_d, pfm_r = din("pfm", [128, NPFM])
    sinks_d, sinks_r = din("sinks", [1, 32])
    c128_d, c128_r = din("c128", [128, 644])
    maskn_d, maskn_r = din("maskn", [32, 32])
    tfm_d, tfm_r = din("tfm", [npass * 128, 2 * NM])
    ttok_d, ttok_r = din("ttok", [npass * 128, 7 * 32])
    kval_d, kval_r = din("kval", [npass * 128, 2])

    y_p, y_p_r = dout("y_p", [npass * 512, D])
    y_s, y_s_r = dout("y_s", [npass * 32, D])
    nk_p, nk_p_r = dout("nk_p", [npass * 128, 256])
    nv_p, nv_p_r = dout("nv_p", [npass * 128, 256])
    nc_p, nc_p_r = dout("nc_p", [npass * 2, 2048])
    nf_p, nf_p_r = dout("nf_p", [npass * 2, DFF])
    nk_s, nk_s_r = dout("nk_s", [npass * 8 * 128, 256])
    nv_s, nv_s_r = dout("nv_s", [npass * 8 * 128, 256])
    nc_s, nc_s_r = dout("nc_s", [npass * 16, 2048])
    nf_s, nf_s_r = dout("nf_s", [npass * 16, DFF])
    hmid, _ = dout("hmid", [npass * 32 * 128, NM], kind="Internal")
    hout, _ = dout("hout", [npass * 32 * 128, NO], kind="Internal")
    hmid_rs = [[B.dres("hmid%d_%d" % (p, j)) for j in range(32)] for p in range(npass)]
    hout_rs = [[B.dres("hout%d_%d" % (p, j)) for j in range(32)] for p in range(npass)]
    out_res = [y_p_r, y_s_r, nk_p_r, nv_p_r, nc_p_r, nf_p_r, nk_s_r, nv_s_r, nc_s_r, nf_s_r]

    PS = B.psum
    bank = B.bank

    o = G0
    identf, identf_r = B.sb("identf", o, [128, 128], F32); o += 512
    onesf, onesf_r = B.sb("onesf", o, [128, 128], F32); o += 512
    identb, identb_r = B.sb("identb", o, [128, 128], BF16); o += 256
    onesb, onesb_r = B.sb("onesb", o, [128, 128], BF16); o += 256
    permb, permb_r = B.sb("permb", o, [128, 128], BF16); o += 256
    maskO, maskO_r = B.sb("maskO", o, [128, 128], BF16); o += 256
    maskP, maskP_r = B.sb("maskP", o, [128, 128], BF16); o += 256
    maskc, maskc_r = B.sb("maskc", o, [128, 4], BF16); o += 64
    masknb, masknb_r = B.sb("masknb", o, [128, 32], BF16); o += 64
    pfm, pfm_sr = B.sb("pfm", o, [128, NPFM], F32); o += 1792
    sinkexp, sinkexp_r = B.sb("sinkexp", o, [128, 32], F32); o += 128
    epsc, epsc_r = B.sb("epsc", o, [128, 1], F32); o += 64
    kvalid, kvalid_r = B.sb("kvalid", o, [128, 2], F32); o += 64
    ssv, ssv_r = B.sb("ssv", o, [128, 8], F32); o += 64
    rsv, rsv_r = B.sb("rsv", o, [128, 8], F32); o += 64
    rstd_f, rstd_f_r = B.sb("rstd_f", o, [128, NM], F32); o += 2240
    assert o <= S0, o
    gmix = pfm[:, 0:32]
    gffn = pfm[:, 32:64]
    gattn = pfm[:, 64:80]
    gconv = pfm[:, 80:96]
    convw = pfm[:, 96:144].rearrange("p (j i) -> p j i", i=3)
    fconvw = pfm[:, 144:402].rearrange("p (j i) -> p j i", i=3)
    gfin = pfm[:, 402:434]

    WS = WStream(B)

    def add_pair(W, wname, nkt, c0):
        k0 = 0
        while k0 < nkt:
            nk = min(16, nkt - k0)
            WS.add(W, wname, k0, nk, c0)
            k0 += nk

    for p in range(npass):
        add_pair(w_in, "w_in", 32, 2048)
        add_pair(w_in, "w_in", 32, 2304)
        for s in range(8):
            add_pair(w_in, "w_in", 32, 256 * s)
        for sg in range(8):
            add_pair(w_in, "w_in", 32, 2560 + 256 * sg)
            add_pair(w_in, "w_in", 32, 4608 + 256 * sg)
            add_pair(w_in, "w_in", 32, 6656 + 256 * sg)
        for s in range(16):
            add_pair(w_out, "w_out", 32, 256 * s)
        for s in range(43):
            add_pair(w_gu, "w_gu", 32, 256 * s)
            add_pair(w_gu, "w_gu", 32, DFF + 256 * s)
        for s in range(16):
            add_pair(w_dn, "w_dn", NFT, 256 * s)

    def pair_view(pi):
        return PS[:, 2 * pi:2 * pi + 2, :].rearrange("p a b -> p (a b)")

    def pair_res(pi):
        return [bank[2 * pi], bank[2 * pi + 1]]

    cst, cst_r = B.sb("cst", ZX, [128, 644], F32)
    B.dma(B.qsp, cst, c128_d, [c128_r], [cst_r])
    B.copy(B.DVE, identf, cst[:, 0:128], [cst_r], [identf_r])
    B.copy(B.DVE, identb, cst[:, 0:128], [cst_r], [identb_r])
    B.copy(B.DVE, maskO, cst[:, 128:256], [cst_r], [maskO_r])
    B.copy(B.DVE, maskP, cst[:, 256:384], [cst_r], [maskP_r])
    B.copy(B.DVE, permb, cst[:, 384:512], [cst_r], [permb_r])
    B.copy(B.DVE, onesb, cst[:, 512:640], [cst_r], [onesb_r])
    B.copy(B.DVE, onesf, cst[:, 512:640], [cst_r], [onesf_r])
    B.copy(B.DVE, maskc, cst[:, 640:644], [cst_r], [maskc_r])
    cst2, cst2_r = B.sb("cst2", ZX + 4096, [128, 32], F32)
    B.dma(B.qsp, cst2[0:32, :], maskn_d, [maskn_r], [cst2_r])
    B.copy(B.DVE, masknb[0:32, :], cst2[0:32, :], [cst2_r], [masknb_r])
    B.dma(B.qsp, pfm, pfm_d, [pfm_r], [pfm_sr])
    B.dma(B.qsp, cst2, sinks_d[0, :].partition_broadcast(128), [sinks_r], [cst2_r])
    B.act(sinkexp, cst2, AF.Exp, [cst2_r], [sinkexp_r])
    B.memset(B.DVE, epsc, EPS, [epsc_r])

    WS.fill()

    for p in range(npass):
        E0 = ZX + 51200 + 17664
        o = E0
        CT, CT_r = B.sb("CT", o, [128, NM], F32); o += 2240
        ST, ST_r = B.sb("ST", o, [128, NM], F32); o += 2240
        ttok, ttok_r2 = B.sb("ttok", o, [128, 7, 32], F32); o += 896
        acc, acc_r = B.sb("acc", o, [128, NM], F32); o += 2240
        rstd_t, rstd_t_r = B.sb("rstd_t", o, [128, NM], F32); o += 2240
        cstT, cstT_r = B.sb("cstT", o, [128, 16, 16], F32); o += 1024
        sqt, sqt_r = B.sb("sqt", o, [128, NM], F32); o += 2240
        scf, scf_r = B.sb("scf", o, [16, 2048], F32); o += 8192
        assert o <= ZC, o
        B.dma(B.qsp, CT, tfm_d[p * 128:(p + 1) * 128, 0:NM], [tfm_r], [CT_r])
        B.dma(B.qsp, ST, tfm_d[p * 128:(p + 1) * 128, NM:2 * NM], [tfm_r], [ST_r])
        B.dma(B.qsp, ttok.rearrange("p a b -> p (a b)"), ttok_d[p * 128:(p + 1) * 128, :], [ttok_r], [ttok_r2])
        B.dma(B.qsp, kvalid, kval_d[p * 128:(p + 1) * 128, :], [kval_r], [kvalid_r])

        if stop_after <= 0:
            break
        aT, aT_r = B.sb("aT", ZX, [128, 32, TI], BF16, nres=32)
        convT, convT_r = B.sb("convT", ZX + 51200, [128, 16, NM], BF16, nres=16)
        qT, qT_r = B.sb("qT", ZC, [128, 16, NM], BF16, nres=16)
        KT, KT_r = B.sb("KT", ZC + 17536, [128, 4, TI], BF16)
        Vd, Vd_r = B.sb("Vd", ZC + 17536 + 6400, [128, 7, 4, 128], BF16, nres=7)

        xin = [B.sb("xin%d" % i, ZC + i * 16384, [128, D], F32) for i in range(2)]
        xb = [B.sb("xb%d" % i, ZX + 51200 + i * 8192, [128, D], BF16) for i in range(2)]
        for i, (r0, nr) in enumerate(TILES):
            xi, xi_r = xin[i % 2]
            xbb, xbb_r = xb[i % 2]
            B.dma(B.qsp, xi[:nr, :], xh[p * TI + r0:p * TI + r0 + nr, :], [xh_r], [xi_r])
            import os
            cut = int(os.environ.get("K1CUT", "9"))
            if cut <= 0:
                continue
            B.act(xbb[:nr, :], xi[:nr, :], AF.Square, [xi_r], [xbb_r, ssv_r], accum=ssv[:nr, i:i + 1])
            if cut <= 1:
                continue
            B.act(rsv[:nr, i:i + 1], ssv[:nr, i:i + 1], AF.Ln, [ssv_r, epsc_r], [rsv_r], scale=1.0 / D, bias=epsc[:nr, :])
            B.act(rsv[:nr, i:i + 1], rsv[:nr, i:i + 1], AF.Exp, [rsv_r], [rsv_r], scale=-0.5)
            B.act(xbb[:nr, :], xi[:nr, :], AF.Copy, [xi_r, rsv_r], [xbb_r], scale=rsv[:nr, i:i + 1])
            if cut <= 2:
                continue
            for kg in range(4):
                b = (i * 4 + kg) % 6
                pt = PS[:, b, :].bitcast(BF16).rearrange("p (a c) -> p a c", a=8)
                ops = []
                for kk in range(8):
                    k = kg * 8 + kk
                    ops.append((pt[:, kk, 0:nr], xbb[:nr, k * 128:(k + 1) * 128], identb[:nr, :nr]))
                B.transposes(ops, [xbb_r, identb_r], [bank[b]])
                if cut <= 3:
                    continue
                B.tt(aT[:, kg * 8:(kg + 1) * 8, r0:r0 + nr], pt[:, :, 0:nr],
                     gmix[:, kg * 8:(kg + 1) * 8].unsqueeze(2).to_broadcast([128, 8, nr]), ALU.mult,
                     [bank[b], pfm_sr], aT_r[kg * 8:(kg + 1) * 8])
        if p == 0 and cut >= 9:
            B.tap("aT", aT.rearrange("p a b -> p (a b)"), aT_r, [128, 32 * TI])
        if stop_after <= 1:
            break

        o = S0
        Kf, Kf_r = B.sb("Kf", o, [128, 256], F32); o += 1024
        Vf, Vf_r = B.sb("Vf", o, [128, 256], F32); o += 1024
        Kd, Kd_r = B.sb("Kd", o, [128, 4, 128], BF16); o += 1024
        rA, rA_r = B.sb("rA", o, [128, 4, 16], F32); o += 256
        rB, rB_r = B.sb("rB", o, [128, 4, 16], F32); o += 256
        S_after_kv = o
        wKs = [WS.take(), WS.take()]
        wVs = [WS.take(), WS.take()]
        wK_rs = [r for (_, r) in wKs]
        wV_rs = [r for (_, r) in wVs]
        for i, (r0, nr) in enumerate(TILES):
            pk = PS[:, 6, 0:256]
            pv = PS[:, 7, 0:256]
            ops = [(pk[:nr, :], aT[:, k, r0:r0 + nr], wKs[k // 16][0][:, k % 16, :], k == 0, k == 31) for k in range(32)]
            B.mms(ops, aT_r + wK_rs, [bank[6]])
            ops = [(pv[:nr, :], aT[:, k, r0:r0 + nr], wVs[k // 16][0][:, k % 16, :], k == 0, k == 31) for k in range(32)]
            B.mms(ops, aT_r + wV_rs, [bank[7]])
            pv3 = pv[:nr, :].rearrange("p (g d) -> p g d", g=4)
            B.act(Vd[:nr, i, :, 0:64], pv3, AF.Copy, [bank[7]], [Vd_r[i]])
            B.act(Vd[:nr, i, :, 64:128], pv3, AF.Copy, [bank[7]], [Vd_r[i]])
            if i >= 5:
                B.act(Vf[:nr, :], pv[:nr, :], AF.Copy, [bank[7]], [Vf_r])
            B.act(Kf[:nr, :], pk[:nr, :], AF.Copy, [bank[6]], [Kf_r])
            K3 = Kf[:nr, :].rearrange("p (g d) -> p g d", g=4)
            cc_b = ttok[:nr, i, 0:16].unsqueeze(1).to_broadcast([nr, 4, 16])
            ss_b = ttok[:nr, i, 16:32].unsqueeze(1).to_broadcast([nr, 4, 16])
            B.tt(rA[:nr], K3[:, :, 0:16], cc_b, ALU.mult, [Kf_r, ttok_r2], [rA_r])
            B.tt(rB[:nr, :, 0:8], K3[:, :, 8:16], ss_b[:, :, 0:8], ALU.mult, [Kf_r, ttok_r2], [rB_r])
            B.tt(rB[:nr, :, 8:16], K3[:, :, 0:8], ss_b[:, :, 8:16], ALU.mult, [Kf_r, ttok_r2], [rB_r])
            B.tt(K3[:, :, 0:16], rA[:nr], rB[:nr], ALU.add, [rA_r, rB_r], [Kf_r])
            B.copy(B.DVE, Kd[:nr, :, 0:64], K3, [Kf_r], [Kd_r])
            B.copy(B.DVE, Kd[:nr, :, 64:128], K3, [Kf_r], [Kd_r])
            pT = PS[:, 6, :].bitcast(BF16)[:, 0:512].rearrange("p (g c) -> p g c", g=4)
            ops = [(pT[:, g, 0:nr], Kd[:nr, g, :], identb[:nr, :nr]) for g in range(4)]
            B.transposes(ops, [Kd_r, identb_r], [bank[6]])
            B.act(KT[:, :, r0:r0 + nr], pT[:, :, 0:nr], AF.Copy, [bank[6]], [KT_r])
            if i == 5:
                B.dma(B.qsp, nk_p[p * 128:(p + 1) * 128, :], Kf, [Kf_r], [])
                B.dma(B.qsp, nv_p[p * 128:(p + 1) * 128, :], Vf, [Vf_r], [])
            if i == 6:
                nk3 = nk_s.rearrange("(s w) c -> s w c", w=128)
                nv3 = nv_s.rearrange("(s w) c -> s w c", w=128)
                ck3 = ck.rearrange("(s w) c -> s w c", w=128)
                cv3 = cv.rearrange("(s w) c -> s w c", w=128)
                B.dma(B.qsp, nk3[p * 8:(p + 1) * 8, 124:128, :], Kf[0:32, :], [Kf_r], [])
                B.dma(B.qsp, nv3[p * 8:(p + 1) * 8, 124:128, :], Vf[0:32, :], [Vf_r], [])
                B.dma(B.qsp, nk3[p * 8:(p + 1) * 8, 0:124, :], ck3[p * 8:(p + 1) * 8, 4:128, :], [ck_r], [])
                B.dma(B.qsp, nv3[p * 8:(p + 1) * 8, 0:124, :], cv3[p * 8:(p + 1) * 8, 4:128, :], [cv_r], [])
        for _ in range(4):
            WS.done()
        if p == 0:
            B.tap("KT", KT.rearrange("p a b -> p (a b)"), KT_r, [128, 4 * TI])
            B.tap("Vd", Vd.rearrange("p a b c -> p (a b c)"), Vd_r, [128, 7 * 4 * 128])

        o = S_after_kv
        qraw = [B.sb("qraw%d" % i, o + i * 1152, [128, NM], BF16) for i in range(2)]; o += 2304
        m1 = [B.sb("m1_%d" % i, o + i * 2240, [128, NM], F32) for i in range(2)]; o += 4480
        m2 = [B.sb("m2_%d" % i, o + i * 2240, [128, NM], F32) for i in range(2)]; o += 4480
        S_after_q = o

        def pair_mm(nkt, rhs_of_k, rhs_res, c0=M0):
            pis = [next_pair(), next_pair()]
            k0 = 0
            while k0 < nkt:
                nk = min(16, nkt - k0)
                w, w_r = WS.take()
                for jj in range(2):
                    pv_ = pair_view(pis[jj])
                    ops = []
                    for kk in range(nk):
                        k = k0 + kk
                        ops.append((pv_[:, c0:512], w[:, kk, jj * 128:(jj + 1) * 128], rhs_of_k(k, c0, 512), k == 0, k == nkt - 1))
                    for kk in range(nk):
                        k = k0 + kk
                        ops.append((pv_[:, 512:800], w[:, kk, jj * 128:(jj + 1) * 128], rhs_of_k(k, 512, 800), k == 0, k == nkt - 1))
                    B.mms(ops, rhs_res[k0:k0 + nk] + [w_r], pair_res(pis[jj]))
                WS.done()
                k0 += nk
            return pis

        rot = [0]

        def next_pair():
            pi = rot[0] % 3
            rot[0] += 1
            return pi

        for s in range(8):
            pis = pair_mm(32, lambda k, a, b: aT[:, k, a:b], aT_r)
            for jj in range(2):
                j = 2 * s + jj
                pi = pis[jj]
                pm = pair_view(pi)[:, M0:M1]
                qr, qr_r = qraw[j % 2]
                B.act(qr, pm, AF.Copy, pair_res(pi), [qr_r])
                p3 = pair_view(3)
                B.mms([(p3[:, 252:512], permb, qr[:, 0:260], True, True), (p3[:, 512:800], permb, qr[:, 260:NM], True, True)],
                      [permb_r, qr_r], pair_res(3))
                a1, a1_r = m1[j % 2]
                a2, a2_r = m2[j % 2]
                B.tt(a1, p3[:, M0:M1], ST, ALU.mult, pair_res(3) + [ST_r], [a1_r])
                B.tt(a2, qr, CT, ALU.mult, [qr_r, CT_r], [a2_r])
                B.tt(qT[:, j, :], a1, a2, ALU.add, [a1_r, a2_r], [qT_r[j]])
        if p == 0:
            B.tap("qT", qT.rearrange("p a b -> p (a b)"), qT_r, [128, 16 * NM])
        if stop_after <= 2:
            break

        B.dma(B.qsp, scf, sc[p * 16:(p + 1) * 16, :], [sc_r], [scf_r])
        for jg in range(4):
            pt = PS[:, 6, 0:64].rearrange("p (a c) -> p a c", a=4)
            ops = [(pt[:, jj, :], scf[0:16, (jg * 4 + jj) * 128:(jg * 4 + jj + 1) * 128], identf[0:16, 0:16]) for jj in range(4)]
            B.transposes(ops, [scf_r, identf_r], [bank[6]])
            B.act(cstT[:, jg * 4:(jg + 1) * 4, :], pt, AF.Copy, [bank[6]], [cstT_r])
        o = S_after_kv
        cbS = [B.sb("cbS%d" % i, o + i * 2240, [128, NM], F32) for i in range(2)]; o += 4480
        ccS = [B.sb("ccS%d" % i, o + i * 2240, [128, NM], F32) for i in range(2)]; o += 4480
        ub = [B.sb("ub%d" % i, o + i * 2240, [128, NM], F32) for i in range(2)]; o += 4480
        yb, yb_r = B.sb("yb", o, [128, NM], F32); o += 2240
        ob, ob_r = B.sb("ob", o, [128, NM], F32); o += 2240
        ext, ext_r = B.sb("ext", o, [128, 8, 6], F32); o += 192
        ysb, ysb_r = B.sb("ysb", o, [128, 8, 4], F32); o += 128
        stg, stg_r = B.sb("stg", o, [128, 2, 18], F32); o += 192
        stO = [B.sb("stO%d" % i, o + i * 1024, [18, 256], F32) for i in range(2)]; o += 2048
        assert o <= ZX, o
        B.memset(B.DVE, ob[:, 0:2], 0.0, [ob_r])
        B.memset(B.DVE, acc, 0.0, [acc_r])
        def flush_stg(sg_):
            pt = PS[:, 7, 0:256].rearrange("p (a c) -> p a c", a=2)
            B.transposes([(pt[0:18, jj, :], stg[:, jj, :], identf) for jj in range(2)], [stg_r, identf_r], [bank[7]])
            so, so_r = stO[sg_ % 2]
            B.act(so.rearrange("p (a c) -> p a c", a=2), pt[0:18], AF.Copy, [bank[7]], [so_r])
            B.dma(B.qsp, nc_p[p * 2:(p + 1) * 2, sg_ * 256:(sg_ + 1) * 256], so[0:2, :], [so_r], [])
            B.dma(B.qsp, nc_s[p * 16:(p + 1) * 16, sg_ * 256:(sg_ + 1) * 256], so[2:18, :], [so_r], [])

        for sg in range(8):
            pis = pair_mm(32, lambda k, a, b: aT[:, k, a:b], aT_r)
            if sg > 0:
                flush_stg(sg - 1)
            for jj in range(2):
                B.act(cbS[jj][0], pair_view(pis[jj])[:, M0:M1], AF.Copy, pair_res(pis[jj]), [cbS[jj][1]])
            pis = pair_mm(32, lambda k, a, b: aT[:, k, a:b], aT_r)
            for jj in range(2):
                B.act(ccS[jj][0], pair_view(pis[jj])[:, M0:M1], AF.Copy, pair_res(pis[jj]), [ccS[jj][1]])
            pis = pair_mm(32, lambda k, a, b: aT[:, k, a:b], aT_r)
            for jj in range(2):
                j = 2 * sg + jj
                pi = pis[jj]
                pm = pair_view(pi)[:, M0:M1]
                u, u_r = ub[jj]
                B.tt(u, pm, ccS[jj][0], ALU.mult, pair_res(pi) + [ccS[jj][1]], [u_r])
                B.act(yb[:, 2:516], u[:, 0:514], AF.Copy, [u_r, pfm_sr], [yb_r], scale=convw[:, j, 0:1])
                B.stt(yb[:, 2:516], u[:, 1:515], convw[:, j, 1:2], yb[:, 2:516], ALU.mult, ALU.add, [u_r, pfm_sr, yb_r], [yb_r])
                B.stt(yb[:, 2:516], u[:, 2:516], convw[:, j, 2:3], yb[:, 2:516], ALU.mult, ALU.add, [u_r, pfm_sr, yb_r], [yb_r])
                B.copy(B.DVE, ext[:, :, 0:2], cstT[:, j, :].rearrange("p (s r) -> p s r", r=2), [cstT_r], [ext_r])
                B.copy(B.DVE, ext[:, :, 2:6], u[:, 516:548].rearrange("p (s t) -> p s t", t=4), [u_r], [ext_r])
                B.ts(ysb, ext[:, :, 0:4], convw[:, j, 0:1], ALU.mult, [ext_r, pfm_sr], [ysb_r])
                B.stt(ysb, ext[:, :, 1:5], convw[:, j, 1:2], ysb, ALU.mult, ALU.add, [ext_r, pfm_sr, ysb_r], [ysb_r])
                B.stt(ysb, ext[:, :, 2:6], convw[:, j, 2:3], ysb, ALU.mult, ALU.add, [ext_r, pfm_sr, ysb_r], [ysb_r])
                B.copy(B.DVE, yb[:, 516:548].rearrange("p (s t) -> p s t", t=4), ysb, [ysb_r], [yb_r])
                B.tt(ob[:, 2:NM], yb[:, 2:NM], cbS[jj][0][:, 2:NM], ALU.mult, [yb_r, cbS[jj][1]], [ob_r])
                B.act(convT[:, j, :], ob, AF.Copy, [ob_r], [convT_r[j]])
                B.act(sqt, ob, AF.Square, [ob_r], [sqt_r])
                B.tt(acc, acc, sqt, ALU.add, [acc_r, sqt_r], [acc_r])
                B.copy(B.DVE, stg[:, jj, 0:2], u[:, 514:516], [u_r], [stg_r])
                B.copy(B.DVE, stg[:, jj, 2:18].rearrange("p (s r) -> p s r", r=2), ext[:, :, 4:6], [ext_r], [stg_r])
        flush_stg(7)

        def finish_norm(dst, dst_r, nfeat):
            p3 = pair_view(3)
            B.mms([(p3[:, 252:512], onesf, acc[:, 0:260], True, True), (p3[:, 512:800], onesf, acc[:, 260:NM], True, True)],
                  [onesf_r, acc_r], pair_res(3))
            B.act(dst, p3[:, M0:M1], AF.Ln, pair_res(3) + [epsc_r], [dst_r], scale=1.0 / nfeat, bias=epsc)
            B.act(dst, dst, AF.Exp, [dst_r], [dst_r], scale=-0.5)

        finish_norm(rstd_t, rstd_t_r, 2048)
        for j in range(16):
            B.stt(convT[:, j, :], convT[:, j, :], gconv[:, j:j + 1], rstd_t, ALU.mult, ALU.mult, [convT_r[j], pfm_sr, rstd_t_r], [convT_r[j]])
        if p == 0:
            B.tap("convT", convT.rearrange("p a b -> p (a b)"), convT_r, [128, 16 * NM])
        if stop_after <= 3:
            break

        AOT, AOT_r = B.sb("AOT", ZX, [128, 16, NM], BF16, nres=16)
        o = ZX + 17536
        cKT, cKT_r = B.sb("cKT", o, [128, 8, 4, 128], BF16); o += 8192
        cVd, cVd_r = B.sb("cVd", o, [128, 8, 4, 128], BF16); o += 8192
        Pp = [B.sb("Pp%d" % i, o + i * 1024, [128, 4, 128], BF16) for i in range(2)]; o += 2048
        Po = [B.sb("Po%d" % i, o + i * 1024, [128, 4, 128], BF16) for i in range(2)]; o += 2048
        dsm = [B.sb("dsm%d" % i, o + i * 2048, [128, 512], F32) for i in range(2)]; o += 4096
        ckf = [B.sb("ckf%d" % i, o + i * 1024, [128, 256], F32) for i in range(2)]; o += 2048
        cvf = [B.sb("cvf%d" % i, o + i * 1024, [128, 256], F32) for i in range(2)]; o += 2048
        cKd = [B.sb("cKd%d" % i, o + i * 1024, [128, 4, 128], BF16) for i in range(2)]; o += 2048
        Pc, Pc_r = B.sb("Pc", o, [128, 256], BF16); o += 512
        Pn, Pn_r = B.sb("Pn", o, [128, 256], BF16); o += 512
        assert o <= ZX + 51200

        stT, stT_r = B.sb("stT", S0 + 4480, [128, NFT, 16], F32)
        sff = [B.sb("sff%d" % i, S0 + 4480 + 5504 + i * 4096, [16, 1024], F32) for i in range(3)]
        for step in range(13):
            if step < 11:
                bt = step
                nsub = 8 if bt < 10 else 6
                sfb, sfb_r = sff[bt % 3]
                B.dma(B.qsp, sfb[:, 0:nsub * 128], sf[p * 16:(p + 1) * 16, bt * 1024:bt * 1024 + nsub * 128], [sf_r], [sfb_r])
            if step >= 2:
                bt = step - 2
                nsub = 8 if bt < 10 else 6
                sfb, sfb_r = sff[bt % 3]
                pt = PS[:, bt % 2, 0:128].rearrange("p (a c) -> p a c", a=8)
                B.transposes([(pt[:, jj, :], sfb[0:16, jj * 128:(jj + 1) * 128], identf[0:16, 0:16]) for jj in range(nsub)],
                             [sfb_r, identf_r], [bank[bt % 2]])
                B.act(stT[:, 8 * bt:8 * bt + nsub, :], pt[:, 0:nsub, :], AF.Copy, [bank[bt % 2]], [stT_r])

        QB = [(252, 4, 0, 1, 124)] + [(256 + 128 * i, 128, 1 + i, 2 + i, 0) for i in range(4)]
        import os
        c3 = int(os.environ.get("A3CUT", "99"))
        units = [(qc0, n, tp, to, moff, g, hg) for (qc0, n, tp, to, moff) in QB for g in range(4) for hg in range(2)]

        def stage_a(ui, part):
            qc0, n, tp, to, moff, g, hg = units[ui]
            u2 = ui % 2
            bsS = u2 * 2
            SX = PS[:, bsS + 0, :].rearrange("p (t a q) -> p t a q", t=2, a=2)
            SY = PS[:, bsS + 1, :].rearrange("p (t a q) -> p t a q", t=2, a=2)
            jt0 = 4 * g + 2 * hg
            pp, pp_r = Pp[u2]
            po, po_r = Po[u2]
            if part == 1:
                for hf, SB_, bk in ((0, SX, bsS + 0), (1, SY, bsS + 1)):
                    ops = []
                    for a_ in range(2):
                        jt = jt0 + a_
                        rhs = qT[64 * hf:64 * hf + 64, jt, qc0 - M0:qc0 - M0 + n]
                        ops.append((SB_[:, 0, a_, 0:n], KT[64 * hf:64 * hf + 64, g, tp * 128:(tp + 1) * 128], rhs, True, True))
                        ops.append((SB_[:, 1, a_, 0:n], KT[64 * hf:64 * hf + 64, g, to * 128:(to + 1) * 128], rhs, True, True))
                    B.mms(ops, [KT_r, qT_r[jt0], qT_r[jt0 + 1]], [bank[bk]])
                pp4 = pp.rearrange("p (a b) q -> p a b q", b=2)
                po4 = po.rearrange("p (a b) q -> p a b q", b=2)
                for hf, SB_, bk in ((0, SX, bsS + 0), (1, SY, bsS + 1)):
                    B.act(pp4[:, :, hf, 0:n], SB_[:, 0, :, 0:n], AF.Exp, [bank[bk]], [pp_r], scale=0.125)
                    B.act(po4[:, :, hf, 0:n], SB_[:, 1, :, 0:n], AF.Exp, [bank[bk]], [po_r], scale=0.125)
                return
            mP = maskP[:, moff:moff + n].unsqueeze(1).to_broadcast([128, 4, n])
            mO = maskO[:, moff:moff + n].unsqueeze(1).to_broadcast([128, 4, n])
            if tp <= 1:
                B.stt(pp[:, :, 0:n], pp[:, :, 0:n], kvalid[:, tp:tp + 1], mP, ALU.mult, ALU.mult, [pp_r, kvalid_r, maskP_r], [pp_r])
            else:
                B.tt(pp[:, :, 0:n], pp[:, :, 0:n], mP, ALU.mult, [pp_r, maskP_r], [pp_r])
            if to <= 1:
                B.stt(po[:, :, 0:n], po[:, :, 0:n], kvalid[:, to:to + 1], mO, ALU.mult, ALU.mult, [po_r, kvalid_r, maskO_r], [po_r])
            else:
                B.tt(po[:, :, 0:n], po[:, :, 0:n], mO, ALU.mult, [po_r, maskO_r], [po_r])

        def stage_b(ui, part):
            qc0, n, tp, to, moff, g, hg = units[ui]
            u2 = ui % 2
            bsO = 4 + u2 * 2
            Ob = PS[:, bsO + 0, :].rearrange("p (e q) -> p e q", e=4)
            Db = PS[:, bsO + 1, :].rearrange("p (e q) -> p e q", e=4)
            jt0 = 4 * g + 2 * hg
            pp, pp_r = Pp[u2]
            po, po_r = Po[u2]
            ds, ds_r = dsm[u2]
            ds3 = ds.rearrange("p (e q) -> p e q", e=4)
            if part == 1:
                B.mms([(Ob[:, :, 0:n], Vd[:, tp, g, :], pp[:, :, 0:n], True, False),
                       (Ob[:, :, 0:n], Vd[:, to, g, :], po[:, :, 0:n], False, True)],
                      [Vd_r[tp], Vd_r[to], pp_r, po_r], [bank[bsO + 0]])
                B.mms([(Db[:, :, 0:n], onesb, pp[:, :, 0:n], True, False),
                       (Db[:, :, 0:n], onesb, po[:, :, 0:n], False, True)],
                      [onesb_r, pp_r, po_r], [bank[bsO + 1]])
                h0 = 8 * g + 4 * hg
                B.tt(ds3[:, :, 0:n], Db[:, :, 0:n], sinkexp[:, h0:h0 + 4].unsqueeze(2).to_broadcast([128, 4, n]), ALU.add,
                     [bank[bsO + 1], sinkexp_r], [ds_r])
                return
            B.recip(ds3[:, :, 0:n], ds3[:, :, 0:n], [ds_r], [ds_r], fast=True)
            for hf in range(2):
                O4 = Ob.rearrange("p (a b) q -> p a b q", b=2)[64 * hf:64 * hf + 64, :, hf, 0:n]
                R4 = ds3.rearrange("p (a b) q -> p a b q", b=2)[64 * hf:64 * hf + 64, :, hf, 0:n]
                B.tt(AOT[64 * hf:64 * hf + 64, jt0:jt0 + 2, qc0 - M0:qc0 - M0 + n], O4, R4, ALU.mult,
                     [bank[bsO + 0], ds_r], [AOT_r[jt0], AOT_r[jt0 + 1]])

        nun = len(units)
        stage_a(0, 1)
        stage_a(0, 2)
        for ui in range(1, nun):
            stage_a(ui, 1)
            stage_b(ui - 1, 1)
            stage_a(ui, 2)
            stage_b(ui - 1, 2)
        stage_b(nun - 1, 1)
        stage_b(nun - 1, 2)

        for s in range(8 if c3 > 4 else 0):
            kf, kf_r = ckf[s % 2]
            vf, vf_r = cvf[s % 2]
            kd, kd_r = cKd[s % 2]
            row0 = (p * 8 + s) * 128
            B.dma(B.qsp, kf, ck[row0:row0 + 128, :], [ck_r], [kf_r])
            B.dma(B.qsp, vf, cv[row0:row0 + 128, :], [cv_r], [vf_r])
            kf3 = kf.rearrange("p (g d) -> p g d", g=4)
            vf3 = vf.rearrange("p (g d) -> p g d", g=4)
            B.copy(B.DVE, kd[:, :, 0:64], kf3, [kf_r], [kd_r])
            B.copy(B.DVE, kd[:, :, 64:128], kf3, [kf_r], [kd_r])
            B.act(cVd[:, s, :, 0:64], vf3, AF.Copy, [vf_r], [cVd_r])
            B.act(cVd[:, s, :, 64:128], vf3, AF.Copy, [vf_r], [cVd_r])
            b = s % 2
            pT = PS[:, b, :].bitcast(BF16)[:, 0:512].rearrange("p (g c) -> p g c", g=4)
            B.transposes([(pT[:, g, :], kd[:, g, :], identb) for g in range(4)], [kd_r, identb_r], [bank[b]])
            B.act(cKT[:, s, :, :], pT, AF.Copy, [bank[b]], [cKT_r])
        SC0 = 516
        for g in range(4 if c3 > 5 else 0):
            bs = (g % 2) * 4
            ScX = PS[:, 0, 0:128]
            ScY = PS[:, 1, 0:128]
            SnX = PS[:, 2, 0:128]
            SnY = PS[:, 3, 0:128]
            Ob = PS[:, 4, 0:256]
            Db = PS[:, 5, 0:256]
            bs = 2
            for hf, Sc_, Sn_, bc, bn in ((0, ScX, SnX, 0, 2), (1, ScY, SnY, 1, 3)):
                ops = []
                for s in range(8):
                    for a in range(4):
                        jt = 4 * g + a
                        ops.append((Sc_[:, a * 32 + s * 4:a * 32 + s * 4 + 4], cKT[64 * hf:64 * hf + 64, s, g, :],
                                    qT[64 * hf:64 * hf + 64, jt, SC0 + 4 * s:SC0 + 4 * s + 4], True, True))
                B.mms(ops, [cKT_r] + qT_r[4 * g:4 * g + 4], [bank[bc]])
                ops = []
                for a in range(4):
                    jt = 4 * g + a
                    ops.append((Sn_[0:32, a * 32:(a + 1) * 32], KT[64 * hf:64 * hf + 64, g, 768:800],
                                qT[64 * hf:64 * hf + 64, jt, SC0:SC0 + 32], True, True))
                B.mms(ops, [KT_r] + qT_r[4 * g:4 * g + 4], [bank[bn]])
            if c3 <= 6:
                continue
            Pc5 = Pc.rearrange("p (a b c) -> p a b c", a=4, b=2)
            Pn5 = Pn.rearrange("p (a b c) -> p a b c", a=4, b=2)
            for hf, Sc_, Sn_, bc, bn in ((0, ScX, SnX, 0, 2), (1, ScY, SnY, 1, 3)):
                B.act(Pc5[:, :, hf, :], Sc_.rearrange("p (a c) -> p a c", a=4), AF.Exp, [bank[bc]], [Pc_r], scale=0.125)
                B.act(Pn5[0:32, :, hf, :], Sn_[0:32, :].rearrange("p (a c) -> p a c", a=4), AF.Exp, [bank[bn]], [Pn_r], scale=0.125)
            Pc3 = Pc.rearrange("p (a t) -> p a t", t=4)
            B.tt(Pc3, Pc3, maskc.unsqueeze(1).to_broadcast([128, 64, 4]), ALU.mult, [Pc_r, maskc_r], [Pc_r])
            Pn3 = Pn[0:32, :].rearrange("p (e c) -> p e c", e=8)
            B.tt(Pn3, Pn3, masknb[0:32, :].unsqueeze(1).to_broadcast([32, 8, 32]), ALU.mult, [Pn_r, masknb_r], [Pn_r])
            if c3 <= 7:
                continue
            Pc4 = Pc.rearrange("p (e s t) -> p e s t", e=8, s=8)
            Ob4 = Ob.rearrange("p (e s t) -> p e s t", e=8, s=8)
            ops = [(Ob, Vd[0:32, 6, g, :], Pn[0:32, :], True, False)]
            for s in range(8):
                ops.append((Ob4[:, :, s, :], cVd[:, s, g, :], Pc4[:, :, s, :], False, s == 7))
            B.mms(ops, [Vd_r[6], cVd_r, Pn_r, Pc_r], [bank[4]])
            B.mms([(Db, onesb[0:32, :], Pn[0:32, :], True, False), (Db, onesb, Pc, False, True)],
                  [onesb_r, Pn_r, Pc_r], [bank[5]])
            ds, ds_r = dsm[g % 2]
            ds3 = ds[:, 0:256].rearrange("p (e c) -> p e c", e=8)
            Db3 = Db.rearrange("p (e c) -> p e c", e=8)
            Ob3 = Ob.rearrange("p (e c) -> p e c", e=8)
            B.tt(ds3, Db3, sinkexp[:, 8 * g:8 * g + 8].unsqueeze(2).to_broadcast([128, 8, 32]), ALU.add, [bank[5], sinkexp_r], [ds_r])
            B.recip(ds3, ds3, [ds_r], [ds_r], fast=True)
            for hf in range(2):
                O4 = Ob3.rearrange("p (a b) c -> p a b c", b=2)[64 * hf:64 * hf + 64, :, hf, :]
                R4 = ds3.rearrange("p (a b) c -> p a b c", b=2)[64 * hf:64 * hf + 64, :, hf, :]
                B.tt(AOT[64 * hf:64 * hf + 64, 4 * g:4 * g + 4, SC0:SC0 + 32], O4, R4, ALU.mult,
                     [bank[4], ds_r], AOT_r[4 * g:4 * g + 4])

        B.memset(B.DVE, acc, 0.0, [acc_r])
        for j in range(16):
            B.act(sqt, AOT[:, j, :], AF.Square, [AOT_r[j]], [sqt_r])
            B.tt(acc, acc, sqt, ALU.add, [acc_r, sqt_r], [acc_r])
        finish_norm(rstd_t, rstd_t_r, 2048)
        for j in range(16):
            B.stt(AOT[:, j, :], AOT[:, j, :], gattn[:, j:j + 1], rstd_t, ALU.mult, ALU.mult, [AOT_r[j], pfm_sr, rstd_t_r], [AOT_r[j]])
        if p == 0:
            B.tap("AOT", AOT.rearrange("p a b -> p (a b)"), AOT_r, [128, 16 * NM])
        if stop_after <= 4:
            break

        fT, fT_r = B.sb("fT", ZC, [128, 32, NM], BF16, nres=32)
        xres = [B.sb("xres%d" % i, ZX + 17536 + i * 12288, [128, 6, 512], F32) for i in range(2)]
        o = S0
        hm = [B.sb("hm%d" % i, o + i * 2240, [128, NM], F32) for i in range(2)]; o += 4480
        B.memset(B.DVE, acc, 0.0, [acc_r])

        def load_xres(gi):
            xr, xr_r = xres[gi % 2]
            for pc, (r0, nr) in enumerate(MP):
                B.dma(B.qsp, xr[:nr, pc, :], xh[p * TI + r0:p * TI + r0 + nr, gi * 512:(gi + 1) * 512], [xh_r], [xr_r])

        load_xres(0)

        def rhs4(k, a, b_):
            return (AOT[:, k, a - M0:b_ - M0] if k < 16 else convT[:, k - 16, a - M0:b_ - M0])

        for s in range(16):
            if s % 2 == 0 and s // 2 + 1 < 8:
                load_xres(s // 2 + 1)
            xr, xr_r = xres[(s // 2) % 2]
            pis = pair_mm(32, rhs4, AOT_r + convT_r)
            for jj in range(2):
                j = 2 * s + jj
                pi = pis[jj]
                pm = pair_view(pi)[:, M0:M1]
                xc0 = (s % 2) * 256 + jj * 128
                p3 = pair_view(3)
                B.transposes([(p3[:, r0:r0 + nr], xr[:nr, pc, xc0:xc0 + 128], identf[:nr, :nr]) for pc, (r0, nr) in enumerate(MP)],
                             [xr_r, identf_r], pair_res(3))
                h, h_r = hm[j % 2]
                B.act(h, pm, AF.Copy, pair_res(pi), [h_r])
                B.tt(h, h, p3[:, M0:M1], ALU.add, [h_r] + pair_res(3), [h_r])
                B.dma(B.qsp, hmid[(p * 32 + j) * 128:(p * 32 + j + 1) * 128, :], h, [h_r], [hmid_rs[p][j]])
                B.act(sqt, h, AF.Square, [h_r], [sqt_r])
                B.tt(acc, acc, sqt, ALU.add, [acc_r, sqt_r], [acc_r])
                B.ts(fT[:, j, :], h, gffn[:, j:j + 1], ALU.mult, [h_r, pfm_sr], [fT_r[j]])
        finish_norm(rstd_f, rstd_f_r, D)
        if p == 0:
            B.tap("fT", fT.rearrange("p a b -> p (a b)"), fT_r, [128, 32 * NM])
            B.tap("rstd_f", rstd_f, rstd_f_r, [128, NM])
        if stop_after <= 5:
            break

        actT, actT_r = B.sb("actT", ZX, [128, NFT, NO], BF16, nres=NFT)
        o = S0 + 4480 + 5504
        gs = [B.sb("gs%d" % i, S0 + i * 2240, [128, NM], F32) for i in range(2)]
        yg, yg_r = B.sb("yg", o, [128, NM], F32); o += 2240
        sg_ = [B.sb("sg%d" % i, o + i * 2240, [128, NM], F32) for i in range(4)]; o += 8960
        extf, extf_r = B.sb("extf", o, [128, 8, 6], F32); o += 192
        ysf, ysf_r = B.sb("ysf", o, [128, 8, 4], F32); o += 128
        stgf, stgf_r = B.sb("stgf", o, [128, 2, 18], F32); o += 192
        stOf = [B.sb("stOf%d" % i, o + i * 1024, [18, 256], F32) for i in range(2)]; o += 2048
        assert o <= ZX, o
        def rhs5(k, a, b_):
            return fT[:, k, a - M0:b_ - M0]

        for s in range(43):
            pis = pair_mm(32, rhs5, fT_r)
            for jj in range(2):
                j = 2 * s + jj
                pi = pis[jj]
                pm = pair_view(pi)[:, M0:M1]
                g_, g_r = gs[jj]
                B.tt(g_, pm, rstd_f, ALU.mult, pair_res(pi) + [rstd_f_r], [g_r])
                B.act(yg[:, 2:516], g_[:, 0:514], AF.Copy, [g_r, pfm_sr], [yg_r], scale=fconvw[:, j, 0:1])
                B.stt(yg[:, 2:516], g_[:, 1:515], fconvw[:, j, 1:2], yg[:, 2:516], ALU.mult, ALU.add, [g_r, pfm_sr, yg_r], [yg_r])
                B.stt(yg[:, 2:516], g_[:, 2:516], fconvw[:, j, 2:3], yg[:, 2:516], ALU.mult, ALU.add, [g_r, pfm_sr, yg_r], [yg_r])
                B.copy(B.DVE, extf[:, :, 0:2], stT[:, j, :].rearrange("p (s r) -> p s r", r=2), [stT_r], [extf_r])
                B.copy(B.DVE, extf[:, :, 2:6], g_[:, 516:548].rearrange("p (s t) -> p s t", t=4), [g_r], [extf_r])
                B.ts(ysf, extf[:, :, 0:4], fconvw[:, j, 0:1], ALU.mult, [extf_r, pfm_sr], [ysf_r])
                B.stt(ysf, extf[:, :, 1:5], fconvw[:, j, 1:2], ysf, ALU.mult, ALU.add, [extf_r, pfm_sr, ysf_r], [ysf_r])
                B.stt(ysf, extf[:, :, 2:6], fconvw[:, j, 2:3], ysf, ALU.mult, ALU.add, [extf_r, pfm_sr, ysf_r], [ysf_r])
                B.copy(B.DVE, yg[:, 516:548].rearrange("p (s t) -> p s t", t=4), ysf, [ysf_r], [yg_r])
                sgb, sgb_r = sg_[(s % 2) * 2 + jj]
                B.act(sgb[:, 4:NM], yg[:, 4:NM], AF.Silu, [yg_r], [sgb_r])
                B.tt(sgb[:, 4:NM], sgb[:, 4:NM], rstd_f[:, 4:NM], ALU.mult, [sgb_r, rstd_f_r], [sgb_r])
                B.copy(B.DVE, stgf[:, jj, 0:2], g_[:, 514:516], [g_r], [stgf_r])
                B.copy(B.DVE, stgf[:, jj, 2:18].rearrange("p (s r) -> p s r", r=2), extf[:, :, 4:6], [extf_r], [stgf_r])
            pis = pair_mm(32, rhs5, fT_r)
            pt = PS[:, 6 + s % 2, 0:256].rearrange("p (a c) -> p a c", a=2)
            B.transposes([(pt[0:18, jj, :], stgf[:, jj, :], identf) for jj in range(2)], [stgf_r, identf_r], [bank[6 + s % 2]])
            so, so_r = stOf[s % 2]
            B.act(so.rearrange("p (a c) -> p a c", a=2), pt[0:18], AF.Copy, [bank[6 + s % 2]], [so_r])
            B.dma(B.qsp, nf_p[p * 2:(p + 1) * 2, s * 256:(s + 1) * 256], so[0:2, :], [so_r], [])
            B.dma(B.qsp, nf_s[p * 16:(p + 1) * 16, s * 256:(s + 1) * 256], so[2:18, :], [so_r], [])
            for jj in range(2):
                j = 2 * s + jj
                pi = pis[jj]
                pm = pair_view(pi)[:, M0:M1]
                sgb, sgb_r = sg_[(s % 2) * 2 + jj]
                B.tt(actT[:, j, :], pm[:, 4:NM], sgb[:, 4:NM], ALU.mult, pair_res(pi) + [sgb_r], [actT_r[j]])
        if p == 0:
            B.tap("actT", actT[:, 0:4, :].rearrange("p a b -> p (a b)"), actT_r[0:4], [128, 4 * NO])
        if stop_after <= 6:
            break

        o = S0
        hmb = [B.sb("hmb%d" % i, o + i * 2240, [128, NM], F32) for i in range(4)]; o += 8960
        hob = [B.sb("hob%d" % i, o + i * 2240, [128, NO], F32) for i in range(2)]; o += 4480
        sq6, sq6_r = B.sb("sq6", o, [128, NO], F32); o += 2240
        acc6, acc6_r = B.sb("acc6", o, [128, NM], F32); o += 2240
        rstd_y, rstd_y_r = B.sb("rstd_y", o, [128, NM], F32); o += 2240
        B.memset(B.DVE, acc6, 0.0, [acc6_r])
        def rhs6(k, a, b_):
            return actT[:, k, a - O0:b_ - O0]

        for sp in range(16):
            for jj in range(2):
                j = 2 * sp + jj
                hb, hb_r = hmb[(sp % 2) * 2 + jj]
                B.dma(B.qsp, hb, hmid[(p * 32 + j) * 128:(p * 32 + j + 1) * 128, :], [hmid_rs[p][j]], [hb_r])
            pis = pair_mm(NFT, rhs6, actT_r, c0=O0)
            for jj in range(2):
                j = 2 * sp + jj
                pv_ = pair_view(pis[jj])
                hb, hb_r = hmb[(sp % 2) * 2 + jj]
                ho, ho_r = hob[jj]
                B.tt(ho, pv_[:, O0:M1], hb[:, 4:NM], ALU.add, pair_res(pis[jj]) + [hb_r], [ho_r])
                B.dma(B.qsp, hout[(p * 32 + j) * 128:(p * 32 + j + 1) * 128, :], ho, [ho_r], [hout_rs[p][j]])
                B.act(sq6, ho, AF.Square, [ho_r], [sq6_r])
                B.tt(acc6[:, 4:NM], acc6[:, 4:NM], sq6, ALU.add, [acc6_r, sq6_r], [acc6_r])
        p3 = pair_view(3)
        B.mms([(p3[:, 252:512], onesf, acc6[:, 0:260], True, True), (p3[:, 512:800], onesf, acc6[:, 260:NM], True, True)],
              [onesf_r, acc6_r], pair_res(3))
        B.act(rstd_y, p3[:, M0:M1], AF.Ln, pair_res(3) + [epsc_r], [rstd_y_r], scale=1.0 / D, bias=epsc)
        B.act(rstd_y, rstd_y, AF.Exp, [rstd_y_r], [rstd_y_r], scale=-0.5)
        if stop_after <= 7:
            break

        hin = [B.sb("hin%d" % i, ZX + i * 2240, [128, NO], F32) for i in range(3)]
        yT = [B.sb("yT%d" % i, ZX + 6720 + i * 2240, [128, NO], F32) for i in range(2)]
        ytok = [B.sb("ytok%d" % i, ZX + 11264 + i * 10240, [128, 5, 512], F32) for i in range(2)]
        def load_hin(jx):
            B.dma(B.qsp, hin[jx % 3][0], hout[(p * 32 + jx) * 128:(p * 32 + jx + 1) * 128, :], [hout_rs[p][jx]], [hin[jx % 3][1]])

        load_hin(0)
        load_hin(1)
        for j in range(32):
            if j + 2 < 32:
                load_hin(j + 2)
            hi, hi_r = hin[j % 3]
            yt, yt_r = yT[j % 2]
            B.stt(yt, hi, gfin[:, j:j + 1], rstd_y[:, 4:NM], ALU.mult, ALU.mult, [hi_r, pfm_sr, rstd_y_r], [yt_r])
            b0 = (j % 2) * 2
            pa = PS[:, b0, :].rearrange("p (a c) -> p a c", a=4)
            pb = PS[:, b0 + 1, 0:128]
            ops = [(pa[:, i, :], yt[:, i * 128:(i + 1) * 128], identf) for i in range(4)]
            ops.append((pb[0:32, :], yt[:, 512:544], identf))
            B.transposes(ops, [yt_r, identf_r], [bank[b0], bank[b0 + 1]])
            yk, yk_r = ytok[(j // 4) % 2]
            jc = (j % 4) * 128
            B.act(yk[:, 0:4, jc:jc + 128], pa, AF.Copy, [bank[b0]], [yk_r])
            B.act(yk[0:32, 4, jc:jc + 128], pb[0:32, :], AF.Copy, [bank[b0 + 1]], [yk_r])
            if j % 4 == 3:
                c0 = (j // 4) * 512
                for i in range(4):
                    B.dma(B.qsp, y_p[p * 512 + i * 128:p * 512 + (i + 1) * 128, c0:c0 + 512], yk[:, i, :], [yk_r], [])
                B.dma(B.qsp, y_s[p * 32:(p + 1) * 32, c0:c0 + 512], yk[0:32, 4, :], [yk_r], [])

    fin = B.ACT
    for q in (B.qsp, B.qw):
        for i, (sem, key) in enumerate(q.sems):
            if q.tot[i] > 0 and fin.seen.get(key, 0) < q.tot[i]:
                fin.h.wait_ge(sem, q.tot[i])
                fin.seen[key] = q.tot[i]
    return B


_CACHE = {}


def _rope_tables(pos):
    half = 8
    inv = (500000.0 ** (-np.arange(half, dtype=np.float32) * 2.0 / 16)).astype(np.float32)
    ang = pos.astype(np.float32)[:, None] * inv[None, :]
    return np.cos(ang).astype(np.float32), np.sin(ang).astype(np.float32)


def _const_tables(v):
    pos = np.zeros(TI, np.int64)
    pos[:768] = 512 * v - 240 + np.arange(768)
    for s in range(8):
        for t in range(4):
            pos[768 + 4 * s + t] = 8192 + t
    posc = np.maximum(pos, 0)
    cos, sin = _rope_tables(posc)
    CT = np.ones((128, NM), np.float32)
    ST = np.zeros((128, NM), np.float32)
    for hb in (0, 64):
        CT[hb:hb + 8, :] = cos[M0:M1].T
        CT[hb + 8:hb + 16, :] = cos[M0:M1].T
        ST[hb:hb + 8, :] = -sin[M0:M1].T
        ST[hb + 8:hb + 16, :] = sin[M0:M1].T
    tfm = np.concatenate([CT, ST], axis=1)
    ttok = np.zeros((128, 7, 32), np.float32)
    for i, (r0, nr) in enumerate(TILES):
        ttok[:nr, i, 0:8] = cos[r0:r0 + nr]
        ttok[:nr, i, 8:16] = cos[r0:r0 + nr]
        ttok[:nr, i, 16:24] = -sin[r0:r0 + nr]
        ttok[:nr, i, 24:32] = sin[r0:r0 + nr]
    kval = np.ones((128, 2), np.float32)
    if v == 0:
        kval[:, 0] = 0.0
        kval[:112, 1] = 0.0
    return tfm, ttok.reshape(128, 224), kval


def _c128():
    c = np.zeros((128, 644), np.float32)
    i = np.arange(128)
    c[:, 0:128] = np.eye(128, dtype=np.float32)
    c[:, 128:256] = (i[:, None] <= i[None, :])
    c[:, 256:384] = (i[:, None] > i[None, :])
    perm = np.zeros((128, 128), np.float32)
    for hb in (0, 64):
        for d in range(8):
            perm[hb + d + 8, hb + d] = 1.0
            perm[hb + d, hb + d + 8] = 1.0
    c[:, 384:512] = perm
    c[:, 512:640] = 1.0
    c[:, 640:644] = (i[:, None] > np.arange(4)[None, :])
    maskn = np.zeros((32, 32), np.float32)
    for s2 in range(8):
        for t2 in range(4):
            for t in range(4):
                if t2 <= t:
                    maskn[4 * s2 + t2, 4 * s2 + t] = 1.0
    return c, maskn


def kernel(x_prompt, x_sample, cache_k, cache_v, state_conv, state_ffn_conv, meta_tokens,
           g_mix, w_in, attn_sinks, conv_w, g_attn_out, g_conv_out, w_out, g_ffn,
           w_gate_up, ffn_conv_w, w_down, g_final, _debug=False, _ncores=8, _stop_after=99):
    f32 = np.float32
    x_prompt = np.asarray(x_prompt, f32); x_sample = np.asarray(x_sample, f32)
    ncores = _ncores
    npass = 2
    key = (npass, _debug, _stop_after)
    if key not in _CACHE:
        _CACHE[key] = build_program(npass=npass, debug=_debug, stop_after=_stop_after)
    B = _CACHE[key]

    xp = x_prompt[0]
    xs = x_sample
    fm = lambda a, n: np.ascontiguousarray(np.asarray(a, f32).reshape(n, 128).T)
    pfm = np.zeros((128, NPFM), f32)
    pfm[:, 0:32] = fm(g_mix[0], 32)
    pfm[:, 32:64] = fm(g_ffn[0], 32)
    pfm[:, 64:80] = fm(g_attn_out[0], 16)
    pfm[:, 80:96] = fm(g_conv_out[0], 16)
    cw = np.asarray(conv_w[0], f32)
    pfm[:, 96:144] = np.transpose(cw.reshape(3, 16, 128), (2, 1, 0)).reshape(128, 48)
    fw = np.asarray(ffn_conv_w[0], f32)
    pfm[:, 144:402] = np.transpose(fw.reshape(3, NFT, 128), (2, 1, 0)).reshape(128, 258)
    pfm[:, 402:434] = fm(g_final, 32)
    c128, maskn = _c128()
    w_in2 = np.asarray(w_in[0], f32); w_out2 = np.asarray(w_out[0], f32)
    w_gu2 = np.asarray(w_gate_up[0], f32); w_dn2 = np.asarray(w_down[0], f32)
    ck_all = np.asarray(cache_k[0], f32).reshape(128, 128, 256)
    cv_all = np.asarray(cache_v[0], f32).reshape(128, 128, 256)
    sc_all = np.asarray(state_conv[0], f32)
    sf_all = np.asarray(state_ffn_conv[0], f32)
    meta = np.asarray(meta_tokens, f32)

    in_maps = []
    for c in range(ncores):
        xh = np.zeros((npass, TI, D), f32)
        tfm_l, ttok_l, kval_l = [], [], []
        for p in range(npass):
            v = 2 * c + p
            if v == 0:
                xh[p, 240:256] = meta
            else:
                xh[p, 0:256] = xp[512 * v - 256:512 * v]
            xh[p, 256:768] = xp[512 * v:512 * v + 512]
            xh[p, 768:800] = xs[8 * v:8 * v + 8].reshape(32, D)
            a, b, k = _const_tables(v)
            tfm_l.append(a); ttok_l.append(b); kval_l.append(k)
        m = {
            "xh": xh.reshape(npass * TI, D),
            "ck": ck_all[16 * c:16 * c + 16].reshape(16 * 128, 256),
            "cv": cv_all[16 * c:16 * c + 16].reshape(16 * 128, 256),
            "sc": sc_all[16 * c:16 * c + 16].reshape(32, 2048),
            "sf": sf_all[16 * c:16 * c + 16].reshape(32, DFF),
            "w_in": w_in2, "w_out": w_out2, "w_gu": w_gu2, "w_dn": w_dn2,
            "pfm": pfm, "sinks": np.asarray(attn_sinks, f32).reshape(1, 32),
            "c128": c128, "maskn": maskn,
            "tfm": np.concatenate(tfm_l, 0), "ttok": np.concatenate(ttok_l, 0), "kval": np.concatenate(kval_l, 0),
        }
        in_maps.append(m)
    res = run_bass_kernel_spmd(B.nc, in_maps, core_ids=list(range(ncores)))
    R = res.results
    if _debug:
        return R
    y_prompt = np.concatenate([R[c]["y_p"] for c in range(ncores)], 0).reshape(1, ncores * 1024, D)
    y_sample = np.concatenate([R[c]["y_s"] for c in range(ncores)], 0).reshape(ncores * 16, 4, D)
    last = R[ncores - 1]
    nk_p = last["nk_p"][128:256].reshape(1, 1, 128, 4, 64)
    nv_p = last["nv_p"][128:256].reshape(1, 1, 128, 4, 64)
    nc_p = last["nc_p"][2:4].reshape(1, 1, 2, 2048)
    nf_p = last["nf_p"][2:4].reshape(1, 1, 2, DFF)
    nk_s = np.concatenate([R[c]["nk_s"] for c in range(ncores)], 0).reshape(1, ncores * 16, 128, 4, 64)
    nv_s = np.concatenate([R[c]["nv_s"] for c in range(ncores)], 0).reshape(1, ncores * 16, 128, 4, 64)
    nc_s = np.concatenate([R[c]["nc_s"] for c in range(ncores)], 0).reshape(1, ncores * 16, 2, 2048)
    nf_s = np.concatenate([R[c]["nf_s"] for c in range(ncores)], 0).reshape(1, ncores * 16, 2, DFF)
    return (y_prompt.astype(f32), y_sample.astype(f32), nk_p.astype(f32), nv_p.astype(f32), nc_p.astype(f32),
            nf_p.astype(f32), nk_s.astype(f32), nv_s.astype(f32), nc_s.astype(f32), nf_s.astype(f32))
```

```python
import numpy as np
import concourse.bass as bass
import concourse.mybir as mybir
from concourse.bass_utils import run_bass_kernel_spmd

F32, BF16, U8 = mybir.dt.float32, mybir.dt.bfloat16, mybir.dt.uint8
AF = mybir.ActivationFunctionType
ALU = mybir.AluOpType

D = 4096
DFF = 11008
NFT = 86
TI = 800
M0, M1 = 252, 800
NM = M1 - M0
O0 = 256
NO = M1 - O0
TILES = [(0, 128), (128, 128), (256, 128), (384, 128), (512, 128), (640, 128), (768, 32)]
MP = [(252, 4), (256, 128), (384, 128), (512, 128), (640, 128), (768, 32)]
OP = [(256, 128), (384, 128), (512, 128), (640, 128), (768, 32)]
EPS = 1e-5
NPFM = 32 + 32 + 16 + 16 + 48 + 258 + 32

G0 = 0
S0 = 6912
ZX = 32768
ZC = 126464
ZR = 161792
ARENA = 210944


class Res:
    __slots__ = ("name", "space", "lo", "hi", "w", "r", "al")

    def __init__(self, name, space, lo, hi):
        self.name, self.space, self.lo, self.hi = name, space, lo, hi
        self.w = {}
        self.r = {}
        self.al = [self]


class Eng:
    def __init__(self, name, h, sem, key):
        self.name, self.h, self.sem, self.key = name, h, sem, key
        self.cnt = 0
        self.seen = {}


class DQ:
    def __init__(self, eng, sems):
        self.eng = eng
        self.sems = sems
        self.tot = [0] * len(sems)
        self.n = 0


class Builder:
    def __init__(self, debug=False):
        self.debug = debug
        nc = bass.Bass("TRN2", target_bir_lowering=False)
        self.nc = nc
        self.res = []
        self.semh = {}
        self.nkey = 0
        self.PE = self.mk_eng("pe", nc.tensor)
        self.ACT = self.mk_eng("act", nc.scalar)
        self.DVE = self.mk_eng("dve", nc.vector)
        self.POOL = self.mk_eng("pool", nc.gpsimd)
        self.SP = Eng("sp", nc.sync, None, -1)
        self.qsp = DQ(self.SP, [self.mk_sem("dsp%d" % i) for i in range(16)])
        self.qw = DQ(self.POOL, [self.mk_sem("dw%d" % i) for i in range(NSLOT)])
        self.arena = nc.alloc_sbuf_tensor("arena", [128, ARENA], U8)
        self.psum = nc.alloc_psum_tensor("psum", [128, 8, 512], F32)
        self.bank = [self.new_res("bank%d" % b, "psum", b * 2048, (b + 1) * 2048) for b in range(8)]
        self.dbg_outs = []

    def mk_sem(self, name):
        s = self.nc.alloc_semaphore(name)
        k = self.nkey
        self.nkey += 1
        self.semh[k] = s
        return (s, k)

    def mk_eng(self, name, h):
        s, k = self.mk_sem("s_" + name)
        return Eng(name, h, s, k)

    def new_res(self, name, space, lo, hi):
        r = Res(name, space, lo, hi)
        for o in self.res:
            if o.space == space and o.lo < hi and lo < o.hi:
                o.al.append(r)
                r.al.append(o)
        self.res.append(r)
        return r

    def sb(self, name, off, shape, dt, nres=0):
        n = 1
        for s in shape[1:]:
            n *= s
        esz = 4 if dt == F32 else 2
        nb = n * esz
        assert off % 4 == 0 and off + nb <= ARENA, (name, off, nb)
        v = self.arena[0:shape[0], off:off + nb].bitcast(dt)
        if len(shape) == 3:
            v = v.rearrange("p (a b) -> p a b", a=shape[1])
        elif len(shape) == 4:
            v = v.rearrange("p (a b c) -> p a b c", a=shape[1], b=shape[2])
        if nres:
            per = nb // nres
            rs = [self.new_res("%s%d" % (name, i), "sbuf", off + i * per, off + (i + 1) * per) for i in range(nres)]
            return v, rs
        return v, self.new_res(name, "sbuf", off, off + nb)

    def dres(self, name):
        return self.new_res(name, "d:" + name, 0, 1)

    def sync(self, eng, reads, writes):
        raw = {}
        oth = {}
        for r in reads:
            for a in r.al:
                for k, v in a.w.items():
                    if v > raw.get(k, 0):
                        raw[k] = v
        for w in writes:
            for a in w.al:
                for k, v in a.w.items():
                    if v > oth.get(k, 0):
                        oth[k] = v
                for k, v in a.r.items():
                    if v > oth.get(k, 0):
                        oth[k] = v
        need = {}
        for k, v in raw.items():
            if k == eng.key and eng.name == "pe":
                continue
            need[k] = v
        for k, v in oth.items():
            if k == eng.key:
                continue
            if v > need.get(k, 0):
                need[k] = v
        for k, v in need.items():
            if eng.seen.get(k, 0) < v:
                eng.h.wait_ge(self.semh[k], v)
                eng.seen[k] = v

    def mark(self, key, val, reads, writes):
        for r in reads:
            if r.r.get(key, 0) < val:
                r.r[key] = val
        for w in writes:
            w.w = {key: val}
            w.r = {}

    def done(self, eng, ins, reads, writes):
        ins.then_inc(eng.sem, 1)
        eng.cnt += 1
        self.mark(eng.key, eng.cnt, reads, writes)

    def dma(self, q, out, in_, reads, writes):
        i = q.n % len(q.sems)
        q.n += 1
        sem, key = q.sems[i]
        eng = q.eng
        if q.tot[i] > 0 and eng.seen.get(key, 0) < q.tot[i]:
            eng.h.wait_ge(sem, q.tot[i])
            eng.seen[key] = q.tot[i]
        self.sync(eng, reads, writes)
        ins = eng.h.dma_start(out=out, in_=in_, max_dma_last_dim=16384)
        ins.then_inc(sem, 16)
        q.tot[i] += 16
        self.mark(key, q.tot[i], reads, writes)

    def act(self, out, in_, func, reads, writes, scale=1.0, bias=None, accum=None):
        self.sync(self.ACT, reads, writes)
        kw = {}
        if bias is not None:
            kw["bias"] = bias
        if accum is not None:
            kw["accum_out"] = accum
        ins = self.nc.scalar.activation(out=out, in_=in_, func=func, scale=scale, **kw)
        self.done(self.ACT, ins, reads, writes)

    def tt(self, out, in0, in1, op, reads, writes, eng=None):
        eng = eng or self.DVE
        self.sync(eng, reads, writes)
        ins = eng.h.tensor_tensor(out=out, in0=in0, in1=in1, op=op)
        self.done(eng, ins, reads, writes)

    def ts(self, out, in0, s1, op0, reads, writes, s2=None, op1=None, eng=None):
        eng = eng or self.DVE
        self.sync(eng, reads, writes)
        if op1 is None:
            ins = eng.h.tensor_scalar(out=out, in0=in0, scalar1=s1, scalar2=None, op0=op0)
        else:
            ins = eng.h.tensor_scalar(out=out, in0=in0, scalar1=s1, scalar2=s2, op0=op0, op1=op1)
        self.done(eng, ins, reads, writes)

    def stt(self, out, in0, scalar, in1, op0, op1, reads, writes):
        self.sync(self.DVE, reads, writes)
        ins = self.nc.vector.scalar_tensor_tensor(out=out, in0=in0, scalar=scalar, in1=in1, op0=op0, op1=op1)
        self.done(self.DVE, ins, reads, writes)

    def recip(self, out, in_, reads, writes, fast=False):
        if fast:
            self.act(out, in_, AF.Ln, reads, writes)
            self.act(out, out, AF.Exp, writes, writes, scale=-1.0)
            return
        self.sync(self.DVE, reads, writes)
        ins = self.nc.vector.reciprocal(out=out, in_=in_)
        self.done(self.DVE, ins, reads, writes)

    def copy(self, eng, out, in_, reads, writes):
        self.sync(eng, reads, writes)
        ins = eng.h.tensor_copy(out=out, in_=in_)
        self.done(eng, ins, reads, writes)

    def memset(self, eng, ap, val, writes):
        self.sync(eng, [], writes)
        ins = eng.h.memset(ap, val)
        self.done(eng, ins, [], writes)

    def mms(self, ops, reads, writes):
        self.sync(self.PE, reads, writes)
        ins = None
        for (o, l, r, st, sp) in ops:
            ins = self.nc.tensor.matmul(o, l, r, start=st, stop=sp)
        self.done(self.PE, ins, reads, writes)

    def transposes(self, ops, reads, writes):
        self.sync(self.PE, reads, writes)
        ins = None
        for (o, i, ident) in ops:
            ins = self.nc.tensor.transpose(o, i, ident)
        self.done(self.PE, ins, reads, writes)

    def tap(self, name, view, res, shape):
        if not self.debug:
            return
        d = self.nc.dram_tensor("dbg_" + name, list(shape), view.dtype, kind="ExternalOutput").ap()
        r = self.dres("dbg_" + name)
        n = shape[1]
        for c0 in range(0, n, 4096):
            c1 = min(n, c0 + 4096)
            self.dma(self.qsp, d[:, c0:c1], view[:, c0:c1], [res] if not isinstance(res, list) else res, [])
        self.dbg_outs.append(("dbg_" + name, r))


NSLOT = 6


class WStream:
    def __init__(self, B):
        self.B = B
        self.slabs = []
        self.issued = 0
        self.taken = 0
        self.donec = 0
        self.slots = [B.sb("ring%d" % s, ZR + s * 8192, [128, 16, 256], BF16) for s in range(NSLOT)]
        self.wres = {}

    def add(self, W, wname, k0, nk, c0):
        src = W[k0 * 128:(k0 + nk) * 128, c0:c0 + 256].rearrange("(k p) c -> p k c", p=128)
        v, r = self.slots[len(self.slabs) % NSLOT]
        if wname not in self.wres:
            self.wres[wname] = self.B.dres(wname)
        self.slabs.append((src, v[:, 0:nk, :], r, self.wres[wname]))

    def fill(self):
        while self.issued < len(self.slabs) and self.issued < self.donec + NSLOT:
            src, v, r, wr = self.slabs[self.issued]
            self.B.dma(self.B.qw, v, src, [wr], [r])
            self.issued += 1

    def take(self):
        assert self.taken < self.issued, (self.taken, self.issued)
        src, v, r, wr = self.slabs[self.taken]
        self.taken += 1
        return v, r

    def done(self):
        self.donec += 1
        self.fill()


def build_program(npass=2, debug=False, stop_after=99):
    B = Builder(debug)
    nc = B.nc

    def din(name, shape):
        return nc.dram_tensor(name, list(shape), F32, kind="ExternalInput").ap(), B.dres(name)

    def dout(name, shape, kind="ExternalOutput"):
        return nc.dram_tensor(name, list(shape), F32, kind=kind).ap(), B.dres(name)

    xh, xh_r = din("xh", [npass * TI, D])
    ck, ck_r = din("ck", [npass * 8 * 128, 256])
    cv, cv_r = din("cv", [npass * 8 * 128, 256])
    sc, sc_r = din("sc", [npass * 16, 2048])
    sf, sf_r = din("sf", [npass * 16, DFF])
    w_in, _ = din("w_in", [D, 8704])
    w_out, _ = din("w_out", [D, D])
    w_gu, _ = din("w_gu", [D, 2 * DFF])
    w_dn, _ = din("w_dn", [DFF, D])
    pfm_d, pfm_r = din("pfm", [128, NPFM])
    sinks_d, sinks_r = din("sinks", [1, 32])
    c128_d, c128_r = din("c128", [128, 644])
    maskn_d, maskn_r = din("maskn", [32, 32])
    tfm_d, tfm_r = din("tfm", [npass * 128, 2 * NM])
    ttok_d, ttok_r = din("ttok", [npass * 128, 7 * 32])
    kval_d, kval_r = din("kval", [npass * 128, 2])

    y_p, y_p_r = dout("y_p", [npass * 512, D])
    y_s, y_s_r = dout("y_s", [npass * 32, D])
    nk_p, nk_p_r = dout("nk_p", [npass * 128, 256])
    nv_p, nv_p_r = dout("nv_p", [npass * 128, 256])
    nc_p, nc_p_r = dout("nc_p", [npass * 2, 2048])
    nf_p, nf_p_r = dout("nf_p", [npass * 2, DFF])
    nk_s, nk_s_r = dout("nk_s", [npass * 8 * 128, 256])
    nv_s, nv_s_r = dout("nv_s", [npass * 8 * 128, 256])
    nc_s, nc_s_r = dout("nc_s", [npass * 16, 2048])
    nf_s, nf_s_r = dout("nf_s", [npass * 16, DFF])
    hmid, _ = dout("hmid", [npass * 32 * 128, NM], kind="Internal")
    hout, _ = dout("hout", [npass * 32 * 128, NO], kind="Internal")
    hmid_rs = [[B.dres("hmid%d_%d" % (p, j)) for j in range(32)] for p in range(npass)]
    hout_rs = [[B.dres("hout%d_%d" % (p, j)) for j in range(32)] for p in range(npass)]
    out_res = [y_p_r, y_s_r, nk_p_r, nv_p_r, nc_p_r, nf_p_r, nk_s_r, nv_s_r, nc_s_r, nf_s_r]

    PS = B.psum
    bank = B.bank

    o = G0
    identf, identf_r = B.sb("identf", o, [128, 128], F32); o += 512
    onesf, onesf_r = B.sb("onesf", o, [128, 128], F32); o += 512
    identb, identb_r = B.sb("identb", o, [128, 128], BF16); o += 256
    onesb, onesb_r = B.sb("onesb", o, [128, 128], BF16); o += 256
    permb, permb_r = B.sb("permb", o, [128, 128], BF16); o += 256
    maskO, maskO_r = B.sb("maskO", o, [128, 128], BF16); o += 256
    maskP, maskP_r = B.sb("maskP", o, [128, 128], BF16); o += 256
    maskc, maskc_r = B.sb("maskc", o, [128, 4], BF16); o += 64
    masknb, masknb_r = B.sb("masknb", o, [128, 32], BF16); o += 64
    pfm, pfm_sr = B.sb("pfm", o, [128, NPFM], F32); o += 1792
    sinkexp, sinkexp_r = B.sb("sinkexp", o, [128, 32], F32); o += 128
    epsc, epsc_r = B.sb("epsc", o, [128, 1], F32); o += 64
    kvalid, kvalid_r = B.sb("kvalid", o, [128, 2], F32); o += 64
    ssv, ssv_r = B.sb("ssv", o, [128, 8], F32); o += 64
    rsv, rsv_r = B.sb("rsv", o, [128, 8], F32); o += 64
    rstd_f, rstd_f_r = B.sb("rstd_f", o, [128, NM], F32); o += 2240
    assert o <= S0, o
    gmix = pfm[:, 0:32]
    gffn = pfm[:, 32:64]
    gattn = pfm[:, 64:80]
    gconv = pfm[:, 80:96]
    convw = pfm[:, 96:144].rearrange("p (j i) -> p j i", i=3)
    fconvw = pfm[:, 144:402].rearrange("p (j i) -> p j i", i=3)
    gfin = pfm[:, 402:434]

    WS = WStream(B)

    def add_pair(W, wname, nkt, c0):
        k0 = 0
        while k0 < nkt:
            nk = min(16, nkt - k0)
            WS.add(W, wname, k0, nk, c0)
            k0 += nk

    for p in range(npass):
        add_pair(w_in, "w_in", 32, 2048)
        add_pair(w_in, "w_in", 32, 2304)
        for s in range(8):
            add_pair(w_in, "w_in", 32, 256 * s)
        for sg in range(8):
            add_pair(w_in, "w_in", 32, 2560 + 256 * sg)
            add_pair(w_in, "w_in", 32, 4608 + 256 * sg)
            add_pair(w_in, "w_in", 32, 6656 + 256 * sg)
        for s in range(16):
            add_pair(w_out, "w_out", 32, 256 * s)
        for s in range(43):
            add_pair(w_gu, "w_gu", 32, 256 * s)
            add_pair(w_gu, "w_gu", 32, DFF + 256 * s)
        for s in range(16):
            add_pair(w_dn, "w_dn", NFT, 256 * s)

    def pair_view(pi):
        return PS[:, 2 * pi:2 * pi + 2, :].rearrange("p a b -> p (a b)")

    def pair_res(pi):
        return [bank[2 * pi], bank[2 * pi + 1]]

    cst, cst_r = B.sb("cst", ZX, [128, 644], F32)
    B.dma(B.qsp, cst, c128_d, [c128_r], [cst_r])
    B.copy(B.DVE, identf, cst[:, 0:128], [cst_r], [identf_r])
    B.copy(B.DVE, identb, cst[:, 0:128], [cst_r], [identb_r])
    B.copy(B.DVE, maskO, cst[:, 128:256], [cst_r], [maskO_r])
    B.copy(B.DVE, maskP, cst[:, 256:384], [cst_r], [maskP_r])
    B.copy(B.DVE, permb, cst[:, 384:512], [cst_r], [permb_r])
    B.copy(B.DVE, onesb, cst[:, 512:640], [cst_r], [onesb_r])
    B.copy(B.DVE, onesf, cst[:, 512:640], [cst_r], [onesf_r])
    B.copy(B.DVE, maskc, cst[:, 640:644], [cst_r], [maskc_r])
    cst2, cst2_r = B.sb("cst2", ZX + 4096, [128, 32], F32)
    B.dma(B.qsp, cst2[0:32, :], maskn_d, [maskn_r], [cst2_r])
    B.copy(B.DVE, masknb[0:32, :], cst2[0:32, :], [cst2_r], [masknb_r])
    B.dma(B.qsp, pfm, pfm_d, [pfm_r], [pfm_sr])
    B.dma(B.qsp, cst2, sinks_d[0, :].partition_broadcast(128), [sinks_r], [cst2_r])
    B.act(sinkexp, cst2, AF.Exp, [cst2_r], [sinkexp_r])
    B.memset(B.DVE, epsc, EPS, [epsc_r])

    WS.fill()

    for p in range(npass):
        E0 = ZX + 51200 + 17664
        o = E0
        CT, CT_r = B.sb("CT", o, [128, NM], F32); o += 2240
        ST, ST_r = B.sb("ST", o, [128, NM], F32); o += 2240
        ttok, ttok_r2 = B.sb("ttok", o, [128, 7, 32], F32); o += 896
        acc, acc_r = B.sb("acc", o, [128, NM], F32); o += 2240
        rstd_t, rstd_t_r = B.sb("rstd_t", o, [128, NM], F32); o += 2240
        cstT, cstT_r = B.sb("cstT", o, [128, 16, 16], F32); o += 1024
        sqt, sqt_r = B.sb("sqt", o, [128, NM], F32); o += 2240
        scf, scf_r = B.sb("scf", o, [16, 2048], F32); o += 8192
        assert o <= ZC, o
        B.dma(B.qsp, CT, tfm_d[p * 128:(p + 1) * 128, 0:NM], [tfm_r], [CT_r])
        B.dma(B.qsp, ST, tfm_d[p * 128:(p + 1) * 128, NM:2 * NM], [tfm_r], [ST_r])
        B.dma(B.qsp, ttok.rearrange("p a b -> p (a b)"), ttok_d[p * 128:(p + 1) * 128, :], [ttok_r], [ttok_r2])
        B.dma(B.qsp, kvalid, kval_d[p * 128:(p + 1) * 128, :], [kval_r], [kvalid_r])

        if stop_after <= 0:
            break
        aT, aT_r = B.sb("aT", ZX, [128, 32, TI], BF16, nres=32)
        convT, convT_r = B.sb("convT", ZX + 51200, [128, 16, NM], BF16, nres=16)
        qT, qT_r = B.sb("qT", ZC, [128, 16, NM], BF16, nres=16)
        KT, KT_r = B.sb("KT", ZC + 17536, [128, 4, TI], BF16)
        Vd, Vd_r = B.sb("Vd", ZC + 17536 + 6400, [128, 7, 4, 128], BF16, nres=7)

        xin = [B.sb("xin%d" % i, ZC + i * 16384, [128, D], F32) for i in range(2)]
        xb = [B.sb("xb%d" % i, ZX + 51200 + i * 8192, [128, D], BF16) for i in range(2)]
        for i, (r0, nr) in enumerate(TILES):
            xi, xi_r = xin[i % 2]
            xbb, xbb_r = xb[i % 2]
            B.dma(B.qsp, xi[:nr, :], xh[p * TI + r0:p * TI + r0 + nr, :], [xh_r], [xi_r])
            import os
            cut = int(os.environ.get("K1CUT", "9"))
            if cut <= 0:
                continue
            B.act(xbb[:nr, :], xi[:nr, :], AF.Square, [xi_r], [xbb_r, ssv_r], accum=ssv[:nr, i:i + 1])
            if cut <= 1:
                continue
            B.act(rsv[:nr, i:i + 1], ssv[:nr, i:i + 1], AF.Ln, [ssv_r, epsc_r], [rsv_r], scale=1.0 / D, bias=epsc[:nr, :])
            B.act(rsv[:nr, i:i + 1], rsv[:nr, i:i + 1], AF.Exp, [rsv_r], [rsv_r], scale=-0.5)
            B.act(xbb[:nr, :], xi[:nr, :], AF.Copy, [xi_r, rsv_r], [xbb_r], scale=rsv[:nr, i:i + 1])
            if cut <= 2:
                continue
            for kg in range(4):
                b = (i * 4 + kg) % 6
                pt = PS[:, b, :].bitcast(BF16).rearrange("p (a c) -> p a c", a=8)
                ops = []
                for kk in range(8):
                    k = kg * 8 + kk
                    ops.append((pt[:, kk, 0:nr], xbb[:nr, k * 128:(k + 1) * 128], identb[:nr, :nr]))
                B.transposes(ops, [xbb_r, identb_r], [bank[b]])
                if cut <= 3:
                    continue
                B.tt(aT[:, kg * 8:(kg + 1) * 8, r0:r0 + nr], pt[:, :, 0:nr],
                     gmix[:, kg * 8:(kg + 1) * 8].unsqueeze(2).to_broadcast([128, 8, nr]), ALU.mult,
                     [bank[b], pfm_sr], aT_r[kg * 8:(kg + 1) * 8])
        if p == 0 and cut >= 9:
            B.tap("aT", aT.rearrange("p a b -> p (a b)"), aT_r, [128, 32 * TI])
        if stop_after <= 1:
            break

        o = S0
        Kf, Kf_r = B.sb("Kf", o, [128, 256], F32); o += 1024
        Vf, Vf_r = B.sb("Vf", o, [128, 256], F32); o += 1024
        Kd, Kd_r = B.sb("Kd", o, [128, 4, 128], BF16); o += 1024
        rA, rA_r = B.sb("rA", o, [128, 4, 16], F32); o += 256
        rB, rB_r = B.sb("rB", o, [128, 4, 16], F32); o += 256
        S_after_kv = o
        wKs = [WS.take(), WS.take()]
        wVs = [WS.take(), WS.take()]
        wK_rs = [r for (_, r) in wKs]
        wV_rs = [r for (_, r) in wVs]
        for i, (r0, nr) in enumerate(TILES):
            pk = PS[:, 6, 0:256]
            pv = PS[:, 7, 0:256]
            ops = [(pk[:nr, :], aT[:, k, r0:r0 + nr], wKs[k // 16][0][:, k % 16, :], k == 0, k == 31) for k in range(32)]
            B.mms(ops, aT_r + wK_rs, [bank[6]])
            ops = [(pv[:nr, :], aT[:, k, r0:r0 + nr], wVs[k // 16][0][:, k % 16, :], k == 0, k == 31) for k in range(32)]
            B.mms(ops, aT_r + wV_rs, [bank[7]])
            pv3 = pv[:nr, :].rearrange("p (g d) -> p g d", g=4)
            B.act(Vd[:nr, i, :, 0:64], pv3, AF.Copy, [bank[7]], [Vd_r[i]])
            B.act(Vd[:nr, i, :, 64:128], pv3, AF.Copy, [bank[7]], [Vd_r[i]])
            if i >= 5:
                B.act(Vf[:nr, :], pv[:nr, :], AF.Copy, [bank[7]], [Vf_r])
            B.act(Kf[:nr, :], pk[:nr, :], AF.Copy, [bank[6]], [Kf_r])
            K3 = Kf[:nr, :].rearrange("p (g d) -> p g d", g=4)
            cc_b = ttok[:nr, i, 0:16].unsqueeze(1).to_broadcast([nr, 4, 16])
            ss_b = ttok[:nr, i, 16:32].unsqueeze(1).to_broadcast([nr, 4, 16])
            B.tt(rA[:nr], K3[:, :, 0:16], cc_b, ALU.mult, [Kf_r, ttok_r2], [rA_r])
            B.tt(rB[:nr, :, 0:8], K3[:, :, 8:16], ss_b[:, :, 0:8], ALU.mult, [Kf_r, ttok_r2], [rB_r])
            B.tt(rB[:nr, :, 8:16], K3[:, :, 0:8], ss_b[:, :, 8:16], ALU.mult, [Kf_r, ttok_r2], [rB_r])
            B.tt(K3[:, :, 0:16], rA[:nr], rB[:nr], ALU.add, [rA_r, rB_r], [Kf_r])
            B.copy(B.DVE, Kd[:nr, :, 0:64], K3, [Kf_r], [Kd_r])
            B.copy(B.DVE, Kd[:nr, :, 64:128], K3, [Kf_r], [Kd_r])
            pT = PS[:, 6, :].bitcast(BF16)[:, 0:512].rearrange("p (g c) -> p g c", g=4)
            ops = [(pT[:, g, 0:nr], Kd[:nr, g, :], identb[:nr, :nr]) for g in range(4)]
            B.transposes(ops, [Kd_r, identb_r], [bank[6]])
            B.act(KT[:, :, r0:r0 + nr], pT[:, :, 0:nr], AF.Copy, [bank[6]], [KT_r])
            if i == 5:
                B.dma(B.qsp, nk_p[p * 128:(p + 1) * 128, :], Kf, [Kf_r], [])
                B.dma(B.qsp, nv_p[p * 128:(p + 1) * 128, :], Vf, [Vf_r], [])
            if i == 6:
                nk3 = nk_s.rearrange("(s w) c -> s w c", w=128)
                nv3 = nv_s.rearrange("(s w) c -> s w c", w=128)
                ck3 = ck.rearrange("(s w) c -> s w c", w=128)
                cv3 = cv.rearrange("(s w) c -> s w c", w=128)
                B.dma(B.qsp, nk3[p * 8:(p + 1) * 8, 124:128, :], Kf[0:32, :], [Kf_r], [])
                B.dma(B.qsp, nv3[p * 8:(p + 1) * 8, 124:128, :], Vf[0:32, :], [Vf_r], [])
                B.dma(B.qsp, nk3[p * 8:(p + 1) * 8, 0:124, :], ck3[p * 8:(p + 1) * 8, 4:128, :], [ck_r], [])
                B.dma(B.qsp, nv3[p * 8:(p + 1) * 8, 0:124, :], cv3[p * 8:(p + 1) * 8, 4:128, :], [cv_r], [])
        for _ in range(4):
            WS.done()
        if p == 0:
            B.tap("KT", KT.rearrange("p a b -> p (a b)"), KT_r, [128, 4 * TI])
            B.tap("Vd", Vd.rearrange("p a b c -> p (a b c)"), Vd_r, [128, 7 * 4 * 128])

        o = S_after_kv
        qraw = [B.sb("qraw%d" % i, o + i * 1152, [128, NM], BF16) for i in range(2)]; o += 2304
        m1 = [B.sb("m1_%d" % i, o + i * 2240, [128, NM], F32) for i in range(2)]; o += 4480
        m2 = [B.sb("m2_%d" % i, o + i * 2240, [128, NM], F32) for i in range(2)]; o += 4480
        S_after_q = o

        def pair_mm(nkt, rhs_of_k, rhs_res, c0=M0):
            pis = [next_pair(), next_pair()]
            k0 = 0
            while k0 < nkt:
                nk = min(16, nkt - k0)
                w, w_r = WS.take()
                for jj in range(2):
                    pv_ = pair_view(pis[jj])
                    ops = []
                    for kk in range(nk):
                        k = k0 + kk
                        ops.append((pv_[:, c0:512], w[:, kk, jj * 128:(jj + 1) * 128], rhs_of_k(k, c0, 512), k == 0, k == nkt - 1))
                    for kk in range(nk):
                        k = k0 + kk
                        ops.append((pv_[:, 512:800], w[:, kk, jj * 128:(jj + 1) * 128], rhs_of_k(k, 512, 800), k == 0, k == nkt - 1))
                    B.mms(ops, rhs_res[k0:k0 + nk] + [w_r], pair_res(pis[jj]))
                WS.done()
                k0 += nk
            return pis

        rot = [0]

        def next_pair():
            pi = rot[0] % 3
            rot[0] += 1
            return pi

        for s in range(8):
            pis = pair_mm(32, lambda k, a, b: aT[:, k, a:b], aT_r)
            for jj in range(2):
                j = 2 * s + jj
                pi = pis[jj]
                pm = pair_view(pi)[:, M0:M1]
                qr, qr_r = qraw[j % 2]
                B.act(qr, pm, AF.Copy, pair_res(pi), [qr_r])
                p3 = pair_view(3)
                B.mms([(p3[:, 252:512], permb, qr[:, 0:260], True, True), (p3[:, 512:800], permb, qr[:, 260:NM], True, True)],
                      [permb_r, qr_r], pair_res(3))
                a1, a1_r = m1[j % 2]
                a2, a2_r = m2[j % 2]
                B.tt(a1, p3[:, M0:M1], ST, ALU.mult, pair_res(3) + [ST_r], [a1_r])
                B.tt(a2, qr, CT, ALU.mult, [qr_r, CT_r], [a2_r])
                B.tt(qT[:, j, :], a1, a2, ALU.add, [a1_r, a2_r], [qT_r[j]])
        if p == 0:
            B.tap("qT", qT.rearrange("p a b -> p (a b)"), qT_r, [128, 16 * NM])
        if stop_after <= 2:
            break

        B.dma(B.qsp, scf, sc[p * 16:(p + 1) * 16, :], [sc_r], [scf_r])
        for jg in range(4):
            pt = PS[:, 6, 0:64].rearrange("p (a c) -> p a c", a=4)
            ops = [(pt[:, jj, :], scf[0:16, (jg * 4 + jj) * 128:(jg * 4 + jj + 1) * 128], identf[0:16, 0:16]) for jj in range(4)]
            B.transposes(ops, [scf_r, identf_r], [bank[6]])
            B.act(cstT[:, jg * 4:(jg + 1) * 4, :], pt, AF.Copy, [bank[6]], [cstT_r])
        o = S_after_kv
        cbS = [B.sb("cbS%d" % i, o + i * 2240, [128, NM], F32) for i in range(2)]; o += 4480
        ccS = [B.sb("ccS%d" % i, o + i * 2240, [128, NM], F32) for i in range(2)]; o += 4480
        ub = [B.sb("ub%d" % i, o + i * 2240, [128, NM], F32) for i in range(2)]; o += 4480
        yb, yb_r = B.sb("yb", o, [128, NM], F32); o += 2240
        ob, ob_r = B.sb("ob", o, [128, NM], F32); o += 2240
        ext, ext_r = B.sb("ext", o, [128, 8, 6], F32); o += 192
        ysb, ysb_r = B.sb("ysb", o, [128, 8, 4], F32); o += 128
        stg, stg_r = B.sb("stg", o, [128, 2, 18], F32); o += 192
        stO = [B.sb("stO%d" % i, o + i * 1024, [18, 256], F32) for i in range(2)]; o += 2048
        assert o <= ZX, o
        B.memset(B.DVE, ob[:, 0:2], 0.0, [ob_r])
        B.memset(B.DVE, acc, 0.0, [acc_r])
        def flush_stg(sg_):
            pt = PS[:, 7, 0:256].rearrange("p (a c) -> p a c", a=2)
            B.transposes([(pt[0:18, jj, :], stg[:, jj, :], identf) for jj in range(2)], [stg_r, identf_r], [bank[7]])
            so, so_r = stO[sg_ % 2]
            B.act(so.rearrange("p (a c) -> p a c", a=2), pt[0:18], AF.Copy, [bank[7]], [so_r])
            B.dma(B.qsp, nc_p[p * 2:(p + 1) * 2, sg_ * 256:(sg_ + 1) * 256], so[0:2, :], [so_r], [])
            B.dma(B.qsp, nc_s[p * 16:(p + 1) * 16, sg_ * 256:(sg_ + 1) * 256], so[2:18, :], [so_r], [])

        for sg in range(8):
            pis = pair_mm(32, lambda k, a, b: aT[:, k, a:b], aT_r)
            if sg > 0:
                flush_stg(sg - 1)
            for jj in range(2):
                B.act(cbS[jj][0], pair_view(pis[jj])[:, M0:M1], AF.Copy, pair_res(pis[jj]), [cbS[jj][1]])
            pis = pair_mm(32, lambda k, a, b: aT[:, k, a:b], aT_r)
            for jj in range(2):
                B.act(ccS[jj][0], pair_view(pis[jj])[:, M0:M1], AF.Copy, pair_res(pis[jj]), [ccS[jj][1]])
            pis = pair_mm(32, lambda k, a, b: aT[:, k, a:b], aT_r)
            for jj in range(2):
                j = 2 * sg + jj
                pi = pis[jj]
                pm = pair_view(pi)[:, M0:M1]
                u, u_r = ub[jj]
                B.tt(u, pm, ccS[jj][0], ALU.mult, pair_res(pi) + [ccS[jj][1]], [u_r])
                B.act(yb[:, 2:516], u[:, 0:514], AF.Copy, [u_r, pfm_sr], [yb_r], scale=convw[:, j, 0:1])
                B.stt(yb[:, 2:516], u[:, 1:515], convw[:, j, 1:2], yb[:, 2:516], ALU.mult, ALU.add, [u_r, pfm_sr, yb_r], [yb_r])
                B.stt(yb[:, 2:516], u[:, 2:516], convw[:, j, 2:3], yb[:, 2:516], ALU.mult, ALU.add, [u_r, pfm_sr, yb_r], [yb_r])
                B.copy(B.DVE, ext[:, :, 0:2], cstT[:, j, :].rearrange("p (s r) -> p s r", r=2), [cstT_r], [ext_r])
                B.copy(B.DVE, ext[:, :, 2:6], u[:, 516:548].rearrange("p (s t) -> p s t", t=4), [u_r], [ext_r])
                B.ts(ysb, ext[:, :, 0:4], convw[:, j, 0:1], ALU.mult, [ext_r, pfm_sr], [ysb_r])
                B.stt(ysb, ext[:, :, 1:5], convw[:, j, 1:2], ysb, ALU.mult, ALU.add, [ext_r, pfm_sr, ysb_r], [ysb_r])
                B.stt(ysb, ext[:, :, 2:6], convw[:, j, 2:3], ysb, ALU.mult, ALU.add, [ext_r, pfm_sr, ysb_r], [ysb_r])
                B.copy(B.DVE, yb[:, 516:548].rearrange("p (s t) -> p s t", t=4), ysb, [ysb_r], [yb_r])
                B.tt(ob[:, 2:NM], yb[:, 2:NM], cbS[jj][0][:, 2:NM], ALU.mult, [yb_r, cbS[jj][1]], [ob_r])
                B.act(convT[:, j, :], ob, AF.Copy, [ob_r], [convT_r[j]])
                B.act(sqt, ob, AF.Square, [ob_r], [sqt_r])
                B.tt(acc, acc, sqt, ALU.add, [acc_r, sqt_r], [acc_r])
                B.copy(B.DVE, stg[:, jj, 0:2], u[:, 514:516], [u_r], [stg_r])
                B.copy(B.DVE, stg[:, jj, 2:18].rearrange("p (s r) -> p s r", r=2), ext[:, :, 4:6], [ext_r], [stg_r])
        flush_stg(7)

        def finish_norm(dst, dst_r, nfeat):
            p3 = pair_view(3)
            B.mms([(p3[:, 252:512], onesf, acc[:, 0:260], True, True), (p3[:, 512:800], onesf, acc[:, 260:NM], True, True)],
                  [onesf_r, acc_r], pair_res(3))
            B.act(dst, p3[:, M0:M1], AF.Ln, pair_res(3) + [epsc_r], [dst_r], scale=1.0 / nfeat, bias=epsc)
            B.act(dst, dst, AF.Exp, [dst_r], [dst_r], scale=-0.5)

        finish_norm(rstd_t, rstd_t_r, 2048)
        for j in range(16):
            B.stt(convT[:, j, :], convT[:, j, :], gconv[:, j:j + 1], rstd_t, ALU.mult, ALU.mult, [convT_r[j], pfm_sr, rstd_t_r], [convT_r[j]])
        if p == 0:
            B.tap("convT", convT.rearrange("p a b -> p (a b)"), convT_r, [128, 16 * NM])
        if stop_after <= 3:
            break

        AOT, AOT_r = B.sb("AOT", ZX, [128, 16, NM], BF16, nres=16)
        o = ZX + 17536
        cKT, cKT_r = B.sb("cKT", o, [128, 8, 4, 128], BF16); o += 8192
        cVd, cVd_r = B.sb("cVd", o, [128, 8, 4, 128], BF16); o += 8192
        Pp = [B.sb("Pp%d" % i, o + i * 1024, [128, 4, 128], BF16) for i in range(2)]; o += 2048
        Po = [B.sb("Po%d" % i, o + i * 1024, [128, 4, 128], BF16) for i in range(2)]; o += 2048
        dsm = [B.sb("dsm%d" % i, o + i * 2048, [128, 512], F32) for i in range(2)]; o += 4096
        ckf = [B.sb("ckf%d" % i, o + i * 1024, [128, 256], F32) for i in range(2)]; o += 2048
        cvf = [B.sb("cvf%d" % i, o + i * 1024, [128, 256], F32) for i in range(2)]; o += 2048
        cKd = [B.sb("cKd%d" % i, o + i * 1024, [128, 4, 128], BF16) for i in range(2)]; o += 2048
        Pc, Pc_r = B.sb("Pc", o, [128, 256], BF16); o += 512
        Pn, Pn_r = B.sb("Pn", o, [128, 256], BF16); o += 512
        assert o <= ZX + 51200

        stT, stT_r = B.sb("stT", S0 + 4480, [128, NFT, 16], F32)
        sff = [B.sb("sff%d" % i, S0 + 4480 + 5504 + i * 4096, [16, 1024], F32) for i in range(3)]
        for step in range(13):
            if step < 11:
                bt = step
                nsub = 8 if bt < 10 else 6
                sfb, sfb_r = sff[bt % 3]
                B.dma(B.qsp, sfb[:, 0:nsub * 128], sf[p * 16:(p + 1) * 16, bt * 1024:bt * 1024 + nsub * 128], [sf_r], [sfb_r])
            if step >= 2:
                bt = step - 2
                nsub = 8 if bt < 10 else 6
                sfb, sfb_r = sff[bt % 3]
                pt = PS[:, bt % 2, 0:128].rearrange("p (a c) -> p a c", a=8)
                B.transposes([(pt[:, jj, :], sfb[0:16, jj * 128:(jj + 1) * 128], identf[0:16, 0:16]) for jj in range(nsub)],
                             [sfb_r, identf_r], [bank[bt % 2]])
                B.act(stT[:, 8 * bt:8 * bt + nsub, :], pt[:, 0:nsub, :], AF.Copy, [bank[bt % 2]], [stT_r])

        QB = [(252, 4, 0, 1, 124)] + [(256 + 128 * i, 128, 1 + i, 2 + i, 0) for i in range(4)]
        import os
        c3 = int(os.environ.get("A3CUT", "99"))
        units = [(qc0, n, tp, to, moff, g, hg) for (qc0, n, tp, to, moff) in QB for g in range(4) for hg in range(2)]

        def stage_a(ui, part):
            qc0, n, tp, to, moff, g, hg = units[ui]
            u2 = ui % 2
            bsS = u2 * 2
            SX = PS[:, bsS + 0, :].rearrange("p (t a q) -> p t a q", t=2, a=2)
            SY = PS[:, bsS + 1, :].rearrange("p (t a q) -> p t a q", t=2, a=2)
            jt0 = 4 * g + 2 * hg
            pp, pp_r = Pp[u2]
            po, po_r = Po[u2]
            if part == 1:
                for hf, SB_, bk in ((0, SX, bsS + 0), (1, SY, bsS + 1)):
                    ops = []
                    for a_ in range(2):
                        jt = jt0 + a_
                        rhs = qT[64 * hf:64 * hf + 64, jt, qc0 - M0:qc0 - M0 + n]
                        ops.append((SB_[:, 0, a_, 0:n], KT[64 * hf:64 * hf + 64, g, tp * 128:(tp + 1) * 128], rhs, True, True))
                        ops.append((SB_[:, 1, a_, 0:n], KT[64 * hf:64 * hf + 64, g, to * 128:(to + 1) * 128], rhs, True, True))
                    B.mms(ops, [KT_r, qT_r[jt0], qT_r[jt0 + 1]], [bank[bk]])
                pp4 = pp.rearrange("p (a b) q -> p a b q", b=2)
                po4 = po.rearrange("p (a b) q -> p a b q", b=2)
                for hf, SB_, bk in ((0, SX, bsS + 0), (1, SY, bsS + 1)):
                    B.act(pp4[:, :, hf, 0:n], SB_[:, 0, :, 0:n], AF.Exp, [bank[bk]], [pp_r], scale=0.125)
                    B.act(po4[:, :, hf, 0:n], SB_[:, 1, :, 0:n], AF.Exp, [bank[bk]], [po_r], scale=0.125)
                return
            mP = maskP[:, moff:moff + n].unsqueeze(1).to_broadcast([128, 4, n])
            mO = maskO[:, moff:moff + n].unsqueeze(1).to_broadcast([128, 4, n])
            if tp <= 1:
                B.stt(pp[:, :, 0:n], pp[:, :, 0:n], kvalid[:, tp:tp + 1], mP, ALU.mult, ALU.mult, [pp_r, kvalid_r, maskP_r], [pp_r])
            else:
                B.tt(pp[:, :, 0:n], pp[:, :, 0:n], mP, ALU.mult, [pp_r, maskP_r], [pp_r])
            if to <= 1:
                B.stt(po[:, :, 0:n], po[:, :, 0:n], kvalid[:, to:to + 1], mO, ALU.mult, ALU.mult, [po_r, kvalid_r, maskO_r], [po_r])
            else:
                B.tt(po[:, :, 0:n], po[:, :, 0:n], mO, ALU.mult, [po_r, maskO_r], [po_r])

        def stage_b(ui, part):
            qc0, n, tp, to, moff, g, hg = units[ui]
            u2 = ui % 2
            bsO = 4 + u2 * 2
            Ob = PS[:, bsO + 0, :].rearrange("p (e q) -> p e q", e=4)
            Db = PS[:, bsO + 1, :].rearrange("p (e q) -> p e q", e=4)
            jt0 = 4 * g + 2 * hg
            pp, pp_r = Pp[u2]
            po, po_r = Po[u2]
            ds, ds_r = dsm[u2]
            ds3 = ds.rearrange("p (e q) -> p e q", e=4)
            if part == 1:
                B.mms([(Ob[:, :, 0:n], Vd[:, tp, g, :], pp[:, :, 0:n], True, False),
                       (Ob[:, :, 0:n], Vd[:, to, g, :], po[:, :, 0:n], False, True)],
                      [Vd_r[tp], Vd_r[to], pp_r, po_r], [bank[bsO + 0]])
                B.mms([(Db[:, :, 0:n], onesb, pp[:, :, 0:n], True, False),
                       (Db[:, :, 0:n], onesb, po[:, :, 0:n], False, True)],
                      [onesb_r, pp_r, po_r], [bank[bsO + 1]])
                h0 = 8 * g + 4 * hg
                B.tt(ds3[:, :, 0:n], Db[:, :, 0:n], sinkexp[:, h0:h0 + 4].unsqueeze(2).to_broadcast([128, 4, n]), ALU.add,
                     [bank[bsO + 1], sinkexp_r], [ds_r])
                return
            B.recip(ds3[:, :, 0:n], ds3[:, :, 0:n], [ds_r], [ds_r], fast=True)
            for hf in range(2):
                O4 = Ob.rearrange("p (a b) q -> p a b q", b=2)[64 * hf:64 * hf + 64, :, hf, 0:n]
                R4 = ds3.rearrange("p (a b) q -> p a b q", b=2)[64 * hf:64 * hf + 64, :, hf, 0:n]
                B.tt(AOT[64 * hf:64 * hf + 64, jt0:jt0 + 2, qc0 - M0:qc0 - M0 + n], O4, R4, ALU.mult,
                     [bank[bsO + 0], ds_r], [AOT_r[jt0], AOT_r[jt0 + 1]])

        nun = len(units)
        stage_a(0, 1)
        stage_a(0, 2)
        for ui in range(1, nun):
            stage_a(ui, 1)
            stage_b(ui - 1, 1)
            stage_a(ui, 2)
            stage_b(ui - 1, 2)
        stage_b(nun - 1, 1)
        stage_b(nun - 1, 2)

        for s in range(8 if c3 > 4 else 0):
            kf, kf_r = ckf[s % 2]
            vf, vf_r = cvf[s % 2]
            kd, kd_r = cKd[s % 2]
            row0 = (p * 8 + s) * 128
            B.dma(B.qsp, kf, ck[row0:row0 + 128, :], [ck_r], [kf_r])
            B.dma(B.qsp, vf, cv[row0:row0 + 128, :], [cv_r], [vf_r])
            kf3 = kf.rearrange("p (g d) -> p g d", g=4)
            vf3 = vf.rearrange("p (g d) -> p g d", g=4)
            B.copy(B.DVE, kd[:, :, 0:64], kf3, [kf_r], [kd_r])
            B.copy(B.DVE, kd[:, :, 64:128], kf3, [kf_r], [kd_r])
            B.act(cVd[:, s, :, 0:64], vf3, AF.Copy, [vf_r], [cVd_r])
            B.act(cVd[:, s, :, 64:128], vf3, AF.Copy, [vf_r], [cVd_r])
            b = s % 2
            pT = PS[:, b, :].bitcast(BF16)[:, 0:512].rearrange("p (g c) -> p g c", g=4)
            B.transposes([(pT[:, g, :], kd[:, g, :], identb) for g in range(4)], [kd_r, identb_r], [bank[b]])
            B.act(cKT[:, s, :, :], pT, AF.Copy, [bank[b]], [cKT_r])
        SC0 = 516
        for g in range(4 if c3 > 5 else 0):
            bs = (g % 2) * 4
            ScX = PS[:, 0, 0:128]
            ScY = PS[:, 1, 0:128]
            SnX = PS[:, 2, 0:128]
            SnY = PS[:, 3, 0:128]
            Ob = PS[:, 4, 0:256]
            Db = PS[:, 5, 0:256]
            bs = 2
            for hf, Sc_, Sn_, bc, bn in ((0, ScX, SnX, 0, 2), (1, ScY, SnY, 1, 3)):
                ops = []
                for s in range(8):
                    for a in range(4):
                        jt = 4 * g + a
                        ops.append((Sc_[:, a * 32 + s * 4:a * 32 + s * 4 + 4], cKT[64 * hf:64 * hf + 64, s, g, :],
                                    qT[64 * hf:64 * hf + 64, jt, SC0 + 4 * s:SC0 + 4 * s + 4], True, True))
                B.mms(ops, [cKT_r] + qT_r[4 * g:4 * g + 4], [bank[bc]])
                ops = []
                for a in range(4):
                    jt = 4 * g + a
                    ops.append((Sn_[0:32, a * 32:(a + 1) * 32], KT[64 * hf:64 * hf + 64, g, 768:800],
                                qT[64 * hf:64 * hf + 64, jt, SC0:SC0 + 32], True, True))
                B.mms(ops, [KT_r] + qT_r[4 * g:4 * g + 4], [bank[bn]])
            if c3 <= 6:
                continue
            Pc5 = Pc.rearrange("p (a b c) -> p a b c", a=4, b=2)
            Pn5 = Pn.rearrange("p (a b c) -> p a b c", a=4, b=2)
            for hf, Sc_, Sn_, bc, bn in ((0, ScX, SnX, 0, 2), (1, ScY, SnY, 1, 3)):
                B.act(Pc5[:, :, hf, :], Sc_.rearrange("p (a c) -> p a c", a=4), AF.Exp, [bank[bc]], [Pc_r], scale=0.125)
                B.act(Pn5[0:32, :, hf, :], Sn_[0:32, :].rearrange("p (a c) -> p a c", a=4), AF.Exp, [bank[bn]], [Pn_r], scale=0.125)
            Pc3 = Pc.rearrange("p (a t) -> p a t", t=4)
            B.tt(Pc3, Pc3, maskc.unsqueeze(1).to_broadcast([128, 64, 4]), ALU.mult, [Pc_r, maskc_r], [Pc_r])
            Pn3 = Pn[0:32, :].rearrange("p (e c) -> p e c", e=8)
            B.tt(Pn3, Pn3, masknb[0:32, :].unsqueeze(1).to_broadcast([32, 8, 32]), ALU.mult, [Pn_r, masknb_r], [Pn_r])
            if c3 <= 7:
                continue
            Pc4 = Pc.rearrange("p (e s t) -> p e s t", e=8, s=8)
            Ob4 = Ob.rearrange("p (e s t) -> p e s t", e=8, s=8)
            ops = [(Ob, Vd[0:32, 6, g, :], Pn[0:32, :], True, False)]
            for s in range(8):
                ops.append((Ob4[:, :, s, :], cVd[:, s, g, :], Pc4[:, :, s, :], False, s == 7))
            B.mms(ops, [Vd_r[6], cVd_r, Pn_r, Pc_r], [bank[4]])
            B.mms([(Db, onesb[0:32, :], Pn[0:32, :], True, False), (Db, onesb, Pc, False, True)],
                  [onesb_r, Pn_r, Pc_r], [bank[5]])
            ds, ds_r = dsm[g % 2]
            ds3 = ds[:, 0:256].rearrange("p (e c) -> p e c", e=8)
            Db3 = Db.rearrange("p (e c) -> p e c", e=8)
            Ob3 = Ob.rearrange("p (e c) -> p e c", e=8)
            B.tt(ds3, Db3, sinkexp[:, 8 * g:8 * g + 8].unsqueeze(2).to_broadcast([128, 8, 32]), ALU.add, [bank[5], sinkexp_r], [ds_r])
            B.recip(ds3, ds3, [ds_r], [ds_r], fast=True)
            for hf in range(2):
                O4 = Ob3.rearrange("p (a b) c -> p a b c", b=2)[64 * hf:64 * hf + 64, :, hf, :]
                R4 = ds3.rearrange("p (a b) c -> p a b c", b=2)[64 * hf:64 * hf + 64, :, hf, :]
                B.tt(AOT[64 * hf:64 * hf + 64, 4 * g:4 * g + 4, SC0:SC0 + 32], O4, R4, ALU.mult,
                     [bank[4], ds_r], AOT_r[4 * g:4 * g + 4])

        B.memset(B.DVE, acc, 0.0, [acc_r])
        for j in range(16):
            B.act(sqt, AOT[:, j, :], AF.Square, [AOT_r[j]], [sqt_r])
            B.tt(acc, acc, sqt, ALU.add, [acc_r, sqt_r], [acc_r])
        finish_norm(rstd_t, rstd_t_r, 2048)
        for j in range(16):
            B.stt(AOT[:, j, :], AOT[:, j, :], gattn[:, j:j + 1], rstd_t, ALU.mult, ALU.mult, [AOT_r[j], pfm_sr, rstd_t_r], [AOT_r[j]])
        if p == 0:
            B.tap("AOT", AOT.rearrange("p a b -> p (a b)"), AOT_r, [128, 16 * NM])
        if stop_after <= 4:
            break

        fT, fT_r = B.sb("fT", ZC, [128, 32, NM], BF16, nres=32)
        xres = [B.sb("xres%d_" % i, ZX + 17536 + i * 12288, [128, 6, 512], F32, nres=6) for i in range(2)]
        o = S0
        hm = [B.sb("hm%d" % i, o + i * 2240, [128, NM], F32) for i in range(2)]; o += 4480
        B.memset(B.DVE, acc, 0.0, [acc_r])

        def load_xres(gi):
            xr, xr_r = xres[gi % 2]
            for pc, (r0, nr) in enumerate(MP):
                B.dma(B.qsp, xr[:nr, pc, :], xh[p * TI + r0:p * TI + r0 + nr, gi * 512:(gi + 1) * 512], [xh_r], [xr_r[pc]])

        load_xres(0)

        def rhs4(k, a, b_):
            return (AOT[:, k, a - M0:b_ - M0] if k < 16 else convT[:, k - 16, a - M0:b_ - M0])

        for s in range(16):
            if s % 2 == 0 and s // 2 + 1 < 8:
                load_xres(s // 2 + 1)
            xr, xr_r = xres[(s // 2) % 2]
            pis = pair_mm(32, rhs4, AOT_r + convT_r)
            for jj in range(2):
                j = 2 * s + jj
                pi = pis[jj]
                pm = pair_view(pi)[:, M0:M1]
                xc0 = (s % 2) * 256 + jj * 128
                p3 = pair_view(3)
                B.transposes([(p3[:, r0:r0 + nr], xr[:nr, pc, xc0:xc0 + 128], identf[:nr, :nr]) for pc, (r0, nr) in enumerate(MP)],
                             xr_r + [identf_r], pair_res(3))
                h, h_r = hm[j % 2]
                B.act(h, pm, AF.Copy, pair_res(pi), [h_r])
                B.tt(h, h, p3[:, M0:M1], ALU.add, [h_r] + pair_res(3), [h_r])
                B.dma(B.qsp, hmid[(p * 32 + j) * 128:(p * 32 + j + 1) * 128, :], h, [h_r], [hmid_rs[p][j]])
                B.act(sqt, h, AF.Square, [h_r], [sqt_r])
                B.tt(acc, acc, sqt, ALU.add, [acc_r, sqt_r], [acc_r])
                B.ts(fT[:, j, :], h, gffn[:, j:j + 1], ALU.mult, [h_r, pfm_sr], [fT_r[j]])
        finish_norm(rstd_f, rstd_f_r, D)
        if p == 0:
            B.tap("fT", fT.rearrange("p a b -> p (a b)"), fT_r, [128, 32 * NM])
            B.tap("rstd_f", rstd_f, rstd_f_r, [128, NM])
        if stop_after <= 5:
            break

        actT, actT_r = B.sb("actT", ZX, [128, NFT, NO], BF16, nres=NFT)
        o = S0 + 4480 + 5504
        gs = [B.sb("gs%d" % i, S0 + i * 2240, [128, NM], F32) for i in range(2)]
        yg, yg_r = B.sb("yg", o, [128, NM], F32); o += 2240
        sg_ = [B.sb("sg%d" % i, o + i * 2240, [128, NM], F32) for i in range(4)]; o += 8960
        extf, extf_r = B.sb("extf", o, [128, 8, 6], F32); o += 192
        ysf, ysf_r = B.sb("ysf", o, [128, 8, 4], F32); o += 128
        stgf, stgf_r = B.sb("stgf", o, [128, 2, 18], F32); o += 192
        stOf = [B.sb("stOf%d" % i, o + i * 1024, [18, 256], F32) for i in range(2)]; o += 2048
        assert o <= ZX, o
        def rhs5(k, a, b_):
            return fT[:, k, a - M0:b_ - M0]

        for s in range(43):
            pis = pair_mm(32, rhs5, fT_r)
            for jj in range(2):
                j = 2 * s + jj
                pi = pis[jj]
                pm = pair_view(pi)[:, M0:M1]
                g_, g_r = gs[jj]
                B.tt(g_, pm, rstd_f, ALU.mult, pair_res(pi) + [rstd_f_r], [g_r])
                B.act(yg[:, 2:516], g_[:, 0:514], AF.Copy, [g_r, pfm_sr], [yg_r], scale=fconvw[:, j, 0:1])
                B.stt(yg[:, 2:516], g_[:, 1:515], fconvw[:, j, 1:2], yg[:, 2:516], ALU.mult, ALU.add, [g_r, pfm_sr, yg_r], [yg_r])
                B.stt(yg[:, 2:516], g_[:, 2:516], fconvw[:, j, 2:3], yg[:, 2:516], ALU.mult, ALU.add, [g_r, pfm_sr, yg_r], [yg_r])
                B.copy(B.DVE, extf[:, :, 0:2], stT[:, j, :].rearrange("p (s r) -> p s r", r=2), [stT_r], [extf_r])
                B.copy(B.DVE, extf[:, :, 2:6], g_[:, 516:548].rearrange("p (s t) -> p s t", t=4), [g_r], [extf_r])
                B.ts(ysf, extf[:, :, 0:4], fconvw[:, j, 0:1], ALU.mult, [extf_r, pfm_sr], [ysf_r])
                B.stt(ysf, extf[:, :, 1:5], fconvw[:, j, 1:2], ysf, ALU.mult, ALU.add, [extf_r, pfm_sr, ysf_r], [ysf_r])
                B.stt(ysf, extf[:, :, 2:6], fconvw[:, j, 2:3], ysf, ALU.mult, ALU.add, [extf_r, pfm_sr, ysf_r], [ysf_r])
                B.copy(B.DVE, yg[:, 516:548].rearrange("p (s t) -> p s t", t=4), ysf, [ysf_r], [yg_r])
                sgb, sgb_r = sg_[(s % 2) * 2 + jj]
                B.act(sgb[:, 4:NM], yg[:, 4:NM], AF.Silu, [yg_r], [sgb_r])
                B.tt(sgb[:, 4:NM], sgb[:, 4:NM], rstd_f[:, 4:NM], ALU.mult, [sgb_r, rstd_f_r], [sgb_r])
                B.copy(B.DVE, stgf[:, jj, 0:2], g_[:, 514:516], [g_r], [stgf_r])
                B.copy(B.DVE, stgf[:, jj, 2:18].rearrange("p (s r) -> p s r", r=2), extf[:, :, 4:6], [extf_r], [stgf_r])
            pis = pair_mm(32, rhs5, fT_r)
            pt = PS[:, 6 + s % 2, 0:256].rearrange("p (a c) -> p a c", a=2)
            B.transposes([(pt[0:18, jj, :], stgf[:, jj, :], identf) for jj in range(2)], [stgf_r, identf_r], [bank[6 + s % 2]])
            so, so_r = stOf[s % 2]
            B.act(so.rearrange("p (a c) -> p a c", a=2), pt[0:18], AF.Copy, [bank[6 + s % 2]], [so_r])
            B.dma(B.qsp, nf_p[p * 2:(p + 1) * 2, s * 256:(s + 1) * 256], so[0:2, :], [so_r], [])
            B.dma(B.qsp, nf_s[p * 16:(p + 1) * 16, s * 256:(s + 1) * 256], so[2:18, :], [so_r], [])
            for jj in range(2):
                j = 2 * s + jj
                pi = pis[jj]
                pm = pair_view(pi)[:, M0:M1]
                sgb, sgb_r = sg_[(s % 2) * 2 + jj]
                B.tt(actT[:, j, :], pm[:, 4:NM], sgb[:, 4:NM], ALU.mult, pair_res(pi) + [sgb_r], [actT_r[j]])
        if p == 0:
            B.tap("actT", actT[:, 0:4, :].rearrange("p a b -> p (a b)"), actT_r[0:4], [128, 4 * NO])
        if stop_after <= 6:
            break

        o = S0
        hmb = [B.sb("hmb%d" % i, o + i * 2240, [128, NM], F32) for i in range(4)]; o += 8960
        hob = [B.sb("hob%d" % i, o + i * 2240, [128, NO], F32) for i in range(2)]; o += 4480
        sq6, sq6_r = B.sb("sq6", o, [128, NO], F32); o += 2240
        acc6, acc6_r = B.sb("acc6", o, [128, NM], F32); o += 2240
        rstd_y, rstd_y_r = B.sb("rstd_y", o, [128, NM], F32); o += 2240
        B.memset(B.DVE, acc6, 0.0, [acc6_r])
        def rhs6(k, a, b_):
            return actT[:, k, a - O0:b_ - O0]

        for sp in range(16):
            for jj in range(2):
                j = 2 * sp + jj
                hb, hb_r = hmb[(sp % 2) * 2 + jj]
                B.dma(B.qsp, hb, hmid[(p * 32 + j) * 128:(p * 32 + j + 1) * 128, :], [hmid_rs[p][j]], [hb_r])
            pis = pair_mm(NFT, rhs6, actT_r, c0=O0)
            for jj in range(2):
                j = 2 * sp + jj
                pv_ = pair_view(pis[jj])
                hb, hb_r = hmb[(sp % 2) * 2 + jj]
                ho, ho_r = hob[jj]
                B.tt(ho, pv_[:, O0:M1], hb[:, 4:NM], ALU.add, pair_res(pis[jj]) + [hb_r], [ho_r])
                B.dma(B.qsp, hout[(p * 32 + j) * 128:(p * 32 + j + 1) * 128, :], ho, [ho_r], [hout_rs[p][j]])
                B.act(sq6, ho, AF.Square, [ho_r], [sq6_r])
                B.tt(acc6[:, 4:NM], acc6[:, 4:NM], sq6, ALU.add, [acc6_r, sq6_r], [acc6_r])
        p3 = pair_view(3)
        B.mms([(p3[:, 252:512], onesf, acc6[:, 0:260], True, True), (p3[:, 512:800], onesf, acc6[:, 260:NM], True, True)],
              [onesf_r, acc6_r], pair_res(3))
        B.act(rstd_y, p3[:, M0:M1], AF.Ln, pair_res(3) + [epsc_r], [rstd_y_r], scale=1.0 / D, bias=epsc)
        B.act(rstd_y, rstd_y, AF.Exp, [rstd_y_r], [rstd_y_r], scale=-0.5)
        if stop_after <= 7:
            break

        hin = [B.sb("hin%d" % i, ZX + i * 2240, [128, NO], F32) for i in range(3)]
        yT = [B.sb("yT%d" % i, ZX + 6720 + i * 2240, [128, NO], F32) for i in range(2)]
        ytok = [B.sb("ytok%d" % i, ZX + 11264 + i * 10240, [128, 5, 512], F32) for i in range(2)]
        def load_hin(jx):
            B.dma(B.qsp, hin[jx % 3][0], hout[(p * 32 + jx) * 128:(p * 32 + jx + 1) * 128, :], [hout_rs[p][jx]], [hin[jx % 3][1]])

        load_hin(0)
        load_hin(1)
        for j in range(32):
            if j + 2 < 32:
                load_hin(j + 2)
            hi, hi_r = hin[j % 3]
            yt, yt_r = yT[j % 2]
            B.stt(yt, hi, gfin[:, j:j + 1], rstd_y[:, 4:NM], ALU.mult, ALU.mult, [hi_r, pfm_sr, rstd_y_r], [yt_r])
            b0 = (j % 2) * 2
            pa = PS[:, b0, :].rearrange("p (a c) -> p a c", a=4)
            pb = PS[:, b0 + 1, 0:128]
            ops = [(pa[:, i, :], yt[:, i * 128:(i + 1) * 128], identf) for i in range(4)]
            ops.append((pb[0:32, :], yt[:, 512:544], identf))
            B.transposes(ops, [yt_r, identf_r], [bank[b0], bank[b0 + 1]])
            yk, yk_r = ytok[(j // 4) % 2]
            jc = (j % 4) * 128
            B.act(yk[:, 0:4, jc:jc + 128], pa, AF.Copy, [bank[b0]], [yk_r])
            B.act(yk[0:32, 4, jc:jc + 128], pb[0:32, :], AF.Copy, [bank[b0 + 1]], [yk_r])
            if j % 4 == 3:
                c0 = (j // 4) * 512
                for i in range(4):
                    B.dma(B.qsp, y_p[p * 512 + i * 128:p * 512 + (i + 1) * 128, c0:c0 + 512], yk[:, i, :], [yk_r], [])
                B.dma(B.qsp, y_s[p * 32:(p + 1) * 32, c0:c0 + 512], yk[0:32, 4, :], [yk_r], [])

    fin = B.ACT
    for q in (B.qsp, B.qw):
        for i, (sem, key) in enumerate(q.sems):
            if q.tot[i] > 0 and fin.seen.get(key, 0) < q.tot[i]:
                fin.h.wait_ge(sem, q.tot[i])
                fin.seen[key] = q.tot[i]
    return B


_CACHE = {}


def _rope_tables(pos):
    half = 8
    inv = (500000.0 ** (-np.arange(half, dtype=np.float32) * 2.0 / 16)).astype(np.float32)
    ang = pos.astype(np.float32)[:, None] * inv[None, :]
    return np.cos(ang).astype(np.float32), np.sin(ang).astype(np.float32)


def _const_tables(v):
    pos = np.zeros(TI, np.int64)
    pos[:768] = 512 * v - 240 + np.arange(768)
    for s in range(8):
        for t in range(4):
            pos[768 + 4 * s + t] = 8192 + t
    posc = np.maximum(pos, 0)
    cos, sin = _rope_tables(posc)
    CT = np.ones((128, NM), np.float32)
    ST = np.zeros((128, NM), np.float32)
    for hb in (0, 64):
        CT[hb:hb + 8, :] = cos[M0:M1].T
        CT[hb + 8:hb + 16, :] = cos[M0:M1].T
        ST[hb:hb + 8, :] = -sin[M0:M1].T
        ST[hb + 8:hb + 16, :] = sin[M0:M1].T
    tfm = np.concatenate([CT, ST], axis=1)
    ttok = np.zeros((128, 7, 32), np.float32)
    for i, (r0, nr) in enumerate(TILES):
        ttok[:nr, i, 0:8] = cos[r0:r0 + nr]
        ttok[:nr, i, 8:16] = cos[r0:r0 + nr]
        ttok[:nr, i, 16:24] = -sin[r0:r0 + nr]
        ttok[:nr, i, 24:32] = sin[r0:r0 + nr]
    kval = np.ones((128, 2), np.float32)
    if v == 0:
        kval[:, 0] = 0.0
        kval[:112, 1] = 0.0
    return tfm, ttok.reshape(128, 224), kval


def _c128():
    c = np.zeros((128, 644), np.float32)
    i = np.arange(128)
    c[:, 0:128] = np.eye(128, dtype=np.float32)
    c[:, 128:256] = (i[:, None] <= i[None, :])
    c[:, 256:384] = (i[:, None] > i[None, :])
    perm = np.zeros((128, 128), np.float32)
    for hb in (0, 64):
        for d in range(8):
            perm[hb + d + 8, hb + d] = 1.0
            perm[hb + d, hb + d + 8] = 1.0
    c[:, 384:512] = perm
    c[:, 512:640] = 1.0
    c[:, 640:644] = (i[:, None] > np.arange(4)[None, :])
    maskn = np.zeros((32, 32), np.float32)
    for s2 in range(8):
        for t2 in range(4):
            for t in range(4):
                if t2 <= t:
                    maskn[4 * s2 + t2, 4 * s2 + t] = 1.0
    return c, maskn


def kernel(x_prompt, x_sample, cache_k, cache_v, state_conv, state_ffn_conv, meta_tokens,
           g_mix, w_in, attn_sinks, conv_w, g_attn_out, g_conv_out, w_out, g_ffn,
           w_gate_up, ffn_conv_w, w_down, g_final, _debug=False, _ncores=8, _stop_after=99):
    f32 = np.float32
    x_prompt = np.asarray(x_prompt, f32); x_sample = np.asarray(x_sample, f32)
    ncores = _ncores
    npass = 2
    key = (npass, _debug, _stop_after)
    if key not in _CACHE:
        _CACHE[key] = build_program(npass=npass, debug=_debug, stop_after=_stop_after)
    B = _CACHE[key]

    xp = x_prompt[0]
    xs = x_sample
    fm = lambda a, n: np.ascontiguousarray(np.asarray(a, f32).reshape(n, 128).T)
    pfm = np.zeros((128, NPFM), f32)
    pfm[:, 0:32] = fm(g_mix[0], 32)
    pfm[:, 32:64] = fm(g_ffn[0], 32)
    pfm[:, 64:80] = fm(g_attn_out[0], 16)
    pfm[:, 80:96] = fm(g_conv_out[0], 16)
    cw = np.asarray(conv_w[0], f32)
    pfm[:, 96:144] = np.transpose(cw.reshape(3, 16, 128), (2, 1, 0)).reshape(128, 48)
    fw = np.asarray(ffn_conv_w[0], f32)
    pfm[:, 144:402] = np.transpose(fw.reshape(3, NFT, 128), (2, 1, 0)).reshape(128, 258)
    pfm[:, 402:434] = fm(g_final, 32)
    c128, maskn = _c128()
    w_in2 = np.asarray(w_in[0], f32); w_out2 = np.asarray(w_out[0], f32)
    w_gu2 = np.asarray(w_gate_up[0], f32); w_dn2 = np.asarray(w_down[0], f32)
    ck_all = np.asarray(cache_k[0], f32).reshape(128, 128, 256)
    cv_all = np.asarray(cache_v[0], f32).reshape(128, 128, 256)
    sc_all = np.asarray(state_conv[0], f32)
    sf_all = np.asarray(state_ffn_conv[0], f32)
    meta = np.asarray(meta_tokens, f32)

    in_maps = []
    for c in range(ncores):
        xh = np.zeros((npass, TI, D), f32)
        tfm_l, ttok_l, kval_l = [], [], []
        for p in range(npass):
            v = 2 * c + p
            if v == 0:
                xh[p, 240:256] = meta
            else:
                xh[p, 0:256] = xp[512 * v - 256:512 * v]
            xh[p, 256:768] = xp[512 * v:512 * v + 512]
            xh[p, 768:800] = xs[8 * v:8 * v + 8].reshape(32, D)
            a, b, k = _const_tables(v)
            tfm_l.append(a); ttok_l.append(b); kval_l.append(k)
        m = {
            "xh": xh.reshape(npass * TI, D),
            "ck": ck_all[16 * c:16 * c + 16].reshape(16 * 128, 256),
            "cv": cv_all[16 * c:16 * c + 16].reshape(16 * 128, 256),
            "sc": sc_all[16 * c:16 * c + 16].reshape(32, 2048),
            "sf": sf_all[16 * c:16 * c + 16].reshape(32, DFF),
            "w_in": w_in2, "w_out": w_out2, "w_gu": w_gu2, "w_dn": w_dn2,
            "pfm": pfm, "sinks": np.asarray(attn_sinks, f32).reshape(1, 32),
            "c128": c128, "maskn": maskn,
            "tfm": np.concatenate(tfm_l, 0), "ttok": np.concatenate(ttok_l, 0), "kval": np.concatenate(kval_l, 0),
        }
        in_maps.append(m)
    res = run_bass_kernel_spmd(B.nc, in_maps, core_ids=list(range(ncores)))
    R = res.results
    if _debug:
        return R
    y_prompt = np.concatenate([R[c]["y_p"] for c in range(ncores)], 0).reshape(1, ncores * 1024, D)
    y_sample = np.concatenate([R[c]["y_s"] for c in range(ncores)], 0).reshape(ncores * 16, 4, D)
    last = R[ncores - 1]
    nk_p = last["nk_p"][128:256].reshape(1, 1, 128, 4, 64)
    nv_p = last["nv_p"][128:256].reshape(1, 1, 128, 4, 64)
    nc_p = last["nc_p"][2:4].reshape(1, 1, 2, 2048)
    nf_p = last["nf_p"][2:4].reshape(1, 1, 2, DFF)
    nk_s = np.concatenate([R[c]["nk_s"] for c in range(ncores)], 0).reshape(1, ncores * 16, 128, 4, 64)
    nv_s = np.concatenate([R[c]["nv_s"] for c in range(ncores)], 0).reshape(1, ncores * 16, 128, 4, 64)
    nc_s = np.concatenate([R[c]["nc_s"] for c in range(ncores)], 0).reshape(1, ncores * 16, 2, 2048)
    nf_s = np.concatenate([R[c]["nf_s"] for c in range(ncores)], 0).reshape(1, ncores * 16, 2, DFF)
    return (y_prompt.astype(f32), y_sample.astype(f32), nk_p.astype(f32), nv_p.astype(f32), nc_p.astype(f32),
            nf_p.astype(f32), nk_s.astype(f32), nv_s.astype(f32), nc_s.astype(f32), nf_s.astype(f32))
```
